# Optimizing a Trainium2 kernel written in Bass

```python
import math
import jax, jax.numpy as jnp
from jax import lax
import numpy as np

D_MODEL = 1024
BATCH = 2
SEQ = 8192
DEPTH = 4
DEC_BATCH = 128
DEC_SEQ = 1
PAST_LEN = 8192
PAGE_SIZE = 128

D_A = 512
HD_A = 64
H_A = D_A // HD_A
KV_A = 2
G_A = H_A // KV_A
WINDOW = 128
N_BUCKETS = 32
MAX_DIST = 128
D_B = 512
H_B = 8
DH_B = D_B // H_B
CHUNK_B = 128
D_C = 512
H_C = 4
DK_TOT = D_C // 2
DK_C = DK_TOT // H_C
DV_C = D_C // H_C
GLA_RANK = 16
GLA_TAU = 16.0
GLA_CHUNK = 64

D_MIX = D_A + D_B + D_C
EPS = 1e-6
NEG = -1e30
SPLITS = [D_A, KV_A * HD_A, KV_A * HD_A, D_A,
          D_B, D_B, D_B,
          DK_TOT, DK_TOT, D_C, D_C, GLA_RANK]
D_IN = sum(SPLITS)

kernel_name = "hymba_swa_gmlp_gla_step"


def rmsnorm(x, g):
    xf = x.astype(jnp.float32)
    y = xf * lax.rsqrt(jnp.mean(xf * xf, axis=-1, keepdims=True) + EPS)
    return (y * g.astype(jnp.float32)).astype(x.dtype)


def t5_bucket(dist):
    n = np.maximum(dist, 0)
    max_exact = N_BUCKETS // 2
    large = max_exact + (np.log(np.maximum(n, 1) / max_exact) / np.log(MAX_DIST / max_exact)
                         * (N_BUCKETS - max_exact)).astype(np.int32)
    large = np.minimum(large, N_BUCKETS - 1)
    return np.where(n < max_exact, n, large).astype(np.int32)


def rel_bias_lookup(rel_bias, dist):
    b = jnp.take(rel_bias.astype(jnp.float32), jnp.asarray(t5_bucket(dist)), axis=0)
    return jnp.moveaxis(b, -1, 0).reshape(KV_A, G_A, *dist.shape)


def attend_sinks(q, k, v, bias, mask, sink):
    logits = jnp.einsum('...qkgd,...skd->...kgqs', q, k).astype(jnp.float32) * (HD_A ** -0.5)
    logits = jnp.where(mask, logits + bias, NEG)
    s = sink.astype(jnp.float32).reshape(KV_A, G_A)[:, :, None]
    lse = jnp.logaddexp(jax.nn.logsumexp(logits, axis=-1), s)
    p = jnp.exp(logits - lse[..., None])
    return jnp.einsum('...kgqs,...skd->...qkgd', p.astype(v.dtype), v)


def swa_prompt(q, k, v, rel_bias, sink):
    B, S = q.shape[:2]
    nb = S // WINDOW
    qb = q.reshape(B, nb, WINDOW, KV_A, G_A, HD_A)
    pad = ((0, 0), (WINDOW, 0), (0, 0), (0, 0))
    kb = jnp.pad(k, pad).reshape(B, nb + 1, WINDOW, KV_A, HD_A)
    vb = jnp.pad(v, pad).reshape(B, nb + 1, WINDOW, KV_A, HD_A)
    keys = jnp.concatenate([kb[:, :-1], kb[:, 1:]], axis=2)
    vals = jnp.concatenate([vb[:, :-1], vb[:, 1:]], axis=2)
    i = np.arange(WINDOW)[:, None]
    j = np.arange(2 * WINDOW)[None, :]
    dist = i + WINDOW - j
    band = (dist >= 0) & (dist < WINDOW)
    kpos = np.arange(nb)[:, None, None] * WINDOW + j[None] - WINDOW
    mask = (band[None] & (kpos >= 0))[:, None, None]
    o = attend_sinks(qb, keys, vals, rel_bias_lookup(rel_bias, dist), mask, sink)
    return o.reshape(B, S, D_A)


def swa_sample(q, k, v, buf_k, buf_v, rel_bias, sink):
    Bd, T = q.shape[:2]
    Wb = buf_k.shape[1]
    keys = jnp.concatenate([buf_k.astype(k.dtype), k], axis=1)
    vals = jnp.concatenate([buf_v.astype(v.dtype), v], axis=1)
    dist = np.arange(T)[:, None] + Wb - np.arange(Wb + T)[None, :]
    mask = (dist >= 0) & (dist < WINDOW)
    o = attend_sinks(q.reshape(Bd, T, KV_A, G_A, HD_A), keys, vals,
                     rel_bias_lookup(rel_bias, dist), mask, sink)
    return o.reshape(Bd, T, D_A), keys[:, T:], vals[:, T:]


def chunk_mlp(u, v_raw, w_s, b_s, ln_g, ln_b):
    B, T = u.shape[:2]
    L = CHUNK_B if T % CHUNK_B == 0 else T
    nc = T // L
    vf = v_raw.astype(jnp.float32)
    mu = jnp.mean(vf, axis=-1, keepdims=True)
    var = jnp.mean(jnp.square(vf - mu), axis=-1, keepdims=True)
    vn = (vf - mu) * lax.rsqrt(var + EPS) * ln_g.astype(jnp.float32) + ln_b.astype(jnp.float32)
    wm = jnp.where(np.tril(np.ones((L, L), bool)), w_s[:, :L, :L].astype(jnp.float32), 0.0)
    mix = jnp.einsum('hij,bcjhd->bcihd', wm, vn.reshape(B, nc, L, H_B, DH_B))
    mix = mix + b_s[:, :L].astype(jnp.float32).T[:, :, None]
    out = u.astype(jnp.float32) * mix.reshape(B, T, D_B)
    return out.astype(u.dtype), vn.astype(u.dtype)


def gla(q, k, v, log_a, s0):
    B, T = q.shape[:2]
    L = math.gcd(T, GLA_CHUNK)
    nc = T // L
    f = lambda t, d: t.astype(jnp.float32).reshape(B, nc, L, H_C, d)
    q = f(q, DK_C) * (DK_C ** -0.5)
    k = f(k, DK_C)
    v = f(v, DV_C)
    b = jnp.cumsum(f(log_a, DK_C), axis=2)
    q_dec = q * jnp.exp(b)
    k_inv = k * jnp.exp(-b)
    causal = np.tril(np.ones((L, L), bool))
    att = jnp.where(causal, jnp.einsum('bnqhd,bnshd->bnhqs', q_dec, k_inv), 0.0)
    o = jnp.einsum('bnhqs,bnshv->bnqhv', att, v)
    b_last = b[:, :, -1]
    k_dec = k * jnp.exp(b_last[:, :, None] - b)
    dS = jnp.einsum('bnshd,bnshv->bnhdv', k_dec, v)

    def step(S, xs):
        a, d = xs
        return a[..., None] * S + d, S

    S_fin, S_prev = lax.scan(step, s0.astype(jnp.float32),
                             (jnp.moveaxis(jnp.exp(b_last), 1, 0), jnp.moveaxis(dS, 1, 0)))
    o = o + jnp.einsum('bnqhd,nbhdv->bnqhv', q_dec, S_prev)
    return o.reshape(B, T, H_C, DV_C), S_fin


def layer(x, g_norm, w_in_l, w_out_l, sink_l, sw_l, sb_l, ln_g_l, ln_b_l, wup_l, bup_l, gn_l,
          rel_bias, buf_k, buf_v, s0):
    B, T = x.shape[:2]
    h = rmsnorm(x, g_norm)
    proj = h @ w_in_l
    offs = [int(o) for o in np.cumsum(SPLITS)[:-1]]
    (qa, ka, va, ga, ub, vb, gb, qc, kc, vc, gc, lr) = jnp.split(proj, offs, axis=-1)
    ka = ka.reshape(B, T, KV_A, HD_A)
    va = va.reshape(B, T, KV_A, HD_A)
    if buf_k is None:
        ya = swa_prompt(qa, ka, va, rel_bias, sink_l)
        wb = min(WINDOW, T)
        new_k, new_v = ka[:, T - wb:], va[:, T - wb:]
    else:
        ya, new_k, new_v = swa_sample(qa, ka, va, buf_k, buf_v, rel_bias, sink_l)
    yb, v_rows = chunk_mlp(ub, vb, sw_l, sb_l, ln_g_l, ln_b_l)
    log_a = jax.nn.log_sigmoid((lr @ wup_l + bup_l).astype(jnp.float32)) / GLA_TAU
    oc, S_fin = gla(qc, kc, vc, log_a, s0)
    oc = oc * lax.rsqrt(jnp.mean(oc * oc, axis=-1, keepdims=True) + EPS)
    yc = (oc * gn_l.astype(jnp.float32).reshape(H_C, DV_C)).reshape(B, T, D_C).astype(x.dtype)
    y = jnp.concatenate([ya * jax.nn.silu(ga), yb * jax.nn.silu(gb), yc * jax.nn.silu(gc)], axis=-1)
    x = x + (y @ w_out_l).astype(x.dtype)
    return x, new_k, new_v, S_fin.astype(x.dtype), v_rows


def setup_inputs(seed: int = 0) -> dict:
    key = jax.random.key(seed)
    ks = jax.random.split(key, 20)
    nrm = lambda k, shape, s: jax.random.normal(k, shape, jnp.float32) * s
    wb = min(WINDOW, PAST_LEN)
    return {
        "x_prompt": nrm(ks[0], (BATCH, SEQ, D_MODEL), 1.0),
        "x_sample": nrm(ks[1], (DEC_BATCH, DEC_SEQ, D_MODEL), 1.0),
        "state_swa_k": nrm(ks[2], (DEPTH, DEC_BATCH, wb, KV_A, HD_A), 1.0),
        "state_swa_v": nrm(ks[3], (DEPTH, DEC_BATCH, wb, KV_A, HD_A), 1.0),
        "state_gla": nrm(ks[4], (DEPTH, DEC_BATCH, H_C, DK_C, DV_C), 0.5),
        "rel_bias": nrm(ks[5], (N_BUCKETS, H_A), 0.5),
        "norm_g": 1.0 + nrm(ks[6], (DEPTH, D_MODEL), 0.01),
        "w_in": nrm(ks[7], (DEPTH, D_MODEL, D_IN), D_MODEL ** -0.5),
        "sinks": nrm(ks[8], (DEPTH, H_A), 0.5),
        "spatial_w": nrm(ks[9], (DEPTH, H_B, CHUNK_B, CHUNK_B), 0.5 * CHUNK_B ** -0.5),
        "spatial_b": 1.0 + nrm(ks[10], (DEPTH, H_B, CHUNK_B), 0.01),
        "chunk_ln_g": 1.0 + nrm(ks[11], (DEPTH, D_B), 0.01),
        "chunk_ln_b": nrm(ks[12], (DEPTH, D_B), 0.01),
        "gla_w_up": nrm(ks[13], (DEPTH, GLA_RANK, DK_TOT), GLA_RANK ** -0.5),
        "gla_b_up": nrm(ks[14], (DEPTH, DK_TOT), 0.1),
        "gla_norm_g": 1.0 + nrm(ks[15], (DEPTH, D_C), 0.01),
        "w_out": nrm(ks[16], (DEPTH, D_MIX, D_MODEL), 0.5 * D_MIX ** -0.5),
        "final_norm_g": 1.0 + nrm(ks[17], (D_MODEL,), 0.01),
    }


def reference(x_prompt, x_sample, state_swa_k, state_swa_v, state_gla, rel_bias, norm_g, w_in, sinks,
              spatial_w, spatial_b, chunk_ln_g, chunk_ln_b, gla_w_up, gla_b_up, gla_norm_g, w_out,
              final_norm_g):
    xp, xs = x_prompt, x_sample
    s0_prompt = jnp.zeros((xp.shape[0], H_C, DK_C, DV_C), jnp.float32)
    kp_l, vp_l, sp_l, ks_l, vs_l, ss_l, cv_l = [], [], [], [], [], [], []
    for l in range(DEPTH):
        w = (norm_g[l], w_in[l], w_out[l], sinks[l], spatial_w[l], spatial_b[l], chunk_ln_g[l],
             chunk_ln_b[l], gla_w_up[l], gla_b_up[l], gla_norm_g[l], rel_bias)
        xp, kp, vp, sp, _ = layer(xp, *w, None, None, s0_prompt)
        xs, k_s, v_s, s_s, cv = layer(xs, *w, state_swa_k[l], state_swa_v[l], state_gla[l])
        kp_l.append(kp); vp_l.append(vp); sp_l.append(sp)
        ks_l.append(k_s); vs_l.append(v_s); ss_l.append(s_s); cv_l.append(cv)
    y_prompt = rmsnorm(xp, final_norm_g)
    y_sample = rmsnorm(xs, final_norm_g)
    return (y_prompt, y_sample, jnp.stack(kp_l), jnp.stack(vp_l), jnp.stack(sp_l),
            jnp.stack(ks_l), jnp.stack(vs_l), jnp.stack(ss_l), jnp.stack(cv_l))
```

```python
import numpy as np
import concourse.bass as bass
import concourse.mybir as mybir
from concourse.bass_utils import run_bass_kernel_spmd

F32 = mybir.dt.float32
BF16 = mybir.dt.bfloat16
I32 = mybir.dt.int32
ALU = mybir.AluOpType
AF = mybir.ActivationFunctionType

D = 1024
DIN = 4368
DMIX = 1536
EPS = 1e-6
NEG = -1e30
NS = 16
MW = 520
C_QA, C_KA, C_VA, C_GA = 0, 512, 640, 768
C_UB, C_VB, C_GB = 1280, 1792, 2304
C_QC, C_KC, C_VC, C_GC, C_LR = 2816, 3072, 3328, 3840, 4352


class Sched:
    def __init__(self, nc, n_dma=32):
        self.nc = nc
        self.engs = {"pe": nc.tensor, "act": nc.scalar, "dve": nc.vector, "pool": nc.gpsimd, "sp": nc.sync}
        self.sem = {k: nc.alloc_semaphore(name=f"sem_{k}") for k in self.engs}
        self.cnt = {k: 0 for k in self.engs}
        self.ops = {k: [] for k in self.engs}
        self.waited = {k: {} for k in self.engs}
        self.lastw = {}
        self.readers = {}
        self.dma_sems = [nc.alloc_semaphore(name=f"sem_dma{i}") for i in range(n_dma)]
        self.dma_cnt = [0] * n_dma
        self.n_sw = 8
        self.dma_rr = {"pool": 0, "sp": self.n_sw}
        self.bg_sems = [nc.alloc_semaphore(name=f"sem_bg{i}") for i in range(8)]
        self.bg_cnt = [0] * 8
        self.bg_rr = 0
        self.cc_sem = nc.alloc_semaphore(name="sem_cc")
        self.cc_cnt = 0
        self.semobj = {("c", 0): self.cc_sem}
        self.keymap = lambda k: k
        for k in self.engs:
            self.semobj[("e", k)] = self.sem[k]
        for i, s in enumerate(self.dma_sems):
            self.semobj[("d", i)] = s
        for i, s in enumerate(self.bg_sems):
            self.semobj[("b", i)] = s

    def _wait(self, eng, tok):
        if tok is None:
            return
        key, val = tok
        if key == ("e", eng) and eng in ("pe", "sp"):
            return
        if self.waited[eng].get(key, 0) >= val:
            return
        self.waited[eng][key] = val
        sem = self.semobj[key]
        self.ops[eng].append(lambda e, sem=sem, val=val: e.wait_ge(sem, val))

    def _deps(self, eng, reads, writes):
        reads = [self.keymap(k) for k in reads]
        writes = [self.keymap(k) for k in writes]
        for r in reads:
            self._wait(eng, self.lastw.get(r))
        for w in writes:
            self._wait(eng, self.lastw.get(w))
            for t in self.readers.get(w, []):
                self._wait(eng, t)

    def _commit(self, tok, reads, writes):
        reads = [self.keymap(k) for k in reads]
        writes = [self.keymap(k) for k in writes]
        for r in reads:
            self.readers.setdefault(r, []).append(tok)
        for w in writes:
            self.lastw[w] = tok
            self.readers[w] = []

    def op(self, eng, fn, reads=(), writes=(), inc=True):
        self._deps(eng, reads, writes)
        if inc:
            self.cnt[eng] += 1
            sem = self.sem[eng]
            self.ops[eng].append(lambda e, fn=fn, sem=sem: fn(e).then_inc(sem, 1))
        else:
            assert eng == "pe"
            self.ops[eng].append(lambda e, fn=fn: fn(e))
        tok = (("e", eng), self.cnt[eng] + (0 if inc else 1))
        self._commit(tok, reads, writes)
        return tok

    def dma(self, eng, out, in_, reads=(), writes=()):
        lo, hi = (0, self.n_sw) if eng == "pool" else (self.n_sw, len(self.dma_sems))
        i = self.dma_rr[eng]
        self.dma_rr[eng] = lo + (i + 1 - lo) % (hi - lo)
        if self.dma_cnt[i] > 0:
            self._wait(eng, (("d", i), self.dma_cnt[i]))
        self._deps(eng, reads, writes)
        self.dma_cnt[i] += 16
        sem = self.dma_sems[i]
        self.ops[eng].append(lambda e, out=out, in_=in_, sem=sem: e.dma_start(out=out, in_=in_).then_inc(sem, 16))
        tok = (("d", i), self.dma_cnt[i])
        self._commit(tok, reads, writes)
        return tok

    def dma_bg(self, eng, out, in_, reads=(), writes=()):
        i = self.bg_rr
        self.bg_rr = (i + 1) % len(self.bg_sems)
        if self.bg_cnt[i] > 0:
            self._wait(eng, (("b", i), self.bg_cnt[i]))
        self._deps(eng, reads, writes)
        self.bg_cnt[i] += 16
        sem = self.bg_sems[i]
        self.ops[eng].append(lambda e, out=out, in_=in_, sem=sem: e.dma_start(out=out, in_=in_).then_inc(sem, 16))
        tok = (("b", i), self.bg_cnt[i])
        self._commit(tok, reads, writes)
        return tok

    def allgather(self, src, dst, groups, reads=(), writes=()):
        eng = "pool"
        self._deps(eng, reads, writes)
        self.cc_cnt += 1
        sem = self.cc_sem
        self.ops[eng].append(lambda e: e.collective_compute(
            "AllGather", ALU.bypass, replica_groups=groups, ins=[src], outs=[dst]).then_inc(sem, 1))
        tok = (("c", 0), self.cc_cnt)
        self._commit(tok, reads, writes)
        return tok

    def finish(self, eng="sp"):
        for i, c in enumerate(self.dma_cnt):
            if c:
                self._wait(eng, (("d", i), c))
        for i, c in enumerate(self.bg_cnt):
            if c:
                self._wait(eng, (("b", i), c))
        for k in self.engs:
            if k != eng and self.cnt[k]:
                self._wait(eng, (("e", k), self.cnt[k]))

    def emit(self):
        with self.nc.Block() as block:
            @block.tensor
            def _(e):
                for f in self.ops["pe"]:
                    f(e)

            @block.scalar
            def _(e):
                for f in self.ops["act"]:
                    f(e)

            @block.vector
            def _(e):
                for f in self.ops["dve"]:
                    f(e)

            @block.gpsimd
            def _(e):
                for f in self.ops["pool"]:
                    f(e)

            @block.sync
            def _(e):
                for f in self.ops["sp"]:
                    f(e)


def build_program(NB, DEPTH):
    nc = bass.Bass("TRN2", target_bir_lowering=False)
    S = Sched(nc)
    NT = NB * 128

    def din(name, shape):
        return nc.dram_tensor(name, list(shape), F32, kind="ExternalInput").ap()

    def dout(name, shape):
        return nc.dram_tensor(name, list(shape), F32, kind="ExternalOutput").ap()

    xp = din("xp", [NT, D])
    xsm = din("xs", [NS, D])
    swk = din("swk", [DEPTH, NS, 128, 128])
    swv = din("swv", [DEPTH, NS, 128, 128])
    sgl = din("sgl", [DEPTH, NS, 256, 128])
    w_in = din("w_in", [DEPTH, D, DIN])
    w_out = din("w_out", [DEPTH, DMIX, D])
    norm_g = din("norm_g", [DEPTH * 8, 128])
    fin_g = din("fin_g", [1, D])
    sinks = din("sinks", [DEPTH, 8])
    sp_w = din("sp_w", [DEPTH, 8, 128, 128])
    sp_b = din("sp_b", [DEPTH * 8, 128])
    ln_g = din("ln_g", [DEPTH, 512])
    ln_b = din("ln_b", [DEPTH, 512])
    wup = din("wup", [DEPTH, 16, 256])
    bup = din("bup", [DEPTH * 2, 128])
    gn_g = din("gn_g", [DEPTH, 512])
    b_cur = din("b_cur", [128, 1024])
    b_prev = din("b_prev", [128, 1024])
    b_prev0 = din("b_prev0", [128, 1024])
    b_smp = din("b_smp", [128, 8])
    cmask = din("cmask", [128, 8])

    yp = dout("yp", [NT, D])
    ys = dout("ys", [NS, D])
    okp = dout("okp", [DEPTH, 128, 128])
    ovp = dout("ovp", [DEPTH, 128, 128])
    ogp = dout("ogp", [DEPTH, 256, 128])
    oks = dout("oks", [DEPTH, NS, 128, 128])
    ovs = dout("ovs", [DEPTH, NS, 128, 128])
    ogs = dout("ogs", [DEPTH, NS, 256, 128])
    ocv = dout("ocv", [DEPTH, NS, 512])

    Xs = nc.dram_tensor("Xs", [NT, D], F32)
    Wib = nc.dram_tensor("Wib", [DEPTH, D, DIN], BF16)
    Wob = nc.dram_tensor("Wob", [DEPTH, DMIX, D], BF16)
    msg = nc.dram_tensor("msg", [128, MW], F32)
    gath = nc.dram_tensor("gath", [512, MW], F32)

    def sb(name, shape, dt=F32):
        return nc.alloc_sbuf_tensor(name, list(shape), dt)

    par = {"p": 0}
    DBL = set()

    class Dbl:
        def __init__(self, name, shape, dt=F32):
            if len(shape) == 2 and shape[1] <= 8:
                self.t = [nc.alloc_sbuf_tensor(f"{name}{i}", [128, 32], dt, align_bytes=128) for i in range(2)]
            else:
                self.t = [sb(f"{name}{i}", shape, dt) for i in range(2)]
            DBL.add(name)
            self.name = name

        def __getitem__(self, idx):
            return self.t[par["p"]][idx]

    S.keymap = lambda k: (k + str(par["p"])) if k in DBL else k

    Wi = sb("Wi", [128, 8, DIN], BF16)
    Wo = sb("Wo", [128, 12, D], BF16)
    ident = sb("ident", [128, 128], BF16)
    identf = sb("identf", [128, 128])
    maskLT = sb("maskLT", [128, 128])
    cmA = sb("cmA", [128, 128])
    eye16 = sb("eye16", [128, 16, 16], BF16)
    esel = sb("esel", [16, 16, 64], BF16)
    onesf = sb("onesf", [1, 128])
    w0rowf = sb("w0rowf", [1, 16])
    ones_bf = sb("ones_bf", [1, 128], BF16)
    ones64 = sb("ones64", [128, 64])
    cneg05 = sb("cneg05", [128, 8])
    bcur = sb("bcur", [128, 1024], BF16)
    bprev = sb("bprev", [128, 1024], BF16)
    bprev0 = sb("bprev0", [128, 1024], BF16)
    bsmp = sb("bsmp", [128, 8])
    cmk = sb("cmk", [128, 8])
    rowsA = sb("rowsA", [DEPTH * 8 + DEPTH * 2, 128])
    colsA = sb("colsA", [128, DEPTH * 10])
    rowsB = sb("rowsB", [DEPTH * 8, 128])
    bsT = sb("bsT", [128, DEPTH * 8])
    fG = sb("fG", [128, D])
    lnG = sb("lnG", [128, 512])
    lnB = sb("lnB", [128, 512])
    gnG = sb("gnG", [128, 512])
    esink = sb("esink", [128, 8])
    wupb = sb("wupb", [16, 256], BF16)
    WsT = sb("WsT", [128, 8, 128], BF16)
    w00 = sb("w00", [128, 8])
    b00 = sb("b00", [128, 8])
    Xb = [sb(f"Xb{i}", [128, D]) for i in range(2)]
    xsmp = sb("xsmp", [128, D])
    ssq = Dbl("ssq", [128, 8])
    nwI = Dbl("nwI", [128, 8], I32)
    nwA = Dbl("nwA", [128, 8])
    nwB = Dbl("nwB", [128, 8])
    rst = Dbl("rst", [128, 8])
    xsb = Dbl("xsb", [128, D], BF16)
    hT = Dbl("hT", [128, 8, 128], BF16)
    QTs = Dbl("QTs", [128, 512], BF16)
    KTb = [sb(f"KTb{i}", [128, 128], BF16) for i in range(3)]
    KTh = sb("KTh", [128, 128], BF16)
    Vaug = [sb(f"Vaug{i}", [128, 2, 72], BF16) for i in range(3)]
    Vaugh = sb("Vaugh", [128, 2, 72], BF16)
    kvtok = sb("kvtok", [128, 256])
    Tt = sb("Tt", [128, 512])
    Ga = Dbl("Ga", [128, 512], BF16)
    UG = Dbl("UG", [128, 512], BF16)
    Gc = Dbl("Gc", [128, 512], BF16)
    PT = sb("PT", [128, 4, 512], BF16)
    den = Dbl("den", [128, 8])
    rden = Dbl("rden", [128, 8])
    Y = Dbl("Y", [128, DMIX], BF16)
    bst = Dbl("bst", [128, 6])
    bmv = Dbl("bmv", [128, 2])
    vnf = sb("vnf", [128, 512])
    vnb = Dbl("vnb", [128, 512], BF16)
    lrTb = sb("lrTb", [16, 128], BF16)
    ee = sb("ee", [128, 256])
    aa = sb("aa", [128, 256])
    EbT = Dbl("EbT", [128, 256])
    EnbT = sb("EnbT", [128, 256])
    qdT = Dbl("qdT", [128, 256], BF16)
    kiT = Dbl("kiT", [128, 256], BF16)
    kitok = Dbl("kitok", [128, 256], BF16)
    vcb = Dbl("vcb", [128, 512], BF16)
    attT = sb("attT", [128, 512], BF16)
    Sst = sb("Sst", [128, 256])
    Sbf = [sb(f"Sbf{i}", [128, 256], BF16) for i in range(2)]
    Ptot = sb("Ptot", [128, 2])
    yT = sb("yT", [128, 12, 128], BF16)
    msgt = sb("msgt", [128, MW])
    KVw = sb("KVw", [128, NS, 128])
    KVx = sb("KVx", [128, NS * 144], BF16)
    KwT = KVx[:, 0:NS * 128].rearrange("p (i r) -> p i r", i=NS)
    Vaus = KVx[:, :].rearrange("p (i a e) -> p i a e", i=NS, a=2)
    Ls = sb("Ls", [128, 128])
    PTs = sb("PTs", [128, 128], BF16)
    PTm = PT[:, :, :].rearrange("p a (b c d) -> p (a b) c d", b=2, c=16)
    SS = sb("SS", [128, 4, 2, 128])
    SSb = sb("SSb", [128, 4, 2, 128], BF16)
    wsStage = SS[:, :, :, :].rearrange("p a b c -> p (a b) c")
    qTs = sb("qTs", [128, 256])
    kTs = sb("kTs", [128, 256])
    QM = sb("QM", [128, 2, 16, 16], BF16)

    psf = [nc.alloc_psum_tensor(f"psf{i}", [128, 512], F32) for i in range(6)]
    psb = [nc.alloc_psum_tensor(f"psb{i}", [128, 1024], BF16) for i in range(2)]
    ring = {"f": 0, "b": 0}
    pinned = set()

    def PSF(pin=False):
        i = ring["f"]
        while i in pinned:
            i = (i + 1) % 6
        ring["f"] = (i + 1) % 6
        if pin:
            pinned.add(i)
        return psf[i], f"psf{i}"

    def unpin(key):
        pinned.discard(int(key[3:]))

    def PSB():
        i = ring["b"]
        ring["b"] = (i + 1) % 2
        return psb[i], f"psb{i}"

    def mm(out, lhsT, rhs, start, stop, reads, writes, inc):
        S.op("pe", lambda e: e.matmul(out, lhsT=lhsT, rhs=rhs, start=start, stop=stop),
             reads=reads, writes=writes, inc=inc)

    def tr(out, in_, idn, reads, writes, inc=True):
        S.op("pe", lambda e: e.transpose(out=out, in_=in_, identity=idn), reads=reads, writes=writes, inc=inc)

    def act(out, in_, func, reads, writes, scale=1.0, bias=None, accum=None):
        kw = {}
        if bias is not None:
            kw["bias"] = bias
        if accum is not None:
            kw["accum_out"] = accum
        S.op("act", lambda e: e.activation(out=out, in_=in_, func=func, scale=scale, **kw), reads=reads, writes=writes)

    def tt(eng, out, in0, in1, op, reads, writes):
        S.op(eng, lambda e: e.tensor_tensor(out=out, in0=in0, in1=in1, op=op), reads=reads, writes=writes)

    def ts(eng, out, in0, s1, s2, op0, op1, reads, writes):
        if s2 is None:
            S.op(eng, lambda e: e.tensor_scalar(out=out, in0=in0, scalar1=s1, scalar2=None, op0=op0), reads=reads, writes=writes)
        else:
            S.op(eng, lambda e: e.tensor_scalar(out=out, in0=in0, scalar1=s1, scalar2=s2, op0=op0, op1=op1), reads=reads, writes=writes)

    def stt(eng, out, in0, scalar, in1, op0, op1, reads, writes):
        S.op(eng, lambda e: e.scalar_tensor_tensor(out=out, in0=in0, scalar=scalar, in1=in1, op0=op0, op1=op1),
             reads=reads, writes=writes)

    def cp(eng, out, in_, reads, writes):
        if eng == "act":
            S.op(eng, lambda e: e.activation(out=out, in_=in_, func=AF.Copy), reads=reads, writes=writes)
        else:
            S.op(eng, lambda e: e.tensor_copy(out=out, in_=in_), reads=reads, writes=writes)

    def rsqrt_eps(src, dst, n, scale, reads_key, writes_key):
        ts("dve", dst, src, scale, EPS, ALU.mult, ALU.add, [reads_key], [writes_key])
        yi = nwA[:, 0:n]
        S.op("dve", lambda e, o=nwI[:, 0:n], i=dst.bitcast(I32): e.tensor_scalar(out=o, in0=i, scalar1=1, scalar2=None,
                                                                                  op0=ALU.arith_shift_right),
             reads=[writes_key], writes=["nwI"])
        S.op("dve", lambda e, o=yi.bitcast(I32), i=nwI[:, 0:n]: e.tensor_scalar(out=o, in0=i, scalar1=-1.0, scalar2=1597463007.0,
                                                                                op0=ALU.mult, op1=ALU.add),
             reads=["nwI"], writes=["nwA"])
        for it in range(2):
            tt("dve", nwB[:, 0:n], yi, dst, ALU.mult, ["nwA", writes_key], ["nwB"])
            stt("dve", nwB[:, 0:n], nwB[:, 0:n], -0.5, yi, ALU.mult, ALU.mult, ["nwB", "nwA"], ["nwB"])
            if it == 0:
                stt("dve", yi, nwB[:, 0:n], 1.5, yi, ALU.add, ALU.mult, ["nwB", "nwA"], ["nwA"])
            else:
                stt("dve", dst, nwB[:, 0:n], 1.5, yi, ALU.add, ALU.mult, ["nwB", "nwA"], [writes_key])

    def wkeys():
        return [f"Wi{k}" for k in range(8)]

    def wkey(c0):
        return "WiA" if c0 < C_UB else ("WiB" if c0 < C_QC else "WiC")

    def proj_fm(ps_ap, pskey, c0, M, last=True):
        for k in range(8):
            mm(ps_ap, Wi[:, k, c0:c0 + M], hT[:, k, :], k == 0, k == 7, ["hT", wkey(c0)], [pskey], inc=(last and k == 7))

    def proj_tm(ps_ap, pskey, c0, N, last=True):
        for k in range(8):
            mm(ps_ap, hT[:, k, :], Wi[:, k, c0:c0 + N], k == 0, k == 7, ["hT", wkey(c0)], [pskey], inc=(last and k == 7))

    S.op("pool", lambda e: e.memset(identf[:, :], 0.0), writes=["identf"])
    S.op("pool", lambda e: e.affine_select(out=identf[:, :], in_=identf[:, :], pattern=[[-1, 128]],
                                           compare_op=ALU.not_equal, fill=1.0, base=0, channel_multiplier=1),
         reads=["identf"], writes=["identf"])
    cp("dve", ident[:, :], identf[:, :], ["identf"], ["ident"])
    S.op("pool", lambda e: e.memset(maskLT[:, :], 1.0), writes=["maskLT"])
    S.op("pool", lambda e: e.affine_select(out=maskLT[:, :], in_=maskLT[:, :], pattern=[[1, 128]],
                                           compare_op=ALU.is_ge, fill=0.0, base=0, channel_multiplier=-1),
         reads=["maskLT"], writes=["maskLT"])
    cp("dve", cmA[:, :], maskLT[:, :], ["maskLT"], ["cmA"])
    S.op("dve", lambda e: e.memset(cmA[0:64, 64:128], 0.0), reads=["cmA"], writes=["cmA"])
    S.op("pool", lambda e: e.memset(eye16[:, :, :], 0.0), writes=["eye16"])
    S.op("pool", lambda e: e.affine_select(out=eye16[:, :, :], in_=eye16[:, :, :], pattern=[[1, 16], [-1, 16]],
                                           compare_op=ALU.not_equal, fill=1.0, base=0, channel_multiplier=0),
         reads=["eye16"], writes=["eye16"])
    S.op("pool", lambda e: e.memset(esel[:, :, :], 0.0), writes=["esel"])
    S.op("pool", lambda e: e.affine_select(out=esel[:, :, :], in_=esel[:, :, :], pattern=[[1, 16], [0, 64]],
                                           compare_op=ALU.not_equal, fill=1.0, base=0, channel_multiplier=-1),
         reads=["esel"], writes=["esel"])
    S.op("dve", lambda e: e.memset(onesf[:, :], 1.0), writes=["onesf"])
    S.op("dve", lambda e: e.memset(ones_bf[:, :], 1.0), writes=["ones_bf"])
    S.op("dve", lambda e: e.memset(ones64[:, :], 1.0), writes=["ones64"])
    S.op("dve", lambda e: e.memset(cneg05[:, :], -0.5), writes=["cneg05"])
    S.op("dve", lambda e: e.memset(xsmp[:, :], 0.0), writes=["xsmp"])
    S.op("dve", lambda e: e.memset(kvtok[:, :], 0.0), writes=["kvtok"])
    S.op("dve", lambda e: e.memset(msgt[:, :], 0.0), writes=["msgt"])
    for d_ in (ssq, rst, den, rden, bst, bmv, nwA, nwB, Y):
        for i in range(2):
            S.op("dve", lambda e, t=d_.t[i]: e.memset(t[:, :], 1.0), writes=[f"{d_.name}{i}"])
    for i in range(2):
        S.op("dve", lambda e, t=nwI.t[i]: e.memset(t[:, :], 0), writes=[f"nwI{i}"])
    for i in range(3):
        S.op("dve", lambda e, i=i: e.memset(Vaug[i][:, :, :], 1.0), writes=[f"Vaug{i}"])
    S.op("dve", lambda e: e.memset(Vaugh[:, :, :], 1.0), writes=["Vaugh"])
    S.dma("pool", bcur[:, :], b_cur[:, :], writes=["bcur"])
    S.dma("pool", bprev[:, :], b_prev[:, :], writes=["bprev"])
    S.dma("pool", bprev0[:, :], b_prev0[:, :], writes=["bprev0"])
    S.dma("sp", bsmp[:, :], b_smp[:, :], writes=["bsmp"])
    S.dma("sp", cmk[:, :], cmask[:, :], writes=["cmk"])
    S.dma("sp", fG[:, :], fin_g[0:1, :].partition_broadcast(128), writes=["fG"])
    S.dma("sp", xsmp[0:NS, :], xsm[:, :], reads=[], writes=["xsmp"])
    nA = DEPTH * 8
    S.dma("sp", rowsA[0:nA, :], norm_g[:, :], writes=["rowsA"])
    S.dma("sp", rowsA[nA:nA + DEPTH * 2, :], bup[:, :], writes=["rowsA"])
    S.dma("sp", rowsB[:, :], sp_b[:, :], writes=["rowsB"])
    pA, kA = PSF()
    nr = DEPTH * 10
    tr(pA[:, 0:nr], rowsA[0:nr, :], identf[0:nr, 0:nr], ["rowsA", "identf"], [kA])
    cp("dve", colsA[:, 0:nA], pA[:, 0:nA], [kA], ["colsA"])
    ts("dve", colsA[:, nA:nr], pA[:, nA:nr], -1.0, None, ALU.mult, None, [kA], ["colsA"])
    pB, kB = PSF()
    tr(pB[:, 0:nA], rowsB[0:nA, :], identf[0:nA, 0:nA], ["rowsB", "identf"], [kB])
    cp("dve", bsT[:, :], pB[:, 0:nA], [kB], ["bsT"])

    def convert_layer(l):
        for k in range(4):
            S.dma_bg("pool", Wib.ap()[l, k * 256:(k + 1) * 256, :], w_in[l, k * 256:(k + 1) * 256, :], writes=[f"Wcv{l}"])
        for k in range(3):
            S.dma_bg("pool", Wob.ap()[l, k * 512:(k + 1) * 512, :], w_out[l, k * 512:(k + 1) * 512, :], writes=[f"Wcv{l}"])

    def load_layer_params(l):
        S.dma("sp", lnG[:, :], ln_g[l:l + 1, :].partition_broadcast(128), writes=["lnG"])
        S.dma("sp", lnB[:, :], ln_b[l:l + 1, :].partition_broadcast(128), writes=["lnB"])
        S.dma("sp", gnG[:, :], gn_g[l:l + 1, :].partition_broadcast(128), writes=["gnG"])
        S.dma("sp", esink[:, :], sinks[l:l + 1, :].partition_broadcast(128), writes=["esink"])
        act(esink[:, :], esink[:, :], AF.Exp, ["esink"], ["esink"])
        S.dma("pool", wupb[:, :], wup[l, :, :], writes=["wupb"])
        S.dma("sp", wsStage, sp_w[l].rearrange("h i j -> i h j"), writes=["SS"])
        for half in range(2):
            pw, kw = PSF()
            for hq in range(4):
                h = half * 4 + hq
                tr(pw[:, hq * 128:(hq + 1) * 128], wsStage[:, h, :], identf[:, :], ["SS", "identf"], [kw], inc=(hq == 3))
            tt("dve", WsT[:, half * 4:half * 4 + 4, :], pw[:, :].rearrange("p (h i) -> p h i", h=4),
               maskLT[:, :].unsqueeze(1).to_broadcast([128, 4, 128]), ALU.mult, [kw, "maskLT"], ["WsT"])
        cp("dve", w0rowf[0:1, 0:8], wsStage[0:1, :, 0], ["SS"], ["w0rowf"])
        cp("dve", w0rowf[0:1, 8:16], bsT[0:1, l * 8:(l + 1) * 8], ["bsT"], ["w0rowf"])
        pw, kw = PSF()
        mm(pw[:, 0:16], onesf[0:1, :], w0rowf[0:1, :], True, True, ["onesf", "w0rowf"], [kw], True)
        cp("dve", w00[:, :], pw[:, 0:8], [kw], ["w00"])
        cp("dve", b00[:, :], pw[:, 8:16], [kw], ["b00"])
        if l == 0:
            q, wsrc, wosrc, rk = "pool", w_in[l], w_out[l], []
        else:
            q, wsrc, wosrc, rk = "sp", Wib.ap()[l], Wob.ap()[l], [f"Wcv{l}"]
        for k in range(8):
            src = wsrc[k * 128:(k + 1) * 128, :]
            S.dma(q, Wi[:, k, C_QC:DIN], src[:, C_QC:DIN], reads=rk, writes=["WiC"])
        for k in range(8):
            src = wsrc[k * 128:(k + 1) * 128, :]
            for a_ in range(2):
                S.dma(q, Wi[:, k, 0:512].rearrange("p (j a i) -> p j a i", j=4, a=2, i=64)[:, :, a_, :],
                      src[:, a_ * 256:(a_ + 1) * 256].rearrange("p (j i) -> p j i", j=4, i=64), reads=rk, writes=["WiA"])
            S.dma(q, Wi[:, k, 512:C_UB], src[:, 512:C_UB], reads=rk, writes=["WiA"])
        for k in range(8):
            src = wsrc[k * 128:(k + 1) * 128, :]
            S.dma(q, Wi[:, k, C_UB:C_QC], src[:, C_UB:C_QC], reads=rk, writes=["WiB"])
        for k in range(12):
            S.dma(q, Wo[:, k, :], wosrc[k * 128:(k + 1) * 128, :], reads=rk, writes=[f"Wo{k}"])

    prefetched = set()

    def load_x(l, blk, ph=2):
        xt = Xb[blk % 2]
        key = f"Xb{blk % 2}"
        if (l, blk, ph) in prefetched:
            return xt, key
        src = xp if l == 0 else Xs.ap()
        S.dma("sp", xt[:, :], src[blk * 128:(blk + 1) * 128, :], reads=[f"Xs{blk}"], writes=[key])
        return xt, key

    def norm_T(l, xt, xkey):
        act(xsb[:, :], xt[:, :], AF.Square, [xkey], ["xsb", "ssq"], accum=ssq[:, 0:1])
        rsqrt_eps(ssq[:, 0:1], rst[:, 0:1], 1, 1.0 / D, "ssq", "rst")
        act(xsb[:, :], xt[:, :], AF.Copy, [xkey, "rst"], ["xsb"], scale=rst[:, 0:1])

    def norm_T2(l):
        pt, kt = PSB()
        for k in range(8):
            tr(pt[:, k * 128:(k + 1) * 128], xsb[:, k * 128:(k + 1) * 128], ident[:, :], ["xsb", "ident"], [kt], inc=(k == 7))
        tt("dve", hT[:, :, :], pt[:, :].rearrange("p (k t) -> p k t", k=8),
           colsA[:, l * 8:(l + 1) * 8].unsqueeze(2).to_broadcast([128, 8, 128]), ALU.mult, [kt, "colsA"], ["hT"])

    def proj_kv(cur, want_tok):
        pk, kk = PSF()
        proj_fm(pk[:, 0:128], kk, C_KA, 128, last=False)
        proj_tm(pk[:, 128:384], kk, C_KA, 256)
        cp("act", KTb[cur][:, :], pk[:, 0:128], [kk], [f"KTb{cur}"])
        for a in range(2):
            cp("dve", Vaug[cur][:, a, 0:64], pk[:, 256 + a * 64:256 + (a + 1) * 64], [kk], [f"Vaug{cur}"])
        if want_tok:
            cp("dve", kvtok[:, :], pk[:, 128:384], [kk], ["kvtok"])
        return pk, kk

    def gate(c0, out_tile, out_key):
        pg, kg = PSF()
        proj_tm(pg[:, :], kg, c0, 512)
        act(Tt[:, :], pg[:, :], AF.Tanh, [kg], ["Tt"], scale=0.5)
        stt("dve", out_tile[:, :], Tt[:, :], 1.0, pg[:, :], ALU.add, ALU.mult, ["Tt", kg], [out_key])

    def gla_decay_a(l):
        pl, kl = PSF()
        proj_fm(pl[0:16, 0:128], kl, C_LR, 16)
        cp("act", lrTb[:, :], pl[0:16, 0:128], [kl], ["lrTb"])

    def gla_decay_b(l):
        pz, kz = PSF()
        for hh in range(2):
            mm(pz[:, hh * 128:(hh + 1) * 128], wupb[0:16, hh * 128:(hh + 1) * 128], lrTb[0:16, :], True, True,
               ["wupb", "lrTb"], [kz], inc=(hh == 1))
        for hh in range(2):
            c = DEPTH * 8 + l * 2 + hh
            act(ee[:, hh * 128:(hh + 1) * 128], pz[:, hh * 128:(hh + 1) * 128], AF.Exp, [kz, "colsA"], ["ee"],
                scale=-1.0, bias=colsA[:, c:c + 1])
        act(ee[:, :], ee[:, :], AF.Ln, ["ee"], ["ee"], bias=1.0)
        act(aa[:, :], ee[:, :], AF.Exp, ["ee"], ["aa"], scale=-1.0 / 16.0)

    def gla_decay_c():
        for hh in range(2):
            for c in range(2):
                o = hh * 128 + c * 64
                S.op("dve", lambda e, o=o, eo=EbT[:, o:o + 64]: e.tensor_tensor_scan(out=eo, data0=aa[:, o:o + 64], data1=ones64[:, :],
                                                                                   initial=1.0, op0=ALU.mult, op1=ALU.mult),
                     reads=["aa", "ones64"], writes=["EbT"])
        S.op("dve", lambda e, ei=EbT[:, :]: e.reciprocal(out=EnbT[:, :], in_=ei), reads=["EbT"], writes=["EnbT"])

    def gla_decay(l, is_sample):
        gla_decay_a(l)
        gla_decay_b(l)
        if not is_sample:
            gla_decay_c()

    def gla_state_chunk(c, want_p, want_bf=True):
        pd, kd = PSF()
        for h in range(4):
            hh, hp = h // 2, h % 2
            mm(pd[hp * 64:(hp + 1) * 64, hh * 128:(hh + 1) * 128], kitok[c * 64:(c + 1) * 64, h * 64:(h + 1) * 64],
               vcb[c * 64:(c + 1) * 64, h * 128:(h + 1) * 128], True, True, ["kitok", "vcb"], [kd], inc=(h == 3))
        tt("dve", Sst[:, :], Sst[:, :], pd[:, 0:256], ALU.add, ["Sst", kd], ["Sst"])
        for hh in range(2):
            col = hh * 128 + c * 64 + 63
            ts("dve", Sst[:, hh * 128:(hh + 1) * 128], Sst[:, hh * 128:(hh + 1) * 128], EbT[:, col:col + 1], None,
               ALU.mult, None, ["Sst", "EbT"], ["Sst"])
        if want_bf:
            cp("act", Sbf[(c + 1) % 2][:, :], Sst[:, :], ["Sst"], [f"Sbf{(c + 1) % 2}"])
        if want_p:
            a_ap = EbT[:, :].rearrange("p (hh t) -> p hh t", hh=2)[:, :, c * 64 + 63]
            tt("dve", Ptot[:, :], Ptot[:, :], a_ap, ALU.mult, ["Ptot", "EbT"], ["Ptot"])

    def kvp_a():
        pqk, kq = PSF()
        for hh in range(2):
            proj_fm(pqk[:, 256 + hh * 128:256 + (hh + 1) * 128], kq, C_KC + hh * 128, 128, last=(hh == 1))
        tt("dve", kiT[:, :], pqk[:, 256:512], EnbT[:, :], ALU.mult, [kq, "EnbT"], ["kiT"])

    def kvp_b():
        pt, kt = PSB()
        for hh in range(2):
            tr(pt[:, hh * 128:(hh + 1) * 128], kiT[:, hh * 128:(hh + 1) * 128], ident[:, :], ["kiT", "ident"], [kt], inc=(hh == 1))
        cp("act", kitok[:, :], pt[:, 0:256], [kt], ["kitok"])
        pv, kv = PSF()
        proj_tm(pv[:, :], kv, C_VC, 512)
        cp("act", vcb[:, :], pv[:, :], [kv], ["vcb"])

    def gla_kv_proj():
        kvp_a()
        kvp_b()

    def group_norm_out(po, kpo, rows):
        for h in range(4):
            act(xsb[0:rows, h * 128:(h + 1) * 128], po[0:rows, h * 128:(h + 1) * 128], AF.Square, [kpo], ["xsb", "ssq"],
                accum=ssq[0:rows, 4 + h:5 + h])
        rsqrt_eps(ssq[:, 4:8], rst[:, 4:8], 4, 1.0 / 128.0, "ssq", "rst")
        for h in range(4):
            stt("dve", Y[0:rows, 1024 + h * 128:1024 + (h + 1) * 128], po[0:rows, h * 128:(h + 1) * 128], rst[0:rows, 4 + h:5 + h],
                Gc[0:rows, h * 128:(h + 1) * 128], ALU.mult, ALU.mult, [kpo, "rst", "Gc"], ["Y"])

    def attn_norm_out(pos, kpos, rows):
        for kv in range(2):
            v = pos[kv][0:rows, 0:260].rearrange("p (g e) -> p g e", g=4)
            tt("dve", den[0:rows, kv * 4:(kv + 1) * 4], v[:, :, 64], esink[0:rows, kv * 4:(kv + 1) * 4], ALU.add,
               [kpos[kv], "esink"], ["den"])
        S.op("dve", lambda e, ro=rden[0:rows, 0:8], di=den[0:rows, 0:8]: e.reciprocal(out=ro, in_=di), reads=["den"], writes=["rden"])
        for h in range(8):
            kv, g = h // 4, h % 4
            stt("dve", Y[0:rows, h * 64:(h + 1) * 64], pos[kv][0:rows, g * 65:g * 65 + 64], rden[0:rows, h:h + 1],
                Ga[0:rows, h * 64:(h + 1) * 64], ALU.mult, ALU.mult, [kpos[kv], "rden", "Ga"], ["Y"])

    def ln_v(l):
        pv, kv = PSF()
        proj_tm(pv[:, :], kv, C_VB, 512)
        S.op("dve", lambda e, bo=bst[:, 0:6], pi=pv[:, :]: e.bn_stats(out=bo, in_=pi), reads=[kv], writes=["bst"])
        S.op("dve", lambda e, mo=bmv[:, 0:2], bi=bst[:, 0:6]: e.bn_aggr(out=mo, in_=bi), reads=["bst"], writes=["bmv"])
        rsqrt_eps(bmv[:, 1:2], rst[:, 1:2], 1, 1.0, "bmv", "rst")
        stt("dve", vnf[:, :], pv[:, :], bmv[:, 0:1], lnG[:, :], ALU.subtract, ALU.mult, [kv, "bmv", "lnG"], ["vnf"])
        stt("dve", vnf[:, :], vnf[:, :], rst[:, 1:2], lnB[:, :], ALU.mult, ALU.add, ["vnf", "rst", "lnB"], ["vnf"])

    def out_proj(l, xt, xkey, rows, dst_ap, dst_key, final):
        out_proj_a()
        out_proj_b(l, xt, xkey, rows, final)

    def out_proj_a():
        pt0, kt0 = PSB()
        for k in range(8):
            tr(pt0[:, k * 128:(k + 1) * 128], Y[:, k * 128:(k + 1) * 128], ident[:, :], ["Y", "ident"], [kt0], inc=(k == 7))
        cp("act", yT[:, 0:8, :], pt0[:, :].rearrange("p (k t) -> p k t", k=8), [kt0], ["yT"])
        pt1, kt1 = PSB()
        for k in range(4):
            tr(pt1[:, k * 128:(k + 1) * 128], Y[:, (8 + k) * 128:(9 + k) * 128], ident[:, :], ["Y", "ident"], [kt1], inc=(k == 3))
        cp("dve", yT[:, 8:12, :], pt1[:, 0:512].rearrange("p (k t) -> p k t", k=4), [kt1], ["yT"])

    def out_proj_b(l, xt, xkey, rows, final):
        for n in range(2):
            po, ko = PSF()
            for k in range(12):
                mm(po[:, :], yT[:, k, :], Wo[:, k, n * 512:(n + 1) * 512], k == 0, k == 11, ["yT", f"Wo{k}"], [ko], inc=(k == 11))
            stt("dve", xt[0:rows, n * 512:(n + 1) * 512], po[0:rows, :], 0.5, xt[0:rows, n * 512:(n + 1) * 512], ALU.mult, ALU.add,
                [ko, xkey], [xkey])
        if not final:
            return
        act(xsb[0:rows, :], xt[0:rows, :], AF.Square, [xkey], ["xsb", "ssq"], accum=ssq[0:rows, 2:3])
        rsqrt_eps(ssq[:, 2:3], rst[:, 2:3], 1, 1.0 / D, "ssq", "rst")
        stt("dve", xt[0:rows, :], xt[0:rows, :], rst[0:rows, 2:3], fG[0:rows, :], ALU.mult, ALU.mult, [xkey, "rst", "fG"], [xkey])

    def phase1_block(l, blk):
        p = blk % 2
        par["p"] = p
        xt, xkey = load_x(l, blk, 1)
        norm_T(l, xt, xkey)
        yield
        par["p"] = p
        norm_T2(l)
        yield
        par["p"] = p
        gla_decay_a(l)
        if blk == NB - 1:
            pk, kk = PSF()
            proj_fm(pk[:, 0:128], kk, C_KA, 128, last=False)
            proj_tm(pk[:, 128:384], kk, C_KA, 256)
            cp("dve", msgt[:, 256:384], pk[:, 0:128], [kk], ["msgt"])
            cp("dve", msgt[:, 384:512], pk[:, 256:384], [kk], ["msgt"])
        yield
        par["p"] = p
        gla_decay_b(l)
        yield
        par["p"] = p
        gla_decay_c()
        kvp_a()
        yield
        par["p"] = p
        kvp_b()
        yield
        par["p"] = p
        for c in range(2):
            gla_state_chunk(c, True, False)

    def exchange_a(l):
        cp("dve", msgt[:, 0:256], Sst[:, :], ["Sst"], ["msgt"])
        cp("dve", msgt[:, 512:514], Ptot[:, :], ["Ptot"], ["msgt"])
        S.dma("pool", msg.ap(), msgt[:, :], reads=["msgt"], writes=["msg"])
        S.allgather(msg.ap().opt(), gath.ap().opt(), [[0, 1, 2, 3], [4, 5, 6, 7]], reads=["msg"], writes=["gath"])

    def exchange_b(l):
        S.op("dve", lambda e: e.memset(Sst[:, :], 0.0), reads=[], writes=["Sst"])
        S.op("dve", lambda e: e.memset(aa[:, :], 0.0), reads=[], writes=["aa"])
        for j in range(3):
            S.dma("pool", msgt[:, :], gath.ap()[j * 128:(j + 1) * 128, :], reads=["gath"], writes=["msgt"])
            for hh in range(2):
                stt("dve", ee[:, hh * 128:(hh + 1) * 128], Sst[:, hh * 128:(hh + 1) * 128], msgt[:, 512 + hh:513 + hh],
                    msgt[:, hh * 128:(hh + 1) * 128], ALU.mult, ALU.add, ["Sst", "msgt"], ["ee"])
            tt("dve", ee[:, :], ee[:, :], Sst[:, :], ALU.subtract, ["ee", "Sst"], ["ee"])
            stt("dve", Sst[:, :], ee[:, :], cmk[:, j:j + 1], Sst[:, :], ALU.mult, ALU.add, ["ee", "cmk", "Sst"], ["Sst"])
            stt("dve", aa[:, :], msgt[:, 256:512], cmk[:, 3 + j:4 + j], aa[:, :], ALU.mult, ALU.add, ["msgt", "cmk", "aa"], ["aa"])
        cp("act", Sbf[0][:, :], Sst[:, :], ["Sst"], ["Sbf0"])
        cp("act", KTh[:, :], aa[:, 0:128], ["aa"], ["KTh"])
        cp("dve", Vaugh[:, :, 0:64], aa[:, 128:256].rearrange("p (a d) -> p a d", a=2), ["aa"], ["Vaugh"])

    def phase2_block(l, blk, last_layer):
        p = blk % 2
        cur = blk % 3
        prv = (blk + 2) % 3
        par["p"] = p
        xt, xkey = load_x(l, blk)
        norm_T(l, xt, xkey)
        yield
        par["p"] = p
        norm_T2(l)
        yield
        par["p"] = p
        pq, kq = PSF()
        for j in range(4):
            proj_fm(pq[:, j * 128:(j + 1) * 128], kq, C_QA + j * 128, 128, last=(j == 3))
        act(QTs[:, :], pq[:, :], AF.Copy, [kq], ["QTs"], scale=0.125)
        proj_kv(cur, blk == NB - 1)
        if blk == NB - 1:
            S.dma("sp", okp[l, :, :], kvtok[:, 0:128], reads=["kvtok"])
            S.dma("sp", ovp[l, :, :], kvtok[:, 128:256], reads=["kvtok"])
        gate(C_GA, Ga, "Ga")
        yield
        par["p"] = p
        gate(C_GB, UG, "UG")
        pu, ku = PSF()
        proj_tm(pu[:, :], ku, C_UB, 512)
        tt("dve", UG[:, :], pu[:, :], UG[:, :], ALU.mult, [ku, "UG"], ["UG"])
        ln_v(l)
        cp("act", vnb[:, :], vnf[:, :], ["vnf"], ["vnb"])
        yield
        par["p"] = p
        gla_decay_a(l)
        gate(C_GC, Gc, "Gc")
        tt("dve", Gc[:, :], Gc[:, :], gnG[:, :], ALU.mult, ["Gc", "gnG"], ["Gc"])
        yield
        par["p"] = p
        gla_decay_b(l)
        yield
        par["p"] = p
        gla_decay_c()
        pqk, kqk = PSF()
        for hh in range(2):
            proj_fm(pqk[:, hh * 128:(hh + 1) * 128], kqk, C_QC + hh * 128, 128, last=(hh == 1))
        stt("dve", qdT[:, :], pqk[:, 0:256], 0.125, EbT[:, :], ALU.mult, ALU.mult, [kqk, "EbT"], ["qdT"])
        kvp_a()
        yield
        par["p"] = p
        kvp_b()
        yield
        par["p"] = p
        if blk == 0:
            kprev, kprev_key, vprev, vprev_key, bp, bp_key = KTh, "KTh", Vaugh, "Vaugh", bprev0, "bprev0"
        else:
            kprev, kprev_key, vprev, vprev_key, bp, bp_key = KTb[prv], f"KTb{prv}", Vaug[prv], f"Vaug{prv}", bprev, "bprev"
        for kv in range(2):
            for kb in range(2):
                kt_, ktk, bt, btk = (kprev, kprev_key, bp, bp_key) if kb == 0 else (KTb[cur], f"KTb{cur}", bcur, "bcur")
                ps, kps = PSF()
                mm(ps[:, :], kt_[kv * 64:(kv + 1) * 64, :], QTs[kv * 64:(kv + 1) * 64, :], True, False, [ktk, "QTs"], [kps], False)
                mm(ps[:, :], ident[:, :], bt[:, kv * 512:(kv + 1) * 512], False, True, ["ident", btk], [kps], True)
                act(PT[:, kv * 2 + kb, :], ps[:, :], AF.Exp, [kps], ["PT"])
        yield
        par["p"] = p
        pos, kpos = [], []
        for kv in range(2):
            po, kpo = PSF()
            pos.append(po)
            kpos.append(kpo)
            for g in range(4):
                for kb in range(2):
                    va, vak = (vprev, vprev_key) if kb == 0 else (Vaug[cur], f"Vaug{cur}")
                    mm(po[:, g * 65:(g + 1) * 65], PT[:, kv * 2 + kb, g * 128:(g + 1) * 128], va[:, kv, 0:65], kb == 0, kb == 1,
                       ["PT", vak], [kpo], inc=(g == 3 and kb == 1))
        attn_norm_out(pos, kpos, 128)
        yield
        par["p"] = p
        pm, km = PSF()
        for h in range(8):
            mm(pm[:, h * 64:(h + 1) * 64], WsT[:, h, :], vnb[:, h * 64:(h + 1) * 64], True, True, ["WsT", "vnb"], [km], inc=(h == 7))
        for h in range(8):
            stt("dve", Y[:, 512 + h * 64:512 + (h + 1) * 64], pm[:, h * 64:(h + 1) * 64], bsT[:, l * 8 + h:l * 8 + h + 1],
                UG[:, h * 64:(h + 1) * 64], ALU.add, ALU.mult, [km, "bsT", "UG"], ["Y"])
        yield
        par["p"] = p
        pas = [PSF(), PSF()]
        for hp in range(2):
            pa, ka = pas[hp]
            for hh in range(2):
                mm(pa[:, hh * 128:(hh + 1) * 128], kiT[hp * 64:(hp + 1) * 64, hh * 128:(hh + 1) * 128],
                   qdT[hp * 64:(hp + 1) * 64, hh * 128:(hh + 1) * 128], True, True, ["kiT", "qdT"], [ka], inc=(hh == 1))
        for h in range(4):
            hh, hp = h // 2, h % 2
            pa, ka = pas[hp]
            tt("dve", attT[:, h * 128:(h + 1) * 128], pa[:, hh * 128:(hh + 1) * 128], cmA[:, :], ALU.mult, [ka, "cmA"], ["attT"])
        gla_state_chunk(0, False)
        yield
        par["p"] = p
        po, kpo = PSF()
        for h in range(4):
            hh, hp = h // 2, h % 2
            mm(po[:, h * 128:(h + 1) * 128], attT[:, h * 128:(h + 1) * 128], vcb[:, h * 128:(h + 1) * 128], True, False,
               ["attT", "vcb"], [kpo], False)
            for c in range(2):
                mm(po[c * 64:(c + 1) * 64, h * 128:(h + 1) * 128],
                   qdT[hp * 64:(hp + 1) * 64, hh * 128 + c * 64:hh * 128 + (c + 1) * 64],
                   Sbf[c][hp * 64:(hp + 1) * 64, hh * 128:(hh + 1) * 128], False, True, ["qdT", f"Sbf{c}"], [kpo],
                   inc=(h == 3 and c == 1))
        gla_state_chunk(1, False)
        group_norm_out(po, kpo, 128)
        yield
        par["p"] = p
        out_proj_a()
        yield
        par["p"] = p
        out_proj_b(l, xt, xkey, 128, last_layer)
        if last_layer:
            S.dma("sp", yp[blk * 128:(blk + 1) * 128, :], xt[:, :], reads=[xkey])
        else:
            S.dma("sp", Xs.ap()[blk * 128:(blk + 1) * 128, :], xt[:, :], reads=[xkey], writes=[f"Xs{blk}"])

    def pipeline(gens, window=2, tag=""):
        import os
        window = int(os.environ.get("KWIN" + tag, str(window)))
        pending = list(gens)
        active = []
        while pending or active:
            nxt = []
            for g in active:
                try:
                    next(g)
                    nxt.append(g)
                except StopIteration:
                    pass
            active = nxt
            if pending and len(active) < window:
                g = pending.pop(0)
                try:
                    next(g)
                    active.append(g)
                except StopIteration:
                    pass

    def sample_block(l, last_layer):
        R = NS
        par["p"] = 0
        norm_T(l, xsmp, "xsmp")
        norm_T2(l)
        pq, kq = PSF()
        for j in range(4):
            proj_fm(pq[:, j * 128:(j + 1) * 128], kq, C_QA + j * 128, 128, last=(j == 3))
        act(QTs[:, :], pq[:, :], AF.Copy, [kq], ["QTs"], scale=0.125)
        pk, kk = PSF()
        proj_tm(pk[:, 0:256], kk, C_KA, 256)
        cp("dve", kvtok[:, :], pk[:, 0:256], [kk], ["kvtok"])
        gate(C_GA, Ga, "Ga")
        S.dma("sp", KVw[127:128, :, :], kvtok[0:NS, 0:128], reads=["kvtok"], writes=["KVw"])
        S.dma("sp", oks[l].rearrange("i r f -> r i f"), KVw[:, :, :], reads=["KVw"])
        for q4 in range(4):
            pt, kt = PSF()
            for ii in range(4):
                i = q4 * 4 + ii
                tr(pt[:, ii * 128:(ii + 1) * 128], KVw[:, i, :], identf[:, :], ["KVw", "identf"], [kt], inc=(ii == 3))
            cp("act", KwT[:, q4 * 4:(q4 + 1) * 4, :], pt[:, :].rearrange("p (i r) -> p i r", i=4), [kt], ["KVx"])
        QTv = QTs[:, :].rearrange("p (j t) -> p j t", j=4)
        pls = [PSF(), PSF()]
        for kv in range(2):
            pl, kl = pls[kv]
            for i in range(NS):
                mm(pl[:, i * 4:i * 4 + 4], KwT[kv * 64:(kv + 1) * 64, i, :], QTv[kv * 64:(kv + 1) * 64, :, i],
                   True, True, ["KVx", "QTs"], [kl], inc=(i == NS - 1))
        for kv in range(2):
            pl, kl = pls[kv]
            tt("dve", Ls[:, kv * 64:(kv + 1) * 64].rearrange("p (i j) -> p i j", j=4), pl[:, 0:64].rearrange("p (i j) -> p i j", j=4),
               bsmp[:, kv * 4:(kv + 1) * 4].unsqueeze(1).to_broadcast([128, NS, 4]), ALU.add, [kl, "bsmp"], ["Ls"])
        act(PTs[:, :], Ls[:, :], AF.Exp, ["Ls"], ["PTs"])
        for kv in range(2):
            tt("dve", PTm[:, kv * 4:(kv + 1) * 4, :, :],
               PTs[:, kv * 64:(kv + 1) * 64].rearrange("p (i j) -> p j i", j=4).unsqueeze(3).to_broadcast([128, 4, 16, 16]),
               eye16[:, :, :].unsqueeze(1).to_broadcast([128, 4, 16, 16]), ALU.mult, ["PTs", "eye16"], ["PT"])
        S.dma("sp", KVw[0:127, :, :], swv[l, :, 1:128, :].rearrange("i r f -> r i f"), writes=["KVw"])
        S.dma("sp", KVw[127:128, :, :], kvtok[0:NS, 128:256], reads=["kvtok"], writes=["KVw"])
        S.dma("sp", ovs[l].rearrange("i r f -> r i f"), KVw[:, :, :], reads=["KVw"])
        S.op("dve", lambda e: e.memset(KVx[:, :], 1.0), reads=[], writes=["KVx"])
        cp("dve", Vaus[:, :, :, 0:64], KVw[:, :, :].rearrange("p i (a d) -> p i a d", a=2), ["KVw"], ["KVx"])
        pos, kpos = [], []
        for kv in range(2):
            po, kpo = PSF()
            pos.append(po)
            kpos.append(kpo)
            for g in range(4):
                h = kv * 4 + g
                for i in range(NS):
                    mm(po[0:R, g * 65:(g + 1) * 65], PTm[:, h, i, :], Vaus[:, i, kv, 0:65], i == 0, i == NS - 1, ["PT", "KVx"], [kpo],
                       inc=(g == 3 and i == NS - 1))
        attn_norm_out(pos, kpos, R)
        gate(C_GB, UG, "UG")
        pu, ku = PSF()
        proj_tm(pu[:, :], ku, C_UB, 512)
        tt("dve", UG[:, :], pu[:, :], UG[:, :], ALU.mult, [ku, "UG"], ["UG"])
        ln_v(l)
        S.dma("sp", ocv[l, :, :], vnf[0:R, :], reads=["vnf"])
        tt("dve", Tt[0:R, :].rearrange("p (h d) -> p h d", h=8), vnf[0:R, :].rearrange("p (h d) -> p h d", h=8),
           w00[0:R, :].unsqueeze(2).to_broadcast([R, 8, 64]), ALU.mult, ["vnf", "w00"], ["Tt"])
        tt("dve", Tt[0:R, :].rearrange("p (h d) -> p h d", h=8), Tt[0:R, :].rearrange("p (h d) -> p h d", h=8),
           b00[0:R, :].unsqueeze(2).to_broadcast([R, 8, 64]), ALU.add, ["Tt", "b00"], ["Tt"])
        tt("dve", Y[0:R, 512:1024], Tt[0:R, :], UG[0:R, :], ALU.mult, ["Tt", "UG"], ["Y"])
        gla_decay(l, True)
        pqk, kqk = PSF()
        for hh in range(2):
            proj_fm(pqk[:, hh * 128:(hh + 1) * 128], kqk, C_QC + hh * 128, 128, last=False)
        for hh in range(2):
            proj_fm(pqk[:, 256 + hh * 128:256 + (hh + 1) * 128], kqk, C_KC + hh * 128, 128, last=(hh == 1))
        ts("dve", qTs[:, :], pqk[:, 0:256], 0.125, None, ALU.mult, None, [kqk], ["qTs"])
        cp("dve", kTs[:, :], pqk[:, 256:512], [kqk], ["kTs"])
        pv, kv_ = PSF()
        proj_tm(pv[:, :], kv_, C_VC, 512)
        cp("act", vcb[:, :], pv[:, :], [kv_], ["vcb"])
        gate(C_GC, Gc, "Gc")
        tt("dve", Gc[:, :], Gc[:, :], gnG[:, :], ALU.mult, ["Gc", "gnG"], ["Gc"])
        q_v = qTs[:, :].rearrange("p (hh t) -> p hh t", hh=2)[:, :, 0:NS]
        tt("dve", QM[:, :, :, :], q_v.unsqueeze(3).to_broadcast([128, 2, 16, 16]),
           eye16[:, :, :].unsqueeze(1).to_broadcast([128, 2, 16, 16]), ALU.mult, ["qTs", "eye16"], ["QM"])
        a_v = aa[:, :].rearrange("p (hh t) -> p hh t", hh=2)
        for q4 in range(NS // 4):
            S.dma("sp", SS[:, :, :, :], sgl[l, q4 * 4:(q4 + 1) * 4].rearrange("i (hh p) v -> p i hh v", p=128), writes=["SS"])
            tt("dve", SS[:, :, :, :], SS[:, :, :, :],
               a_v[:, :, q4 * 4:(q4 + 1) * 4].rearrange("p hh i -> p i hh").unsqueeze(3).to_broadcast([128, 4, 2, 128]),
               ALU.mult, ["SS", "aa"], ["SS"])
            for i2 in range(2):
                pvb, kvb = PSF()
                for il in range(2):
                    i = q4 * 4 + i2 * 2 + il
                    for h in range(4):
                        hh, hp = h // 2, h % 2
                        mm(pvb[hp * 64:(hp + 1) * 64, (il * 2 + hh) * 128:(il * 2 + hh + 1) * 128], esel[0:NS, i, :],
                           vcb[0:NS, h * 128:(h + 1) * 128], True, True, ["esel", "vcb"], [kvb], inc=(il == 1 and h == 3))
                for il in range(2):
                    i = q4 * 4 + i2 * 2 + il
                    for hh in range(2):
                        stt("dve", SS[:, i2 * 2 + il, hh, :], pvb[:, (il * 2 + hh) * 128:(il * 2 + hh + 1) * 128],
                            kTs[:, hh * 128 + i:hh * 128 + i + 1], SS[:, i2 * 2 + il, hh, :], ALU.mult, ALU.add,
                            [kvb, "kTs", "SS"], ["SS"])
            S.dma("sp", ogs[l, q4 * 4:(q4 + 1) * 4].rearrange("i (hh p) v -> p i hh v", p=128), SS[:, :, :, :], reads=["SS"])
            cp("act", SSb[:, :, :, :], SS[:, :, :, :], ["SS"], ["SSb"])
            pos2 = [PSF(), PSF()]
            for hp in range(2):
                po, kpo = pos2[hp]
                for hh in range(2):
                    for il in range(4):
                        i = q4 * 4 + il
                        mm(po[0:R, hh * 128:(hh + 1) * 128], QM[hp * 64:(hp + 1) * 64, hh, i, :], SSb[hp * 64:(hp + 1) * 64, il, hh, :],
                           il == 0, il == 3, ["QM", "SSb"], [kpo], inc=(hh == 1 and il == 3))
            for h in range(4):
                hh, hp = h // 2, h % 2
                po, kpo = pos2[hp]
                if q4 == 0:
                    cp("dve", vnf[0:R, h * 128:(h + 1) * 128], po[0:R, hh * 128:(hh + 1) * 128], [kpo], ["vnf"])
                else:
                    tt("dve", vnf[0:R, h * 128:(h + 1) * 128], vnf[0:R, h * 128:(h + 1) * 128], po[0:R, hh * 128:(hh + 1) * 128],
                       ALU.add, ["vnf", kpo], ["vnf"])
        group_norm_out(vnf, "vnf", R)
        out_proj(l, xsmp, "xsmp", R, None, None, last_layer)
        if last_layer:
            S.dma("sp", ys[:, :], xsmp[0:R, :], reads=["xsmp"])

    for l in range(DEPTH):
        par["p"] = 0
        for blk in range(min(2, NB)):
            load_x(l, blk, 1)
            prefetched.add((l, blk, 1))
        load_layer_params(l)
        if l + 1 < DEPTH:
            convert_layer(l + 1)
        S.dma("sp", KVw[0:127, :, :], swk[l, :, 1:128, :].rearrange("i r f -> r i f"), writes=["KVw"])
        S.op("dve", lambda e: e.memset(Sst[:, :], 0.0), reads=[], writes=["Sst"])
        S.op("dve", lambda e: e.memset(Ptot[:, :], 1.0), reads=[], writes=["Ptot"])
        pipeline([phase1_block(l, blk) for blk in range(NB)], tag="1")
        par["p"] = 0
        exchange_a(l)
        sample_block(l, l == DEPTH - 1)
        par["p"] = 0
        exchange_b(l)
        pipeline([phase2_block(l, blk, l == DEPTH - 1) for blk in range(NB)], tag="2")
        par["p"] = 0
        S.dma("sp", ogp[l].rearrange("(hh p) v -> p hh v", p=128), Sst[:, :].rearrange("p (hh v) -> p hh v", hh=2), reads=["Sst"])
    S.finish("sp")
    S.emit()
    return nc


def t5_bucket(dist):
    n = np.maximum(dist, 0)
    max_exact = 16
    large = max_exact + (np.log(np.maximum(n, 1) / max_exact) / np.log(128 / max_exact) * (32 - max_exact)).astype(np.int32)
    large = np.minimum(large, 31)
    return np.where(n < max_exact, n, large).astype(np.int32)


_PROG = {}


def prepare(inputs, NB, DEPTH):
    f32 = lambda a: np.ascontiguousarray(np.asarray(a, dtype=np.float32))
    xpr = f32(inputs["x_prompt"])
    xsa = f32(inputs["x_sample"]).reshape(128, D)
    swk = f32(inputs["state_swa_k"]).reshape(DEPTH, 128, 128, 128)
    swv = f32(inputs["state_swa_v"]).reshape(DEPTH, 128, 128, 128)
    sgl = f32(inputs["state_gla"]).reshape(DEPTH, 128, 256, 128)
    rel_bias = f32(inputs["rel_bias"])
    i = np.arange(128)[:, None]
    j = np.arange(256)[None, :]
    dist = i + 128 - j
    band = (dist >= 0) & (dist < 128)
    table = np.concatenate([rel_bias, np.full((1, 8), NEG, np.float32)], 0)
    full = table[np.where(band, t5_bucket(dist), 32)]
    b_prev = np.ascontiguousarray(full[:, 0:128, :].transpose(1, 2, 0)).reshape(128, 1024)
    b_cur = np.ascontiguousarray(full[:, 128:256, :].transpose(1, 2, 0)).reshape(128, 1024)
    b_neg = np.full((128, 1024), NEG, np.float32)
    b_smp = np.ascontiguousarray(rel_bias[t5_bucket(127 - np.arange(128))])
    shared = {
        "w_in": f32(inputs["w_in"]), "w_out": f32(inputs["w_out"]),
        "norm_g": f32(inputs["norm_g"]).reshape(DEPTH * 8, 128), "fin_g": f32(inputs["final_norm_g"]).reshape(1, D),
        "sinks": f32(inputs["sinks"]), "sp_w": f32(inputs["spatial_w"]),
        "sp_b": f32(inputs["spatial_b"]).reshape(DEPTH * 8, 128),
        "ln_g": f32(inputs["chunk_ln_g"]), "ln_b": f32(inputs["chunk_ln_b"]),
        "wup": f32(inputs["gla_w_up"]), "bup": f32(inputs["gla_b_up"]).reshape(DEPTH * 2, 128),
        "gn_g": f32(inputs["gla_norm_g"]), "b_cur": b_cur, "b_prev": b_prev, "b_smp": b_smp,
    }
    NT = NB * 128
    in_maps = []
    for c in range(8):
        b, s = c // 4, c % 4
        cm = np.zeros((128, 8), np.float32)
        for jj in range(3):
            cm[:, jj] = 1.0 if jj < s else 0.0
            cm[:, 3 + jj] = 1.0 if jj == s - 1 else 0.0
        m = dict(shared)
        m.update({
            "xp": np.ascontiguousarray(xpr[b, s * NT:(s + 1) * NT, :]),
            "xs": np.ascontiguousarray(xsa[c * NS:(c + 1) * NS]),
            "swk": np.ascontiguousarray(swk[:, c * NS:(c + 1) * NS]),
            "swv": np.ascontiguousarray(swv[:, c * NS:(c + 1) * NS]),
            "sgl": np.ascontiguousarray(sgl[:, c * NS:(c + 1) * NS]),
            "b_prev0": b_neg if s == 0 else b_prev,
            "cmask": cm,
        })
        in_maps.append(m)
    return in_maps


def assemble(r, DEPTH):
    y_prompt = np.stack([np.concatenate([r[b * 4 + s]["yp"] for s in range(4)], 0) for b in range(2)], 0)
    y_sample = np.concatenate([r[c]["ys"] for c in range(8)], 0).reshape(128, 1, D)
    kp = np.stack([r[b * 4 + 3]["okp"] for b in range(2)], 1).reshape(DEPTH, 2, 128, 2, 64)
    vp = np.stack([r[b * 4 + 3]["ovp"] for b in range(2)], 1).reshape(DEPTH, 2, 128, 2, 64)
    gp = np.stack([r[b * 4 + 3]["ogp"] for b in range(2)], 1).reshape(DEPTH, 2, 4, 64, 128)
    ks = np.concatenate([r[c]["oks"] for c in range(8)], 1).reshape(DEPTH, 128, 128, 2, 64)
    vs = np.concatenate([r[c]["ovs"] for c in range(8)], 1).reshape(DEPTH, 128, 128, 2, 64)
    gs = np.concatenate([r[c]["ogs"] for c in range(8)], 1).reshape(DEPTH, 128, 4, 64, 128)
    cv = np.concatenate([r[c]["ocv"] for c in range(8)], 1).reshape(DEPTH, 128, 1, 512)
    return tuple(np.ascontiguousarray(np.asarray(a, dtype=np.float32)) for a in (y_prompt, y_sample, kp, vp, gp, ks, vs, gs, cv))


def run(inputs, NB, DEPTH):
    in_maps = prepare(inputs, NB, DEPTH)
    key = (NB, DEPTH)
    if key not in _PROG:
        _PROG[key] = build_program(NB, DEPTH)
    res = run_bass_kernel_spmd(_PROG[key], in_maps, core_ids=list(range(8)))
    return assemble(res.results, DEPTH)


def kernel(**inputs):
    return run(inputs, 16, 4)
```

```python
import numpy as np
import concourse.bass as bass
import concourse.mybir as mybir
from concourse.bass_utils import run_bass_kernel_spmd

F32 = mybir.dt.float32
BF16 = mybir.dt.bfloat16
I32 = mybir.dt.int32
ALU = mybir.AluOpType
AF = mybir.ActivationFunctionType

D = 1024
DIN = 4368
DMIX = 1536
EPS = 1e-6
NEG = -1e30
NS = 16
MW = 520
C_QA, C_KA, C_VA, C_GA = 0, 512, 640, 768
C_UB, C_VB, C_GB = 1280, 1792, 2304
C_QC, C_KC, C_VC, C_GC, C_LR = 2816, 3072, 3328, 3840, 4352


class Sched:
    def __init__(self, nc, n_dma=32):
        self.nc = nc
        self.engs = {"pe": nc.tensor, "act": nc.scalar, "dve": nc.vector, "pool": nc.gpsimd, "sp": nc.sync}
        self.sem = {k: nc.alloc_semaphore(name=f"sem_{k}") for k in self.engs}
        self.cnt = {k: 0 for k in self.engs}
        self.ops = {k: [] for k in self.engs}
        self.waited = {k: {} for k in self.engs}
        self.lastw = {}
        self.readers = {}
        self.dma_sems = [nc.alloc_semaphore(name=f"sem_dma{i}") for i in range(n_dma)]
        self.dma_cnt = [0] * n_dma
        self.n_sw = 8
        self.dma_rr = {"pool": 0, "sp": self.n_sw}
        self.cc_sem = nc.alloc_semaphore(name="sem_cc")
        self.cc_cnt = 0
        self.semobj = {("c", 0): self.cc_sem}
        self.keymap = lambda k: k
        for k in self.engs:
            self.semobj[("e", k)] = self.sem[k]
        for i, s in enumerate(self.dma_sems):
            self.semobj[("d", i)] = s

    def _wait(self, eng, tok):
        if tok is None:
            return
        key, val = tok
        if key == ("e", eng) and eng in ("pe", "sp"):
            return
        if self.waited[eng].get(key, 0) >= val:
            return
        self.waited[eng][key] = val
        sem = self.semobj[key]
        self.ops[eng].append(lambda e, sem=sem, val=val: e.wait_ge(sem, val))

    def _deps(self, eng, reads, writes):
        reads = [self.keymap(k) for k in reads]
        writes = [self.keymap(k) for k in writes]
        for r in reads:
            self._wait(eng, self.lastw.get(r))
        for w in writes:
            self._wait(eng, self.lastw.get(w))
            for t in self.readers.get(w, []):
                self._wait(eng, t)

    def _commit(self, tok, reads, writes):
        reads = [self.keymap(k) for k in reads]
        writes = [self.keymap(k) for k in writes]
        for r in reads:
            self.readers.setdefault(r, []).append(tok)
        for w in writes:
            self.lastw[w] = tok
            self.readers[w] = []

    def op(self, eng, fn, reads=(), writes=(), inc=True):
        self._deps(eng, reads, writes)
        if inc:
            self.cnt[eng] += 1
            sem = self.sem[eng]
            self.ops[eng].append(lambda e, fn=fn, sem=sem: fn(e).then_inc(sem, 1))
        else:
            assert eng == "pe"
            self.ops[eng].append(lambda e, fn=fn: fn(e))
        tok = (("e", eng), self.cnt[eng] + (0 if inc else 1))
        self._commit(tok, reads, writes)
        return tok

    def dma(self, eng, out, in_, reads=(), writes=()):
        lo, hi = (0, self.n_sw) if eng == "pool" else (self.n_sw, len(self.dma_sems))
        i = self.dma_rr[eng]
        self.dma_rr[eng] = lo + (i + 1 - lo) % (hi - lo)
        if self.dma_cnt[i] > 0:
            self._wait(eng, (("d", i), self.dma_cnt[i]))
        self._deps(eng, reads, writes)
        self.dma_cnt[i] += 16
        sem = self.dma_sems[i]
        self.ops[eng].append(lambda e, out=out, in_=in_, sem=sem: e.dma_start(out=out, in_=in_).then_inc(sem, 16))
        tok = (("d", i), self.dma_cnt[i])
        self._commit(tok, reads, writes)
        return tok

    def allgather(self, src, dst, groups, reads=(), writes=()):
        eng = "pool"
        self._deps(eng, reads, writes)
        self.cc_cnt += 1
        sem = self.cc_sem
        self.ops[eng].append(lambda e: e.collective_compute(
            "AllGather", ALU.bypass, replica_groups=groups, ins=[src], outs=[dst]).then_inc(sem, 1))
        tok = (("c", 0), self.cc_cnt)
        self._commit(tok, reads, writes)
        return tok

    def finish(self, eng="sp"):
        for i, c in enumerate(self.dma_cnt):
            if c:
                self._wait(eng, (("d", i), c))
        for k in self.engs:
            if k != eng and self.cnt[k]:
                self._wait(eng, (("e", k), self.cnt[k]))

    def emit(self):
        with self.nc.Block() as block:
            @block.tensor
            def _(e):
                for f in self.ops["pe"]:
                    f(e)

            @block.scalar
            def _(e):
                for f in self.ops["act"]:
                    f(e)

            @block.vector
            def _(e):
                for f in self.ops["dve"]:
                    f(e)

            @block.gpsimd
            def _(e):
                for f in self.ops["pool"]:
                    f(e)

            @block.sync
            def _(e):
                for f in self.ops["sp"]:
                    f(e)


def build_program(NB, DEPTH):
    nc = bass.Bass("TRN2", target_bir_lowering=False)
    S = Sched(nc)
    NT = NB * 128

    def din(name, shape):
        return nc.dram_tensor(name, list(shape), F32, kind="ExternalInput").ap()

    def dout(name, shape):
        return nc.dram_tensor(name, list(shape), F32, kind="ExternalOutput").ap()

    xp = din("xp", [NT, D])
    xsm = din("xs", [NS, D])
    swk = din("swk", [DEPTH, NS, 128, 128])
    swv = din("swv", [DEPTH, NS, 128, 128])
    sgl = din("sgl", [DEPTH, NS, 256, 128])
    w_in = din("w_in", [DEPTH, D, DIN])
    w_out = din("w_out", [DEPTH, DMIX, D])
    norm_g = din("norm_g", [DEPTH * 8, 128])
    fin_g = din("fin_g", [1, D])
    sinks = din("sinks", [DEPTH, 8])
    sp_w = din("sp_w", [DEPTH, 8, 128, 128])
    sp_b = din("sp_b", [DEPTH * 8, 128])
    ln_g = din("ln_g", [DEPTH, 512])
    ln_b = din("ln_b", [DEPTH, 512])
    wup = din("wup", [DEPTH, 16, 256])
    bup = din("bup", [DEPTH * 2, 128])
    gn_g = din("gn_g", [DEPTH, 512])
    b_cur = din("b_cur", [128, 1024])
    b_prev = din("b_prev", [128, 1024])
    b_prev0 = din("b_prev0", [128, 1024])
    b_smp = din("b_smp", [128, 8])
    cmask = din("cmask", [128, 8])

    yp = dout("yp", [NT, D])
    ys = dout("ys", [NS, D])
    okp = dout("okp", [DEPTH, 128, 128])
    ovp = dout("ovp", [DEPTH, 128, 128])
    ogp = dout("ogp", [DEPTH, 256, 128])
    oks = dout("oks", [DEPTH, NS, 128, 128])
    ovs = dout("ovs", [DEPTH, NS, 128, 128])
    ogs = dout("ogs", [DEPTH, NS, 256, 128])
    ocv = dout("ocv", [DEPTH, NS, 512])

    Xs = nc.dram_tensor("Xs", [NT, D], F32)
    Wib = nc.dram_tensor("Wib", [DEPTH, D, DIN], BF16)
    Wob = nc.dram_tensor("Wob", [DEPTH, DMIX, D], BF16)
    msg = nc.dram_tensor("msg", [128, MW], F32)
    gath = nc.dram_tensor("gath", [512, MW], F32)

    def sb(name, shape, dt=F32):
        return nc.alloc_sbuf_tensor(name, list(shape), dt)

    par = {"p": 0}
    DBL = set()

    class Dbl:
        def __init__(self, name, shape, dt=F32):
            if len(shape) == 2 and shape[1] <= 8:
                self.t = [nc.alloc_sbuf_tensor(f"{name}{i}", [128, 32], dt, align_bytes=128) for i in range(2)]
            else:
                self.t = [sb(f"{name}{i}", shape, dt) for i in range(2)]
            DBL.add(name)
            self.name = name

        def __getitem__(self, idx):
            return self.t[par["p"]][idx]

    S.keymap = lambda k: (k + str(par["p"])) if k in DBL else k

    Wi = sb("Wi", [128, 8, DIN], BF16)
    Wo = sb("Wo", [128, 12, D], BF16)
    ident = sb("ident", [128, 128], BF16)
    identf = sb("identf", [128, 128])
    maskLT = sb("maskLT", [128, 128])
    cmA = sb("cmA", [128, 128])
    eye16 = sb("eye16", [128, 16, 16], BF16)
    esel = sb("esel", [16, 16, 64], BF16)
    onesf = sb("onesf", [1, 128])
    w0rowf = sb("w0rowf", [1, 16])
    ones_bf = sb("ones_bf", [1, 128], BF16)
    ones64 = sb("ones64", [128, 64])
    cneg05 = sb("cneg05", [128, 8])
    bcur = sb("bcur", [128, 1024], BF16)
    bprev = sb("bprev", [128, 1024], BF16)
    bprev0 = sb("bprev0", [128, 1024], BF16)
    bsmp = sb("bsmp", [128, 8])
    cmk = sb("cmk", [128, 8])
    rowsA = sb("rowsA", [DEPTH * 8 + DEPTH * 2, 128])
    colsA = sb("colsA", [128, DEPTH * 10])
    rowsB = sb("rowsB", [DEPTH * 8, 128])
    bsT = sb("bsT", [128, DEPTH * 8])
    fG = sb("fG", [128, D])
    lnG = sb("lnG", [128, 512])
    lnB = sb("lnB", [128, 512])
    gnG = sb("gnG", [128, 512])
    esink = sb("esink", [128, 8])
    wupb = sb("wupb", [16, 256], BF16)
    WsT = sb("WsT", [128, 8, 128], BF16)
    w00 = sb("w00", [128, 8])
    b00 = sb("b00", [128, 8])
    Xb = [sb(f"Xb{i}", [128, D]) for i in range(2)]
    xsmp = sb("xsmp", [128, D])
    ssq = Dbl("ssq", [128, 8])
    nwI = Dbl("nwI", [128, 8], I32)
    nwA = Dbl("nwA", [128, 8])
    nwB = Dbl("nwB", [128, 8])
    rst = Dbl("rst", [128, 8])
    xsb = Dbl("xsb", [128, D], BF16)
    hT = Dbl("hT", [128, 8, 128], BF16)
    QTs = Dbl("QTs", [128, 512], BF16)
    KTb = [sb(f"KTb{i}", [128, 128], BF16) for i in range(3)]
    KTh = sb("KTh", [128, 128], BF16)
    Vaug = [sb(f"Vaug{i}", [128, 2, 72], BF16) for i in range(3)]
    Vaugh = sb("Vaugh", [128, 2, 72], BF16)
    kvtok = sb("kvtok", [128, 256])
    Tt = sb("Tt", [128, 512])
    Ga = Dbl("Ga", [128, 512], BF16)
    UG = Dbl("UG", [128, 512], BF16)
    Gc = Dbl("Gc", [128, 512], BF16)
    PT = sb("PT", [128, 4, 512], BF16)
    den = Dbl("den", [128, 8])
    rden = Dbl("rden", [128, 8])
    Y = Dbl("Y", [128, DMIX], BF16)
    bst = Dbl("bst", [128, 6])
    bmv = Dbl("bmv", [128, 2])
    vnf = sb("vnf", [128, 512])
    vnb = Dbl("vnb", [128, 512], BF16)
    lrTb = sb("lrTb", [16, 128], BF16)
    ee = sb("ee", [128, 256])
    aa = sb("aa", [128, 256])
    EbT = Dbl("EbT", [128, 256])
    EnbT = sb("EnbT", [128, 256])
    qdT = Dbl("qdT", [128, 256], BF16)
    kiT = Dbl("kiT", [128, 256], BF16)
    kitok = Dbl("kitok", [128, 256], BF16)
    vcb = Dbl("vcb", [128, 512], BF16)
    attT = sb("attT", [128, 512], BF16)
    Sst = sb("Sst", [128, 256])
    Sbf = [sb(f"Sbf{i}", [128, 256], BF16) for i in range(2)]
    Ptot = sb("Ptot", [128, 2])
    yT = sb("yT", [128, 12, 128], BF16)
    msgt = sb("msgt", [128, MW])
    KVw = sb("KVw", [128, NS, 128])
    KVx = sb("KVx", [128, NS * 144], BF16)
    KwT = KVx[:, 0:NS * 128].rearrange("p (i r) -> p i r", i=NS)
    Vaus = KVx[:, :].rearrange("p (i a e) -> p i a e", i=NS, a=2)
    Ls = sb("Ls", [128, 128])
    PTs = sb("PTs", [128, 128], BF16)
    PTm = PT[:, :, :].rearrange("p a (b c d) -> p (a b) c d", b=2, c=16)
    SS = sb("SS", [128, 4, 2, 128])
    SSb = sb("SSb", [128, 4, 2, 128], BF16)
    wsStage = SS[:, :, :, :].rearrange("p a b c -> p (a b) c")
    qTs = sb("qTs", [128, 256])
    kTs = sb("kTs", [128, 256])
    QM = sb("QM", [128, 2, 16, 16], BF16)

    psf = [nc.alloc_psum_tensor(f"psf{i}", [128, 512], F32) for i in range(6)]
    psb = [nc.alloc_psum_tensor(f"psb{i}", [128, 1024], BF16) for i in range(2)]
    ring = {"f": 0, "b": 0}
    pinned = set()

    def PSF(pin=False):
        i = ring["f"]
        while i in pinned:
            i = (i + 1) % 6
        ring["f"] = (i + 1) % 6
        if pin:
            pinned.add(i)
        return psf[i], f"psf{i}"

    def unpin(key):
        pinned.discard(int(key[3:]))

    def PSB():
        i = ring["b"]
        ring["b"] = (i + 1) % 2
        return psb[i], f"psb{i}"

    def mm(out, lhsT, rhs, start, stop, reads, writes, inc):
        S.op("pe", lambda e: e.matmul(out, lhsT=lhsT, rhs=rhs, start=start, stop=stop),
             reads=reads, writes=writes, inc=inc)

    def tr(out, in_, idn, reads, writes, inc=True):
        S.op("pe", lambda e: e.transpose(out=out, in_=in_, identity=idn), reads=reads, writes=writes, inc=inc)

    def act(out, in_, func, reads, writes, scale=1.0, bias=None, accum=None):
        kw = {}
        if bias is not None:
            kw["bias"] = bias
        if accum is not None:
            kw["accum_out"] = accum
        S.op("act", lambda e: e.activation(out=out, in_=in_, func=func, scale=scale, **kw), reads=reads, writes=writes)

    def tt(eng, out, in0, in1, op, reads, writes):
        S.op(eng, lambda e: e.tensor_tensor(out=out, in0=in0, in1=in1, op=op), reads=reads, writes=writes)

    def ts(eng, out, in0, s1, s2, op0, op1, reads, writes):
        if s2 is None:
            S.op(eng, lambda e: e.tensor_scalar(out=out, in0=in0, scalar1=s1, scalar2=None, op0=op0), reads=reads, writes=writes)
        else:
            S.op(eng, lambda e: e.tensor_scalar(out=out, in0=in0, scalar1=s1, scalar2=s2, op0=op0, op1=op1), reads=reads, writes=writes)

    def stt(eng, out, in0, scalar, in1, op0, op1, reads, writes):
        S.op(eng, lambda e: e.scalar_tensor_tensor(out=out, in0=in0, scalar=scalar, in1=in1, op0=op0, op1=op1),
             reads=reads, writes=writes)

    def cp(eng, out, in_, reads, writes):
        if eng == "act":
            S.op(eng, lambda e: e.activation(out=out, in_=in_, func=AF.Copy), reads=reads, writes=writes)
        else:
            S.op(eng, lambda e: e.tensor_copy(out=out, in_=in_), reads=reads, writes=writes)

    def rsqrt_eps(src, dst, n, scale, reads_key, writes_key):
        ts("dve", dst, src, scale, EPS, ALU.mult, ALU.add, [reads_key], [writes_key])
        yi = nwA[:, 0:n]
        S.op("dve", lambda e, o=nwI[:, 0:n], i=dst.bitcast(I32): e.tensor_scalar(out=o, in0=i, scalar1=1, scalar2=None,
                                                                                  op0=ALU.arith_shift_right),
             reads=[writes_key], writes=["nwI"])
        S.op("dve", lambda e, o=yi.bitcast(I32), i=nwI[:, 0:n]: e.tensor_scalar(out=o, in0=i, scalar1=-1.0, scalar2=1597463007.0,
                                                                                op0=ALU.mult, op1=ALU.add),
             reads=["nwI"], writes=["nwA"])
        for it in range(2):
            tt("dve", nwB[:, 0:n], yi, dst, ALU.mult, ["nwA", writes_key], ["nwB"])
            stt("dve", nwB[:, 0:n], nwB[:, 0:n], -0.5, yi, ALU.mult, ALU.mult, ["nwB", "nwA"], ["nwB"])
            if it == 0:
                stt("dve", yi, nwB[:, 0:n], 1.5, yi, ALU.add, ALU.mult, ["nwB", "nwA"], ["nwA"])
            else:
                stt("dve", dst, nwB[:, 0:n], 1.5, yi, ALU.add, ALU.mult, ["nwB", "nwA"], [writes_key])

    def wkeys():
        return [f"Wi{k}" for k in range(8)]

    def wkey(c0):
        return "WiA" if c0 < C_UB else ("WiB" if c0 < C_QC else "WiC")

    def proj_fm(ps_ap, pskey, c0, M, last=True):
        for k in range(8):
            mm(ps_ap, Wi[:, k, c0:c0 + M], hT[:, k, :], k == 0, k == 7, ["hT", wkey(c0)], [pskey], inc=(last and k == 7))

    def proj_tm(ps_ap, pskey, c0, N, last=True):
        for k in range(8):
            mm(ps_ap, hT[:, k, :], Wi[:, k, c0:c0 + N], k == 0, k == 7, ["hT", wkey(c0)], [pskey], inc=(last and k == 7))

    S.op("pool", lambda e: e.memset(identf[:, :], 0.0), writes=["identf"])
    S.op("pool", lambda e: e.affine_select(out=identf[:, :], in_=identf[:, :], pattern=[[-1, 128]],
                                           compare_op=ALU.not_equal, fill=1.0, base=0, channel_multiplier=1),
         reads=["identf"], writes=["identf"])
    cp("dve", ident[:, :], identf[:, :], ["identf"], ["ident"])
    S.op("pool", lambda e: e.memset(maskLT[:, :], 1.0), writes=["maskLT"])
    S.op("pool", lambda e: e.affine_select(out=maskLT[:, :], in_=maskLT[:, :], pattern=[[1, 128]],
                                           compare_op=ALU.is_ge, fill=0.0, base=0, channel_multiplier=-1),
         reads=["maskLT"], writes=["maskLT"])
    cp("dve", cmA[:, :], maskLT[:, :], ["maskLT"], ["cmA"])
    S.op("dve", lambda e: e.memset(cmA[0:64, 64:128], 0.0), reads=["cmA"], writes=["cmA"])
    S.op("pool", lambda e: e.memset(eye16[:, :, :], 0.0), writes=["eye16"])
    S.op("pool", lambda e: e.affine_select(out=eye16[:, :, :], in_=eye16[:, :, :], pattern=[[1, 16], [-1, 16]],
                                           compare_op=ALU.not_equal, fill=1.0, base=0, channel_multiplier=0),
         reads=["eye16"], writes=["eye16"])
    S.op("pool", lambda e: e.memset(esel[:, :, :], 0.0), writes=["esel"])
    S.op("pool", lambda e: e.affine_select(out=esel[:, :, :], in_=esel[:, :, :], pattern=[[1, 16], [0, 64]],
                                           compare_op=ALU.not_equal, fill=1.0, base=0, channel_multiplier=-1),
         reads=["esel"], writes=["esel"])
    S.op("dve", lambda e: e.memset(onesf[:, :], 1.0), writes=["onesf"])
    S.op("dve", lambda e: e.memset(ones_bf[:, :], 1.0), writes=["ones_bf"])
    S.op("dve", lambda e: e.memset(ones64[:, :], 1.0), writes=["ones64"])
    S.op("dve", lambda e: e.memset(cneg05[:, :], -0.5), writes=["cneg05"])
    S.op("dve", lambda e: e.memset(xsmp[:, :], 0.0), writes=["xsmp"])
    S.op("dve", lambda e: e.memset(kvtok[:, :], 0.0), writes=["kvtok"])
    S.op("dve", lambda e: e.memset(msgt[:, :], 0.0), writes=["msgt"])
    for d_ in (ssq, rst, den, rden, bst, bmv, nwA, nwB, Y):
        for i in range(2):
            S.op("dve", lambda e, t=d_.t[i]: e.memset(t[:, :], 1.0), writes=[f"{d_.name}{i}"])
    for i in range(2):
        S.op("dve", lambda e, t=nwI.t[i]: e.memset(t[:, :], 0), writes=[f"nwI{i}"])
    for i in range(3):
        S.op("dve", lambda e, i=i: e.memset(Vaug[i][:, :, :], 1.0), writes=[f"Vaug{i}"])
    S.op("dve", lambda e: e.memset(Vaugh[:, :, :], 1.0), writes=["Vaugh"])
    S.dma("pool", bcur[:, :], b_cur[:, :], writes=["bcur"])
    S.dma("pool", bprev[:, :], b_prev[:, :], writes=["bprev"])
    S.dma("pool", bprev0[:, :], b_prev0[:, :], writes=["bprev0"])
    S.dma("sp", bsmp[:, :], b_smp[:, :], writes=["bsmp"])
    S.dma("sp", cmk[:, :], cmask[:, :], writes=["cmk"])
    S.dma("sp", fG[:, :], fin_g[0:1, :].partition_broadcast(128), writes=["fG"])
    S.dma("sp", xsmp[0:NS, :], xsm[:, :], reads=[], writes=["xsmp"])
    nA = DEPTH * 8
    S.dma("sp", rowsA[0:nA, :], norm_g[:, :], writes=["rowsA"])
    S.dma("sp", rowsA[nA:nA + DEPTH * 2, :], bup[:, :], writes=["rowsA"])
    S.dma("sp", rowsB[:, :], sp_b[:, :], writes=["rowsB"])
    pA, kA = PSF()
    nr = DEPTH * 10
    tr(pA[:, 0:nr], rowsA[0:nr, :], identf[0:nr, 0:nr], ["rowsA", "identf"], [kA])
    cp("dve", colsA[:, 0:nA], pA[:, 0:nA], [kA], ["colsA"])
    ts("dve", colsA[:, nA:nr], pA[:, nA:nr], -1.0, None, ALU.mult, None, [kA], ["colsA"])
    pB, kB = PSF()
    tr(pB[:, 0:nA], rowsB[0:nA, :], identf[0:nA, 0:nA], ["rowsB", "identf"], [kB])
    cp("dve", bsT[:, :], pB[:, 0:nA], [kB], ["bsT"])

    def convert_layer(l):
        for k in range(8):
            S.dma("pool", Wib.ap()[l, k * 128:(k + 1) * 128, :], w_in[l, k * 128:(k + 1) * 128, :], writes=[f"Wcv{l}"])
        for k in range(6):
            S.dma("pool", Wob.ap()[l, k * 256:(k + 1) * 256, :], w_out[l, k * 256:(k + 1) * 256, :], writes=[f"Wcv{l}"])

    def load_layer_params(l):
        S.dma("sp", lnG[:, :], ln_g[l:l + 1, :].partition_broadcast(128), writes=["lnG"])
        S.dma("sp", lnB[:, :], ln_b[l:l + 1, :].partition_broadcast(128), writes=["lnB"])
        S.dma("sp", gnG[:, :], gn_g[l:l + 1, :].partition_broadcast(128), writes=["gnG"])
        S.dma("sp", esink[:, :], sinks[l:l + 1, :].partition_broadcast(128), writes=["esink"])
        act(esink[:, :], esink[:, :], AF.Exp, ["esink"], ["esink"])
        S.dma("pool", wupb[:, :], wup[l, :, :], writes=["wupb"])
        S.dma("pool", wsStage, sp_w[l].rearrange("h i j -> i h j"), writes=["SS"])

        if l == 0:
            q, wsrc, wosrc, rk = "pool", w_in[l], w_out[l], []
        else:
            q, wsrc, wosrc, rk = "sp", Wib.ap()[l], Wob.ap()[l], [f"Wcv{l}"]
        for k in range(8):
            src = wsrc[k * 128:(k + 1) * 128, :]
            S.dma(q, Wi[:, k, C_QC:DIN], src[:, C_QC:DIN], reads=rk, writes=["WiC"])
        for k in range(8):
            src = wsrc[k * 128:(k + 1) * 128, :]
            for a_ in range(2):
                S.dma(q, Wi[:, k, 0:512].rearrange("p (j a i) -> p j a i", j=4, a=2, i=64)[:, :, a_, :],
                      src[:, a_ * 256:(a_ + 1) * 256].rearrange("p (j i) -> p j i", j=4, i=64), reads=rk, writes=["WiA"])
            S.dma(q, Wi[:, k, 512:C_UB], src[:, 512:C_UB], reads=rk, writes=["WiA"])
        for k in range(8):
            src = wsrc[k * 128:(k + 1) * 128, :]
            S.dma(q, Wi[:, k, C_UB:C_QC], src[:, C_UB:C_QC], reads=rk, writes=["WiB"])
        for k in range(12):
            S.dma(q, Wo[:, k, :], wosrc[k * 128:(k + 1) * 128, :], reads=rk, writes=[f"Wo{k}"])

    def prep_spatial(l):
        for half in range(2):
            pw, kw = PSF()
            for hq in range(4):
                h = half * 4 + hq
                tr(pw[:, hq * 128:(hq + 1) * 128], wsStage[:, h, :], identf[:, :], ["SS", "identf"], [kw], inc=(hq == 3))
            tt("dve", WsT[:, half * 4:half * 4 + 4, :], pw[:, :].rearrange("p (h i) -> p h i", h=4),
               maskLT[:, :].unsqueeze(1).to_broadcast([128, 4, 128]), ALU.mult, [kw, "maskLT"], ["WsT"])
        cp("dve", w0rowf[0:1, 0:8], wsStage[0:1, :, 0], ["SS"], ["w0rowf"])
        cp("dve", w0rowf[0:1, 8:16], bsT[0:1, l * 8:(l + 1) * 8], ["bsT"], ["w0rowf"])
        pw, kw = PSF()
        mm(pw[:, 0:16], onesf[0:1, :], w0rowf[0:1, :], True, True, ["onesf", "w0rowf"], [kw], True)
        cp("dve", w00[:, :], pw[:, 0:8], [kw], ["w00"])
        cp("dve", b00[:, :], pw[:, 8:16], [kw], ["b00"])

    prefetched = set()

    def load_x(l, blk, ph=2):
        xt = Xb[blk % 2]
        key = f"Xb{blk % 2}"
        if (l, blk, ph) in prefetched:
            return xt, key
        src = xp if l == 0 else Xs.ap()
        S.dma("sp", xt[:, :], src[blk * 128:(blk + 1) * 128, :], reads=[f"Xs{blk}"], writes=[key])
        return xt, key

    def norm_T(l, xt, xkey):
        act(xsb[:, :], xt[:, :], AF.Square, [xkey], ["xsb", "ssq"], accum=ssq[:, 0:1])
        rsqrt_eps(ssq[:, 0:1], rst[:, 0:1], 1, 1.0 / D, "ssq", "rst")
        act(xsb[:, :], xt[:, :], AF.Copy, [xkey, "rst"], ["xsb"], scale=rst[:, 0:1])

    def norm_T2(l):
        pt, kt = PSB()
        for k in range(8):
            tr(pt[:, k * 128:(k + 1) * 128], xsb[:, k * 128:(k + 1) * 128], ident[:, :], ["xsb", "ident"], [kt], inc=(k == 7))
        tt("dve", hT[:, :, :], pt[:, :].rearrange("p (k t) -> p k t", k=8),
           colsA[:, l * 8:(l + 1) * 8].unsqueeze(2).to_broadcast([128, 8, 128]), ALU.mult, [kt, "colsA"], ["hT"])

    def proj_kv(cur, want_tok):
        pk, kk = PSF()
        proj_fm(pk[:, 0:128], kk, C_KA, 128, last=False)
        proj_tm(pk[:, 128:384], kk, C_KA, 256)
        cp("act", KTb[cur][:, :], pk[:, 0:128], [kk], [f"KTb{cur}"])
        for a in range(2):
            cp("dve", Vaug[cur][:, a, 0:64], pk[:, 256 + a * 64:256 + (a + 1) * 64], [kk], [f"Vaug{cur}"])
        if want_tok:
            cp("dve", kvtok[:, :], pk[:, 128:384], [kk], ["kvtok"])
        return pk, kk

    def gate(c0, out_tile, out_key):
        pg, kg = PSF()
        proj_tm(pg[:, :], kg, c0, 512)
        act(Tt[:, :], pg[:, :], AF.Tanh, [kg], ["Tt"], scale=0.5)
        stt("dve", out_tile[:, :], Tt[:, :], 1.0, pg[:, :], ALU.add, ALU.mult, ["Tt", kg], [out_key])

    def gla_decay_a(l):
        pl, kl = PSF()
        proj_fm(pl[0:16, 0:128], kl, C_LR, 16)
        cp("act", lrTb[:, :], pl[0:16, 0:128], [kl], ["lrTb"])

    def gla_decay_b(l):
        pz, kz = PSF()
        for hh in range(2):
            mm(pz[:, hh * 128:(hh + 1) * 128], wupb[0:16, hh * 128:(hh + 1) * 128], lrTb[0:16, :], True, True,
               ["wupb", "lrTb"], [kz], inc=(hh == 1))
        for hh in range(2):
            c = DEPTH * 8 + l * 2 + hh
            act(ee[:, hh * 128:(hh + 1) * 128], pz[:, hh * 128:(hh + 1) * 128], AF.Exp, [kz, "colsA"], ["ee"],
                scale=-1.0, bias=colsA[:, c:c + 1])
        act(ee[:, :], ee[:, :], AF.Ln, ["ee"], ["ee"], bias=1.0)
        act(aa[:, :], ee[:, :], AF.Exp, ["ee"], ["aa"], scale=-1.0 / 16.0)

    def gla_decay_c():
        for hh in range(2):
            for c in range(2):
                o = hh * 128 + c * 64
                S.op("dve", lambda e, o=o, eo=EbT[:, o:o + 64]: e.tensor_tensor_scan(out=eo, data0=aa[:, o:o + 64], data1=ones64[:, :],
                                                                                   initial=1.0, op0=ALU.mult, op1=ALU.mult),
                     reads=["aa", "ones64"], writes=["EbT"])
        S.op("dve", lambda e, ei=EbT[:, :]: e.reciprocal(out=EnbT[:, :], in_=ei), reads=["EbT"], writes=["EnbT"])

    def gla_decay(l, is_sample):
        gla_decay_a(l)
        gla_decay_b(l)
        if not is_sample:
            gla_decay_c()

    def gla_state_chunk(c, want_p, want_bf=True):
        pd, kd = PSF()
        for h in range(4):
            hh, hp = h // 2, h % 2
            mm(pd[hp * 64:(hp + 1) * 64, hh * 128:(hh + 1) * 128], kitok[c * 64:(c + 1) * 64, h * 64:(h + 1) * 64],
               vcb[c * 64:(c + 1) * 64, h * 128:(h + 1) * 128], True, True, ["kitok", "vcb"], [kd], inc=(h == 3))
        tt("dve", Sst[:, :], Sst[:, :], pd[:, 0:256], ALU.add, ["Sst", kd], ["Sst"])
        for hh in range(2):
            col = hh * 128 + c * 64 + 63
            ts("dve", Sst[:, hh * 128:(hh + 1) * 128], Sst[:, hh * 128:(hh + 1) * 128], EbT[:, col:col + 1], None,
               ALU.mult, None, ["Sst", "EbT"], ["Sst"])
        if want_bf:
            cp("act", Sbf[(c + 1) % 2][:, :], Sst[:, :], ["Sst"], [f"Sbf{(c + 1) % 2}"])
        if want_p:
            a_ap = EbT[:, :].rearrange("p (hh t) -> p hh t", hh=2)[:, :, c * 64 + 63]
            tt("dve", Ptot[:, :], Ptot[:, :], a_ap, ALU.mult, ["Ptot", "EbT"], ["Ptot"])

    def kvp_a():
        pqk, kq = PSF()
        for hh in range(2):
            proj_fm(pqk[:, 256 + hh * 128:256 + (hh + 1) * 128], kq, C_KC + hh * 128, 128, last=(hh == 1))
        tt("dve", kiT[:, :], pqk[:, 256:512], EnbT[:, :], ALU.mult, [kq, "EnbT"], ["kiT"])

    def kvp_b():
        pt, kt = PSB()
        for hh in range(2):
            tr(pt[:, hh * 128:(hh + 1) * 128], kiT[:, hh * 128:(hh + 1) * 128], ident[:, :], ["kiT", "ident"], [kt], inc=(hh == 1))
        cp("act", kitok[:, :], pt[:, 0:256], [kt], ["kitok"])
        pv, kv = PSF()
        proj_tm(pv[:, :], kv, C_VC, 512)
        cp("act", vcb[:, :], pv[:, :], [kv], ["vcb"])

    def gla_kv_proj():
        kvp_a()
        kvp_b()

    def group_norm_out(po, kpo, rows):
        for h in range(4):
            act(xsb[0:rows, h * 128:(h + 1) * 128], po[0:rows, h * 128:(h + 1) * 128], AF.Square, [kpo], ["xsb", "ssq"],
                accum=ssq[0:rows, 4 + h:5 + h])
        rsqrt_eps(ssq[:, 4:8], rst[:, 4:8], 4, 1.0 / 128.0, "ssq", "rst")
        for h in range(4):
            stt("dve", Y[0:rows, 1024 + h * 128:1024 + (h + 1) * 128], po[0:rows, h * 128:(h + 1) * 128], rst[0:rows, 4 + h:5 + h],
                Gc[0:rows, h * 128:(h + 1) * 128], ALU.mult, ALU.mult, [kpo, "rst", "Gc"], ["Y"])

    def attn_norm_out(pos, kpos, rows):
        for kv in range(2):
            v = pos[kv][0:rows, 0:260].rearrange("p (g e) -> p g e", g=4)
            tt("dve", den[0:rows, kv * 4:(kv + 1) * 4], v[:, :, 64], esink[0:rows, kv * 4:(kv + 1) * 4], ALU.add,
               [kpos[kv], "esink"], ["den"])
        S.op("dve", lambda e, ro=rden[0:rows, 0:8], di=den[0:rows, 0:8]: e.reciprocal(out=ro, in_=di), reads=["den"], writes=["rden"])
        for h in range(8):
            kv, g = h // 4, h % 4
            stt("dve", Y[0:rows, h * 64:(h + 1) * 64], pos[kv][0:rows, g * 65:g * 65 + 64], rden[0:rows, h:h + 1],
                Ga[0:rows, h * 64:(h + 1) * 64], ALU.mult, ALU.mult, [kpos[kv], "rden", "Ga"], ["Y"])

    def ln_v(l):
        pv, kv = PSF()
        proj_tm(pv[:, :], kv, C_VB, 512)
        S.op("dve", lambda e, bo=bst[:, 0:6], pi=pv[:, :]: e.bn_stats(out=bo, in_=pi), reads=[kv], writes=["bst"])
        S.op("dve", lambda e, mo=bmv[:, 0:2], bi=bst[:, 0:6]: e.bn_aggr(out=mo, in_=bi), reads=["bst"], writes=["bmv"])
        rsqrt_eps(bmv[:, 1:2], rst[:, 1:2], 1, 1.0, "bmv", "rst")
        stt("dve", vnf[:, :], pv[:, :], bmv[:, 0:1], lnG[:, :], ALU.subtract, ALU.mult, [kv, "bmv", "lnG"], ["vnf"])
        stt("dve", vnf[:, :], vnf[:, :], rst[:, 1:2], lnB[:, :], ALU.mult, ALU.add, ["vnf", "rst", "lnB"], ["vnf"])

    def out_proj(l, xt, xkey, rows, dst_ap, dst_key, final):
        out_proj_a()
        out_proj_b(l, xt, xkey, rows, final)

    def out_proj_a():
        pt0, kt0 = PSB()
        for k in range(8):
            tr(pt0[:, k * 128:(k + 1) * 128], Y[:, k * 128:(k + 1) * 128], ident[:, :], ["Y", "ident"], [kt0], inc=(k == 7))
        cp("act", yT[:, 0:8, :], pt0[:, :].rearrange("p (k t) -> p k t", k=8), [kt0], ["yT"])
        pt1, kt1 = PSB()
        for k in range(4):
            tr(pt1[:, k * 128:(k + 1) * 128], Y[:, (8 + k) * 128:(9 + k) * 128], ident[:, :], ["Y", "ident"], [kt1], inc=(k == 3))
        cp("dve", yT[:, 8:12, :], pt1[:, 0:512].rearrange("p (k t) -> p k t", k=4), [kt1], ["yT"])

    def out_proj_b(l, xt, xkey, rows, final):
        for n in range(2):
            po, ko = PSF()
            for k in range(12):
                mm(po[:, :], yT[:, k, :], Wo[:, k, n * 512:(n + 1) * 512], k == 0, k == 11, ["yT", f"Wo{k}"], [ko], inc=(k == 11))
            stt("dve", xt[0:rows, n * 512:(n + 1) * 512], po[0:rows, :], 0.5, xt[0:rows, n * 512:(n + 1) * 512], ALU.mult, ALU.add,
                [ko, xkey], [xkey])
        if not final:
            return
        act(xsb[0:rows, :], xt[0:rows, :], AF.Square, [xkey], ["xsb", "ssq"], accum=ssq[0:rows, 2:3])
        rsqrt_eps(ssq[:, 2:3], rst[:, 2:3], 1, 1.0 / D, "ssq", "rst")
        stt("dve", xt[0:rows, :], xt[0:rows, :], rst[0:rows, 2:3], fG[0:rows, :], ALU.mult, ALU.mult, [xkey, "rst", "fG"], [xkey])

    def phase1_block(l, blk):
        p = blk % 2
        par["p"] = p
        xt, xkey = load_x(l, blk, 1)
        norm_T(l, xt, xkey)
        yield
        par["p"] = p
        norm_T2(l)
        yield
        par["p"] = p
        gla_decay_a(l)
        if blk == NB - 1:
            pk, kk = PSF()
            proj_fm(pk[:, 0:128], kk, C_KA, 128, last=False)
            proj_tm(pk[:, 128:384], kk, C_KA, 256)
            cp("dve", msgt[:, 256:384], pk[:, 0:128], [kk], ["msgt"])
            cp("dve", msgt[:, 384:512], pk[:, 256:384], [kk], ["msgt"])
        yield
        par["p"] = p
        gla_decay_b(l)
        yield
        par["p"] = p
        gla_decay_c()
        kvp_a()
        yield
        par["p"] = p
        kvp_b()
        yield
        par["p"] = p
        for c in range(2):
            gla_state_chunk(c, True, False)

    def exchange_a(l):
        cp("dve", msgt[:, 0:256], Sst[:, :], ["Sst"], ["msgt"])
        cp("dve", msgt[:, 512:514], Ptot[:, :], ["Ptot"], ["msgt"])
        S.dma("pool", msg.ap(), msgt[:, :], reads=["msgt"], writes=["msg"])
        S.allgather(msg.ap().opt(), gath.ap().opt(), [[0, 1, 2, 3], [4, 5, 6, 7]], reads=["msg"], writes=["gath"])

    def exchange_b(l):
        S.op("dve", lambda e: e.memset(Sst[:, :], 0.0), reads=[], writes=["Sst"])
        S.op("dve", lambda e: e.memset(aa[:, :], 0.0), reads=[], writes=["aa"])
        for j in range(3):
            S.dma("pool", msgt[:, :], gath.ap()[j * 128:(j + 1) * 128, :], reads=["gath"], writes=["msgt"])
            for hh in range(2):
                stt("dve", ee[:, hh * 128:(hh + 1) * 128], Sst[:, hh * 128:(hh + 1) * 128], msgt[:, 512 + hh:513 + hh],
                    msgt[:, hh * 128:(hh + 1) * 128], ALU.mult, ALU.add, ["Sst", "msgt"], ["ee"])
            tt("dve", ee[:, :], ee[:, :], Sst[:, :], ALU.subtract, ["ee", "Sst"], ["ee"])
            stt("dve", Sst[:, :], ee[:, :], cmk[:, j:j + 1], Sst[:, :], ALU.mult, ALU.add, ["ee", "cmk", "Sst"], ["Sst"])
            stt("dve", aa[:, :], msgt[:, 256:512], cmk[:, 3 + j:4 + j], aa[:, :], ALU.mult, ALU.add, ["msgt", "cmk", "aa"], ["aa"])
        cp("act", Sbf[0][:, :], Sst[:, :], ["Sst"], ["Sbf0"])
        cp("act", KTh[:, :], aa[:, 0:128], ["aa"], ["KTh"])
        cp("dve", Vaugh[:, :, 0:64], aa[:, 128:256].rearrange("p (a d) -> p a d", a=2), ["aa"], ["Vaugh"])

    def phase2_block(l, blk, last_layer):
        p = blk % 2
        cur = blk % 3
        prv = (blk + 2) % 3
        par["p"] = p
        xt, xkey = load_x(l, blk)
        norm_T(l, xt, xkey)
        yield
        par["p"] = p
        norm_T2(l)
        yield
        par["p"] = p
        pq, kq = PSF()
        for j in range(4):
            proj_fm(pq[:, j * 128:(j + 1) * 128], kq, C_QA + j * 128, 128, last=(j == 3))
        act(QTs[:, :], pq[:, :], AF.Copy, [kq], ["QTs"], scale=0.125)
        proj_kv(cur, blk == NB - 1)
        if blk == NB - 1:
            S.dma("sp", okp[l, :, :], kvtok[:, 0:128], reads=["kvtok"])
            S.dma("sp", ovp[l, :, :], kvtok[:, 128:256], reads=["kvtok"])
        gate(C_GA, Ga, "Ga")
        yield
        par["p"] = p
        gate(C_GB, UG, "UG")
        pu, ku = PSF()
        proj_tm(pu[:, :], ku, C_UB, 512)
        tt("dve", UG[:, :], pu[:, :], UG[:, :], ALU.mult, [ku, "UG"], ["UG"])
        ln_v(l)
        cp("act", vnb[:, :], vnf[:, :], ["vnf"], ["vnb"])
        yield
        par["p"] = p
        gla_decay_a(l)
        gate(C_GC, Gc, "Gc")
        tt("dve", Gc[:, :], Gc[:, :], gnG[:, :], ALU.mult, ["Gc", "gnG"], ["Gc"])
        yield
        par["p"] = p
        gla_decay_b(l)
        yield
        par["p"] = p
        gla_decay_c()
        pqk, kqk = PSF()
        for hh in range(2):
            proj_fm(pqk[:, hh * 128:(hh + 1) * 128], kqk, C_QC + hh * 128, 128, last=(hh == 1))
        stt("dve", qdT[:, :], pqk[:, 0:256], 0.125, EbT[:, :], ALU.mult, ALU.mult, [kqk, "EbT"], ["qdT"])
        kvp_a()
        yield
        par["p"] = p
        kvp_b()
        yield
        par["p"] = p
        if blk == 0:
            kprev, kprev_key, vprev, vprev_key, bp, bp_key = KTh, "KTh", Vaugh, "Vaugh", bprev0, "bprev0"
        else:
            kprev, kprev_key, vprev, vprev_key, bp, bp_key = KTb[prv], f"KTb{prv}", Vaug[prv], f"Vaug{prv}", bprev, "bprev"
        for kv in range(2):
            for kb in range(2):
                kt_, ktk, bt, btk = (kprev, kprev_key, bp, bp_key) if kb == 0 else (KTb[cur], f"KTb{cur}", bcur, "bcur")
                ps, kps = PSF()
                mm(ps[:, :], kt_[kv * 64:(kv + 1) * 64, :], QTs[kv * 64:(kv + 1) * 64, :], True, False, [ktk, "QTs"], [kps], False)
                mm(ps[:, :], ident[:, :], bt[:, kv * 512:(kv + 1) * 512], False, True, ["ident", btk], [kps], True)
                act(PT[:, kv * 2 + kb, :], ps[:, :], AF.Exp, [kps], ["PT"])
        yield
        par["p"] = p
        pos, kpos = [], []
        for kv in range(2):
            po, kpo = PSF()
            pos.append(po)
            kpos.append(kpo)
            for g in range(4):
                for kb in range(2):
                    va, vak = (vprev, vprev_key) if kb == 0 else (Vaug[cur], f"Vaug{cur}")
                    mm(po[:, g * 65:(g + 1) * 65], PT[:, kv * 2 + kb, g * 128:(g + 1) * 128], va[:, kv, 0:65], kb == 0, kb == 1,
                       ["PT", vak], [kpo], inc=(g == 3 and kb == 1))
        attn_norm_out(pos, kpos, 128)
        yield
        par["p"] = p
        pm, km = PSF()
        for h in range(8):
            mm(pm[:, h * 64:(h + 1) * 64], WsT[:, h, :], vnb[:, h * 64:(h + 1) * 64], True, True, ["WsT", "vnb"], [km], inc=(h == 7))
        for h in range(8):
            stt("dve", Y[:, 512 + h * 64:512 + (h + 1) * 64], pm[:, h * 64:(h + 1) * 64], bsT[:, l * 8 + h:l * 8 + h + 1],
                UG[:, h * 64:(h + 1) * 64], ALU.add, ALU.mult, [km, "bsT", "UG"], ["Y"])
        yield
        par["p"] = p
        pas = [PSF(), PSF()]
        for hp in range(2):
            pa, ka = pas[hp]
            for hh in range(2):
                mm(pa[:, hh * 128:(hh + 1) * 128], kiT[hp * 64:(hp + 1) * 64, hh * 128:(hh + 1) * 128],
                   qdT[hp * 64:(hp + 1) * 64, hh * 128:(hh + 1) * 128], True, True, ["kiT", "qdT"], [ka], inc=(hh == 1))
        for h in range(4):
            hh, hp = h // 2, h % 2
            pa, ka = pas[hp]
            tt("dve", attT[:, h * 128:(h + 1) * 128], pa[:, hh * 128:(hh + 1) * 128], cmA[:, :], ALU.mult, [ka, "cmA"], ["attT"])
        gla_state_chunk(0, False)
        yield
        par["p"] = p
        po, kpo = PSF()
        for h in range(4):
            hh, hp = h // 2, h % 2
            mm(po[:, h * 128:(h + 1) * 128], attT[:, h * 128:(h + 1) * 128], vcb[:, h * 128:(h + 1) * 128], True, False,
               ["attT", "vcb"], [kpo], False)
            for c in range(2):
                mm(po[c * 64:(c + 1) * 64, h * 128:(h + 1) * 128],
                   qdT[hp * 64:(hp + 1) * 64, hh * 128 + c * 64:hh * 128 + (c + 1) * 64],
                   Sbf[c][hp * 64:(hp + 1) * 64, hh * 128:(hh + 1) * 128], False, True, ["qdT", f"Sbf{c}"], [kpo],
                   inc=(h == 3 and c == 1))
        gla_state_chunk(1, False)
        group_norm_out(po, kpo, 128)
        yield
        par["p"] = p
        out_proj_a()
        yield
        par["p"] = p
        out_proj_b(l, xt, xkey, 128, last_layer)
        if last_layer:
            S.dma("sp", yp[blk * 128:(blk + 1) * 128, :], xt[:, :], reads=[xkey])
        else:
            S.dma("sp", Xs.ap()[blk * 128:(blk + 1) * 128, :], xt[:, :], reads=[xkey], writes=[f"Xs{blk}"])

    def pipeline(gens, window=2, tag=""):
        import os
        window = int(os.environ.get("KWIN" + tag, str(window)))
        pending = list(gens)
        active = []
        while pending or active:
            nxt = []
            for g in active:
                try:
                    next(g)
                    nxt.append(g)
                except StopIteration:
                    pass
            active = nxt
            if pending and len(active) < window:
                g = pending.pop(0)
                try:
                    next(g)
                    active.append(g)
                except StopIteration:
                    pass

    def sample_block(l, last_layer):
        R = NS
        par["p"] = 0
        norm_T(l, xsmp, "xsmp")
        norm_T2(l)
        pq, kq = PSF()
        for j in range(4):
            proj_fm(pq[:, j * 128:(j + 1) * 128], kq, C_QA + j * 128, 128, last=(j == 3))
        act(QTs[:, :], pq[:, :], AF.Copy, [kq], ["QTs"], scale=0.125)
        pk, kk = PSF()
        proj_tm(pk[:, 0:256], kk, C_KA, 256)
        cp("dve", kvtok[:, :], pk[:, 0:256], [kk], ["kvtok"])
        gate(C_GA, Ga, "Ga")
        S.dma("sp", KVw[127:128, :, :], kvtok[0:NS, 0:128], reads=["kvtok"], writes=["KVw"])
        S.dma("sp", oks[l].rearrange("i r f -> r i f"), KVw[:, :, :], reads=["KVw"])
        for q4 in range(4):
            pt, kt = PSF()
            for ii in range(4):
                i = q4 * 4 + ii
                tr(pt[:, ii * 128:(ii + 1) * 128], KVw[:, i, :], identf[:, :], ["KVw", "identf"], [kt], inc=(ii == 3))
            cp("act", KwT[:, q4 * 4:(q4 + 1) * 4, :], pt[:, :].rearrange("p (i r) -> p i r", i=4), [kt], ["KVx"])
        QTv = QTs[:, :].rearrange("p (j t) -> p j t", j=4)
        pls = [PSF(), PSF()]
        for kv in range(2):
            pl, kl = pls[kv]
            for i in range(NS):
                mm(pl[:, i * 4:i * 4 + 4], KwT[kv * 64:(kv + 1) * 64, i, :], QTv[kv * 64:(kv + 1) * 64, :, i],
                   True, True, ["KVx", "QTs"], [kl], inc=(i == NS - 1))
        for kv in range(2):
            pl, kl = pls[kv]
            tt("dve", Ls[:, kv * 64:(kv + 1) * 64].rearrange("p (i j) -> p i j", j=4), pl[:, 0:64].rearrange("p (i j) -> p i j", j=4),
               bsmp[:, kv * 4:(kv + 1) * 4].unsqueeze(1).to_broadcast([128, NS, 4]), ALU.add, [kl, "bsmp"], ["Ls"])
        act(PTs[:, :], Ls[:, :], AF.Exp, ["Ls"], ["PTs"])
        for kv in range(2):
            tt("dve", PTm[:, kv * 4:(kv + 1) * 4, :, :],
               PTs[:, kv * 64:(kv + 1) * 64].rearrange("p (i j) -> p j i", j=4).unsqueeze(3).to_broadcast([128, 4, 16, 16]),
               eye16[:, :, :].unsqueeze(1).to_broadcast([128, 4, 16, 16]), ALU.mult, ["PTs", "eye16"], ["PT"])
        S.dma("sp", KVw[0:127, :, :], swv[l, :, 1:128, :].rearrange("i r f -> r i f"), writes=["KVw"])
        S.dma("sp", KVw[127:128, :, :], kvtok[0:NS, 128:256], reads=["kvtok"], writes=["KVw"])
        S.dma("sp", ovs[l].rearrange("i r f -> r i f"), KVw[:, :, :], reads=["KVw"])
        S.op("dve", lambda e: e.memset(KVx[:, :], 1.0), reads=[], writes=["KVx"])
        cp("dve", Vaus[:, :, :, 0:64], KVw[:, :, :].rearrange("p i (a d) -> p i a d", a=2), ["KVw"], ["KVx"])
        pos, kpos = [], []
        for kv in range(2):
            po, kpo = PSF()
            pos.append(po)
            kpos.append(kpo)
            for g in range(4):
                h = kv * 4 + g
                for i in range(NS):
                    mm(po[0:R, g * 65:(g + 1) * 65], PTm[:, h, i, :], Vaus[:, i, kv, 0:65], i == 0, i == NS - 1, ["PT", "KVx"], [kpo],
                       inc=(g == 3 and i == NS - 1))
        attn_norm_out(pos, kpos, R)
        gate(C_GB, UG, "UG")
        pu, ku = PSF()
        proj_tm(pu[:, :], ku, C_UB, 512)
        tt("dve", UG[:, :], pu[:, :], UG[:, :], ALU.mult, [ku, "UG"], ["UG"])
        ln_v(l)
        S.dma("sp", ocv[l, :, :], vnf[0:R, :], reads=["vnf"])
        tt("dve", Tt[0:R, :].rearrange("p (h d) -> p h d", h=8), vnf[0:R, :].rearrange("p (h d) -> p h d", h=8),
           w00[0:R, :].unsqueeze(2).to_broadcast([R, 8, 64]), ALU.mult, ["vnf", "w00"], ["Tt"])
        tt("dve", Tt[0:R, :].rearrange("p (h d) -> p h d", h=8), Tt[0:R, :].rearrange("p (h d) -> p h d", h=8),
           b00[0:R, :].unsqueeze(2).to_broadcast([R, 8, 64]), ALU.add, ["Tt", "b00"], ["Tt"])
        tt("dve", Y[0:R, 512:1024], Tt[0:R, :], UG[0:R, :], ALU.mult, ["Tt", "UG"], ["Y"])
        gla_decay(l, True)
        pqk, kqk = PSF()
        for hh in range(2):
            proj_fm(pqk[:, hh * 128:(hh + 1) * 128], kqk, C_QC + hh * 128, 128, last=False)
        for hh in range(2):
            proj_fm(pqk[:, 256 + hh * 128:256 + (hh + 1) * 128], kqk, C_KC + hh * 128, 128, last=(hh == 1))
        ts("dve", qTs[:, :], pqk[:, 0:256], 0.125, None, ALU.mult, None, [kqk], ["qTs"])
        cp("dve", kTs[:, :], pqk[:, 256:512], [kqk], ["kTs"])
        pv, kv_ = PSF()
        proj_tm(pv[:, :], kv_, C_VC, 512)
        cp("act", vcb[:, :], pv[:, :], [kv_], ["vcb"])
        gate(C_GC, Gc, "Gc")
        tt("dve", Gc[:, :], Gc[:, :], gnG[:, :], ALU.mult, ["Gc", "gnG"], ["Gc"])
        q_v = qTs[:, :].rearrange("p (hh t) -> p hh t", hh=2)[:, :, 0:NS]
        tt("dve", QM[:, :, :, :], q_v.unsqueeze(3).to_broadcast([128, 2, 16, 16]),
           eye16[:, :, :].unsqueeze(1).to_broadcast([128, 2, 16, 16]), ALU.mult, ["qTs", "eye16"], ["QM"])
        a_v = aa[:, :].rearrange("p (hh t) -> p hh t", hh=2)
        for q4 in range(NS // 4):
            S.dma("sp", SS[:, :, :, :], sgl[l, q4 * 4:(q4 + 1) * 4].rearrange("i (hh p) v -> p i hh v", p=128), writes=["SS"])
            tt("dve", SS[:, :, :, :], SS[:, :, :, :],
               a_v[:, :, q4 * 4:(q4 + 1) * 4].rearrange("p hh i -> p i hh").unsqueeze(3).to_broadcast([128, 4, 2, 128]),
               ALU.mult, ["SS", "aa"], ["SS"])
            for i2 in range(2):
                pvb, kvb = PSF()
                for il in range(2):
                    i = q4 * 4 + i2 * 2 + il
                    for h in range(4):
                        hh, hp = h // 2, h % 2
                        mm(pvb[hp * 64:(hp + 1) * 64, (il * 2 + hh) * 128:(il * 2 + hh + 1) * 128], esel[0:NS, i, :],
                           vcb[0:NS, h * 128:(h + 1) * 128], True, True, ["esel", "vcb"], [kvb], inc=(il == 1 and h == 3))
                for il in range(2):
                    i = q4 * 4 + i2 * 2 + il
                    for hh in range(2):
                        stt("dve", SS[:, i2 * 2 + il, hh, :], pvb[:, (il * 2 + hh) * 128:(il * 2 + hh + 1) * 128],
                            kTs[:, hh * 128 + i:hh * 128 + i + 1], SS[:, i2 * 2 + il, hh, :], ALU.mult, ALU.add,
                            [kvb, "kTs", "SS"], ["SS"])
            S.dma("sp", ogs[l, q4 * 4:(q4 + 1) * 4].rearrange("i (hh p) v -> p i hh v", p=128), SS[:, :, :, :], reads=["SS"])
            cp("act", SSb[:, :, :, :], SS[:, :, :, :], ["SS"], ["SSb"])
            pos2 = [PSF(), PSF()]
            for hp in range(2):
                po, kpo = pos2[hp]
                for hh in range(2):
                    for il in range(4):
                        i = q4 * 4 + il
                        mm(po[0:R, hh * 128:(hh + 1) * 128], QM[hp * 64:(hp + 1) * 64, hh, i, :], SSb[hp * 64:(hp + 1) * 64, il, hh, :],
                           il == 0, il == 3, ["QM", "SSb"], [kpo], inc=(hh == 1 and il == 3))
            for h in range(4):
                hh, hp = h // 2, h % 2
                po, kpo = pos2[hp]
                if q4 == 0:
                    cp("dve", vnf[0:R, h * 128:(h + 1) * 128], po[0:R, hh * 128:(hh + 1) * 128], [kpo], ["vnf"])
                else:
                    tt("dve", vnf[0:R, h * 128:(h + 1) * 128], vnf[0:R, h * 128:(h + 1) * 128], po[0:R, hh * 128:(hh + 1) * 128],
                       ALU.add, ["vnf", kpo], ["vnf"])
        group_norm_out(vnf, "vnf", R)
        out_proj(l, xsmp, "xsmp", R, None, None, last_layer)
        if last_layer:
            S.dma("sp", ys[:, :], xsmp[0:R, :], reads=["xsmp"])

    for l in range(DEPTH):
        par["p"] = 0
        for blk in range(min(2, NB)):
            load_x(l, blk, 1)
            prefetched.add((l, blk, 1))
        load_layer_params(l)
        S.dma("sp", KVw[0:127, :, :], swk[l, :, 1:128, :].rearrange("i r f -> r i f"), writes=["KVw"])
        S.op("dve", lambda e: e.memset(Sst[:, :], 0.0), reads=[], writes=["Sst"])
        S.op("dve", lambda e: e.memset(Ptot[:, :], 1.0), reads=[], writes=["Ptot"])
        pipeline([phase1_block(l, blk) for blk in range(NB)], tag="1")
        par["p"] = 0
        prep_spatial(l)
        exchange_a(l)
        sample_block(l, l == DEPTH - 1)
        par["p"] = 0
        exchange_b(l)
        if l + 1 < DEPTH:
            convert_layer(l + 1)
        pipeline([phase2_block(l, blk, l == DEPTH - 1) for blk in range(NB)], tag="2")
        par["p"] = 0
        S.dma("sp", ogp[l].rearrange("(hh p) v -> p hh v", p=128), Sst[:, :].rearrange("p (hh v) -> p hh v", hh=2), reads=["Sst"])
    S.finish("sp")
    S.emit()
    return nc


def t5_bucket(dist):
    n = np.maximum(dist, 0)
    max_exact = 16
    large = max_exact + (np.log(np.maximum(n, 1) / max_exact) / np.log(128 / max_exact) * (32 - max_exact)).astype(np.int32)
    large = np.minimum(large, 31)
    return np.where(n < max_exact, n, large).astype(np.int32)


_PROG = {}


def prepare(inputs, NB, DEPTH):
    f32 = lambda a: np.ascontiguousarray(np.asarray(a, dtype=np.float32))
    xpr = f32(inputs["x_prompt"])
    xsa = f32(inputs["x_sample"]).reshape(128, D)
    swk = f32(inputs["state_swa_k"]).reshape(DEPTH, 128, 128, 128)
    swv = f32(inputs["state_swa_v"]).reshape(DEPTH, 128, 128, 128)
    sgl = f32(inputs["state_gla"]).reshape(DEPTH, 128, 256, 128)
    rel_bias = f32(inputs["rel_bias"])
    i = np.arange(128)[:, None]
    j = np.arange(256)[None, :]
    dist = i + 128 - j
    band = (dist >= 0) & (dist < 128)
    table = np.concatenate([rel_bias, np.full((1, 8), NEG, np.float32)], 0)
    full = table[np.where(band, t5_bucket(dist), 32)]
    b_prev = np.ascontiguousarray(full[:, 0:128, :].transpose(1, 2, 0)).reshape(128, 1024)
    b_cur = np.ascontiguousarray(full[:, 128:256, :].transpose(1, 2, 0)).reshape(128, 1024)
    b_neg = np.full((128, 1024), NEG, np.float32)
    b_smp = np.ascontiguousarray(rel_bias[t5_bucket(127 - np.arange(128))])
    shared = {
        "w_in": f32(inputs["w_in"]), "w_out": f32(inputs["w_out"]),
        "norm_g": f32(inputs["norm_g"]).reshape(DEPTH * 8, 128), "fin_g": f32(inputs["final_norm_g"]).reshape(1, D),
        "sinks": f32(inputs["sinks"]), "sp_w": f32(inputs["spatial_w"]),
        "sp_b": f32(inputs["spatial_b"]).reshape(DEPTH * 8, 128),
        "ln_g": f32(inputs["chunk_ln_g"]), "ln_b": f32(inputs["chunk_ln_b"]),
        "wup": f32(inputs["gla_w_up"]), "bup": f32(inputs["gla_b_up"]).reshape(DEPTH * 2, 128),
        "gn_g": f32(inputs["gla_norm_g"]), "b_cur": b_cur, "b_prev": b_prev, "b_smp": b_smp,
    }
    NT = NB * 128
    in_maps = []
    for c in range(8):
        b, s = c // 4, c % 4
        cm = np.zeros((128, 8), np.float32)
        for jj in range(3):
            cm[:, jj] = 1.0 if jj < s else 0.0
            cm[:, 3 + jj] = 1.0 if jj == s - 1 else 0.0
        m = dict(shared)
        m.update({
            "xp": np.ascontiguousarray(xpr[b, s * NT:(s + 1) * NT, :]),
            "xs": np.ascontiguousarray(xsa[c * NS:(c + 1) * NS]),
            "swk": np.ascontiguousarray(swk[:, c * NS:(c + 1) * NS]),
            "swv": np.ascontiguousarray(swv[:, c * NS:(c + 1) * NS]),
            "sgl": np.ascontiguousarray(sgl[:, c * NS:(c + 1) * NS]),
            "b_prev0": b_neg if s == 0 else b_prev,
            "cmask": cm,
        })
        in_maps.append(m)
    return in_maps


def assemble(r, DEPTH):
    y_prompt = np.stack([np.concatenate([r[b * 4 + s]["yp"] for s in range(4)], 0) for b in range(2)], 0)
    y_sample = np.concatenate([r[c]["ys"] for c in range(8)], 0).reshape(128, 1, D)
    kp = np.stack([r[b * 4 + 3]["okp"] for b in range(2)], 1).reshape(DEPTH, 2, 128, 2, 64)
    vp = np.stack([r[b * 4 + 3]["ovp"] for b in range(2)], 1).reshape(DEPTH, 2, 128, 2, 64)
    gp = np.stack([r[b * 4 + 3]["ogp"] for b in range(2)], 1).reshape(DEPTH, 2, 4, 64, 128)
    ks = np.concatenate([r[c]["oks"] for c in range(8)], 1).reshape(DEPTH, 128, 128, 2, 64)
    vs = np.concatenate([r[c]["ovs"] for c in range(8)], 1).reshape(DEPTH, 128, 128, 2, 64)
    gs = np.concatenate([r[c]["ogs"] for c in range(8)], 1).reshape(DEPTH, 128, 4, 64, 128)
    cv = np.concatenate([r[c]["ocv"] for c in range(8)], 1).reshape(DEPTH, 128, 1, 512)
    return tuple(np.ascontiguousarray(np.asarray(a, dtype=np.float32)) for a in (y_prompt, y_sample, kp, vp, gp, ks, vs, gs, cv))


def run(inputs, NB, DEPTH):
    in_maps = prepare(inputs, NB, DEPTH)
    key = (NB, DEPTH)
    if key not in _PROG:
        _PROG[key] = build_program(NB, DEPTH)
    res = run_bass_kernel_spmd(_PROG[key], in_maps, core_ids=list(range(8)))
    return assemble(res.results, DEPTH)


def kernel(**inputs):
    return run(inputs, 16, 4)
```

```python
import numpy as np
import concourse.bass as bass
import concourse.mybir as mybir
from concourse.bass_utils import run_bass_kernel_spmd

F32 = mybir.dt.float32
BF16 = mybir.dt.bfloat16
I32 = mybir.dt.int32
ALU = mybir.AluOpType
AF = mybir.ActivationFunctionType

D = 1024
DIN = 4368
DMIX = 1536
EPS = 1e-6
NEG = -1e30
NS = 16
MW = 520
C_QA, C_KA, C_VA, C_GA = 0, 512, 640, 768
C_UB, C_VB, C_GB = 1280, 1792, 2304
C_QC, C_KC, C_VC, C_GC, C_LR = 2816, 3072, 3328, 3840, 4352


class Sched:
    def __init__(self, nc, n_dma=32):
        self.nc = nc
        self.engs = {"pe": nc.tensor, "act": nc.scalar, "dve": nc.vector, "pool": nc.gpsimd, "sp": nc.sync}
        self.sem = {k: nc.alloc_semaphore(name=f"sem_{k}") for k in self.engs}
        self.cnt = {k: 0 for k in self.engs}
        self.ops = {k: [] for k in self.engs}
        self.waited = {k: {} for k in self.engs}
        self.lastw = {}
        self.readers = {}
        self.dma_sems = [nc.alloc_semaphore(name=f"sem_dma{i}") for i in range(n_dma)]
        self.dma_cnt = [0] * n_dma
        self.n_sw = 8
        self.dma_rr = {"pool": 0, "sp": self.n_sw}
        self.cc_sem = nc.alloc_semaphore(name="sem_cc")
        self.cc_cnt = 0
        self.semobj = {("c", 0): self.cc_sem}
        self.keymap = lambda k: k
        for k in self.engs:
            self.semobj[("e", k)] = self.sem[k]
        for i, s in enumerate(self.dma_sems):
            self.semobj[("d", i)] = s

    def _wait(self, eng, tok):
        if tok is None:
            return
        key, val = tok
        if key == ("e", eng) and eng in ("pe", "sp"):
            return
        if self.waited[eng].get(key, 0) >= val:
            return
        self.waited[eng][key] = val
        sem = self.semobj[key]
        self.ops[eng].append(lambda e, sem=sem, val=val: e.wait_ge(sem, val))

    def _deps(self, eng, reads, writes):
        reads = [self.keymap(k) for k in reads]
        writes = [self.keymap(k) for k in writes]
        for r in reads:
            self._wait(eng, self.lastw.get(r))
        for w in writes:
            self._wait(eng, self.lastw.get(w))
            for t in self.readers.get(w, []):
                self._wait(eng, t)

    def _commit(self, tok, reads, writes):
        reads = [self.keymap(k) for k in reads]
        writes = [self.keymap(k) for k in writes]
        for r in reads:
            self.readers.setdefault(r, []).append(tok)
        for w in writes:
            self.lastw[w] = tok
            self.readers[w] = []

    def op(self, eng, fn, reads=(), writes=(), inc=True):
        self._deps(eng, reads, writes)
        if inc:
            self.cnt[eng] += 1
            sem = self.sem[eng]
            self.ops[eng].append(lambda e, fn=fn, sem=sem: fn(e).then_inc(sem, 1))
        else:
            assert eng == "pe"
            self.ops[eng].append(lambda e, fn=fn: fn(e))
        tok = (("e", eng), self.cnt[eng] + (0 if inc else 1))
        self._commit(tok, reads, writes)
        return tok

    def dma(self, eng, out, in_, reads=(), writes=()):
        lo, hi = (0, self.n_sw) if eng == "pool" else (self.n_sw, len(self.dma_sems))
        i = self.dma_rr[eng]
        self.dma_rr[eng] = lo + (i + 1 - lo) % (hi - lo)
        if self.dma_cnt[i] > 0:
            self._wait(eng, (("d", i), self.dma_cnt[i]))
        self._deps(eng, reads, writes)
        self.dma_cnt[i] += 16
        sem = self.dma_sems[i]
        self.ops[eng].append(lambda e, out=out, in_=in_, sem=sem: e.dma_start(out=out, in_=in_).then_inc(sem, 16))
        tok = (("d", i), self.dma_cnt[i])
        self._commit(tok, reads, writes)
        return tok

    def allgather(self, src, dst, groups, reads=(), writes=()):
        eng = "pool"
        self._deps(eng, reads, writes)
        self.cc_cnt += 1
        sem = self.cc_sem
        self.ops[eng].append(lambda e: e.collective_compute(
            "AllGather", ALU.bypass, replica_groups=groups, ins=[src], outs=[dst]).then_inc(sem, 1))
        tok = (("c", 0), self.cc_cnt)
        self._commit(tok, reads, writes)
        return tok

    def finish(self, eng="sp"):
        for i, c in enumerate(self.dma_cnt):
            if c:
                self._wait(eng, (("d", i), c))
        for k in self.engs:
            if k != eng and self.cnt[k]:
                self._wait(eng, (("e", k), self.cnt[k]))

    def emit(self):
        with self.nc.Block() as block:
            @block.tensor
            def _(e):
                for f in self.ops["pe"]:
                    f(e)

            @block.scalar
            def _(e):
                for f in self.ops["act"]:
                    f(e)

            @block.vector
            def _(e):
                for f in self.ops["dve"]:
                    f(e)

            @block.gpsimd
            def _(e):
                for f in self.ops["pool"]:
                    f(e)

            @block.sync
            def _(e):
                for f in self.ops["sp"]:
                    f(e)


def build_program(NB, DEPTH):
    nc = bass.Bass("TRN2", target_bir_lowering=False)
    S = Sched(nc)
    NT = NB * 128

    def din(name, shape):
        return nc.dram_tensor(name, list(shape), F32, kind="ExternalInput").ap()

    def dout(name, shape):
        return nc.dram_tensor(name, list(shape), F32, kind="ExternalOutput").ap()

    xp = din("xp", [NT, D])
    xsm = din("xs", [NS, D])
    swk = din("swk", [DEPTH, NS, 128, 128])
    swv = din("swv", [DEPTH, NS, 128, 128])
    sgl = din("sgl", [DEPTH, NS, 256, 128])
    w_in = din("w_in", [DEPTH, D, DIN])
    w_out = din("w_out", [DEPTH, DMIX, D])
    norm_g = din("norm_g", [DEPTH * 8, 128])
    fin_g = din("fin_g", [1, D])
    sinks = din("sinks", [DEPTH, 8])
    sp_w = din("sp_w", [DEPTH, 8, 128, 128])
    sp_b = din("sp_b", [DEPTH * 8, 128])
    ln_g = din("ln_g", [DEPTH, 512])
    ln_b = din("ln_b", [DEPTH, 512])
    wup = din("wup", [DEPTH, 16, 256])
    bup = din("bup", [DEPTH * 2, 128])
    gn_g = din("gn_g", [DEPTH, 512])
    b_cur = din("b_cur", [128, 1024])
    b_prev = din("b_prev", [128, 1024])
    b_prev0 = din("b_prev0", [128, 1024])
    b_smp = din("b_smp", [128, 8])
    cmask = din("cmask", [128, 8])

    yp = dout("yp", [NT, D])
    ys = dout("ys", [NS, D])
    okp = dout("okp", [DEPTH, 128, 128])
    ovp = dout("ovp", [DEPTH, 128, 128])
    ogp = dout("ogp", [DEPTH, 256, 128])
    oks = dout("oks", [DEPTH, NS, 128, 128])
    ovs = dout("ovs", [DEPTH, NS, 128, 128])
    ogs = dout("ogs", [DEPTH, NS, 256, 128])
    ocv = dout("ocv", [DEPTH, NS, 512])

    Xs = nc.dram_tensor("Xs", [NT, D], F32)
    Wib = nc.dram_tensor("Wib", [DEPTH, D, DIN], BF16)
    Wob = nc.dram_tensor("Wob", [DEPTH, DMIX, D], BF16)
    msg = nc.dram_tensor("msg", [128, MW], F32)
    gath = nc.dram_tensor("gath", [512, MW], F32)

    def sb(name, shape, dt=F32):
        return nc.alloc_sbuf_tensor(name, list(shape), dt)

    par = {"p": 0}
    DBL = set()

    class Dbl:
        def __init__(self, name, shape, dt=F32):
            if len(shape) == 2 and shape[1] <= 8:
                self.t = [nc.alloc_sbuf_tensor(f"{name}{i}", [128, 32], dt, align_bytes=128) for i in range(2)]
            else:
                self.t = [sb(f"{name}{i}", shape, dt) for i in range(2)]
            DBL.add(name)
            self.name = name

        def __getitem__(self, idx):
            return self.t[par["p"]][idx]

    S.keymap = lambda k: (k + str(par["p"])) if k in DBL else k

    Wi = sb("Wi", [128, 8, DIN], BF16)
    Wo = sb("Wo", [128, 12, D], BF16)
    ident = sb("ident", [128, 128], BF16)
    identf = sb("identf", [128, 128])
    maskLT = sb("maskLT", [128, 128])
    cmA = sb("cmA", [128, 128])
    eye16 = sb("eye16", [128, 16, 16], BF16)
    esel = sb("esel", [16, 16, 64], BF16)
    onesf = sb("onesf", [1, 128])
    w0rowf = sb("w0rowf", [1, 16])
    ones_bf = sb("ones_bf", [1, 128], BF16)
    ones64 = sb("ones64", [128, 64])
    cneg05 = sb("cneg05", [128, 8])
    bcur = sb("bcur", [128, 1024], BF16)
    bprev = sb("bprev", [128, 1024], BF16)
    bprev0 = sb("bprev0", [128, 1024], BF16)
    bsmp = sb("bsmp", [128, 8])
    cmk = sb("cmk", [128, 8])
    rowsA = sb("rowsA", [DEPTH * 8 + DEPTH * 2, 128])
    colsA = sb("colsA", [128, DEPTH * 10])
    rowsB = sb("rowsB", [DEPTH * 8, 128])
    bsT = sb("bsT", [128, DEPTH * 8])
    fG = sb("fG", [128, D])
    lnG = sb("lnG", [128, 512])
    lnB = sb("lnB", [128, 512])
    gnG = sb("gnG", [128, 512])
    esink = sb("esink", [128, 8])
    wupb = sb("wupb", [16, 256], BF16)
    WsT = sb("WsT", [128, 8, 128], BF16)
    w00 = sb("w00", [128, 8])
    b00 = sb("b00", [128, 8])
    Xb = [sb(f"Xb{i}", [128, D]) for i in range(2)]
    xsmp = sb("xsmp", [128, D])
    ssq = Dbl("ssq", [128, 8])
    nwI = Dbl("nwI", [128, 8], I32)
    nwA = Dbl("nwA", [128, 8])
    nwB = Dbl("nwB", [128, 8])
    rst = Dbl("rst", [128, 8])
    xsb = Dbl("xsb", [128, D], BF16)
    hT = Dbl("hT", [128, 8, 128], BF16)
    QTs = Dbl("QTs", [128, 512], BF16)
    KTb = [sb(f"KTb{i}", [128, 128], BF16) for i in range(3)]
    KTh = sb("KTh", [128, 128], BF16)
    Vaug = [sb(f"Vaug{i}", [128, 2, 72], BF16) for i in range(3)]
    Vaugh = sb("Vaugh", [128, 2, 72], BF16)
    kvtok = sb("kvtok", [128, 256])
    Tt = sb("Tt", [128, 512])
    Ga = Dbl("Ga", [128, 512], BF16)
    UG = Dbl("UG", [128, 512], BF16)
    Gc = Dbl("Gc", [128, 512], BF16)
    PT = sb("PT", [128, 4, 512], BF16)
    den = Dbl("den", [128, 8])
    rden = Dbl("rden", [128, 8])
    Y = Dbl("Y", [128, DMIX], BF16)
    bst = Dbl("bst", [128, 6])
    bmv = Dbl("bmv", [128, 2])
    vnf = sb("vnf", [128, 512])
    vnb = Dbl("vnb", [128, 512], BF16)
    lrTb = sb("lrTb", [16, 128], BF16)
    ee = sb("ee", [128, 256])
    aa = sb("aa", [128, 256])
    EbT = Dbl("EbT", [128, 256])
    EnbT = sb("EnbT", [128, 256])
    qdT = Dbl("qdT", [128, 256], BF16)
    kiT = Dbl("kiT", [128, 256], BF16)
    kitok = Dbl("kitok", [128, 256], BF16)
    vcb = Dbl("vcb", [128, 512], BF16)
    attT = sb("attT", [128, 512], BF16)
    Sst = sb("Sst", [128, 256])
    Sbf = [sb(f"Sbf{i}", [128, 256], BF16) for i in range(2)]
    Ptot = sb("Ptot", [128, 2])
    yT = sb("yT", [128, 12, 128], BF16)
    msgt = sb("msgt", [128, MW])
    KVw = sb("KVw", [128, NS, 128])
    KVx = sb("KVx", [128, NS * 144], BF16)
    KwT = KVx[:, 0:NS * 128].rearrange("p (i r) -> p i r", i=NS)
    Vaus = KVx[:, :].rearrange("p (i a e) -> p i a e", i=NS, a=2)
    Ls = sb("Ls", [128, 128])
    PTs = sb("PTs", [128, 128], BF16)
    PTm = PT[:, :, :].rearrange("p a (b c d) -> p (a b) c d", b=2, c=16)
    SS = sb("SS", [128, 4, 2, 128])
    SSb = sb("SSb", [128, 4, 2, 128], BF16)
    wsStage = SS[:, :, :, :].rearrange("p a b c -> p (a b) c")
    qTs = sb("qTs", [128, 256])
    kTs = sb("kTs", [128, 256])
    QM = sb("QM", [128, 2, 16, 16], BF16)

    psf = [nc.alloc_psum_tensor(f"psf{i}", [128, 512], F32) for i in range(6)]
    psb = [nc.alloc_psum_tensor(f"psb{i}", [128, 1024], BF16) for i in range(2)]
    ring = {"f": 0, "b": 0}
    pinned = set()

    def PSF(pin=False):
        i = ring["f"]
        while i in pinned:
            i = (i + 1) % 6
        ring["f"] = (i + 1) % 6
        if pin:
            pinned.add(i)
        return psf[i], f"psf{i}"

    def unpin(key):
        pinned.discard(int(key[3:]))

    def PSB():
        i = ring["b"]
        ring["b"] = (i + 1) % 2
        return psb[i], f"psb{i}"

    def mm(out, lhsT, rhs, start, stop, reads, writes, inc):
        S.op("pe", lambda e: e.matmul(out, lhsT=lhsT, rhs=rhs, start=start, stop=stop),
             reads=reads, writes=writes, inc=inc)

    def tr(out, in_, idn, reads, writes, inc=True):
        S.op("pe", lambda e: e.transpose(out=out, in_=in_, identity=idn), reads=reads, writes=writes, inc=inc)

    def act(out, in_, func, reads, writes, scale=1.0, bias=None, accum=None):
        kw = {}
        if bias is not None:
            kw["bias"] = bias
        if accum is not None:
            kw["accum_out"] = accum
        S.op("act", lambda e: e.activation(out=out, in_=in_, func=func, scale=scale, **kw), reads=reads, writes=writes)

    def tt(eng, out, in0, in1, op, reads, writes):
        S.op(eng, lambda e: e.tensor_tensor(out=out, in0=in0, in1=in1, op=op), reads=reads, writes=writes)

    def ts(eng, out, in0, s1, s2, op0, op1, reads, writes):
        if s2 is None:
            S.op(eng, lambda e: e.tensor_scalar(out=out, in0=in0, scalar1=s1, scalar2=None, op0=op0), reads=reads, writes=writes)
        else:
            S.op(eng, lambda e: e.tensor_scalar(out=out, in0=in0, scalar1=s1, scalar2=s2, op0=op0, op1=op1), reads=reads, writes=writes)

    def stt(eng, out, in0, scalar, in1, op0, op1, reads, writes):
        S.op(eng, lambda e: e.scalar_tensor_tensor(out=out, in0=in0, scalar=scalar, in1=in1, op0=op0, op1=op1),
             reads=reads, writes=writes)

    def cp(eng, out, in_, reads, writes):
        if eng == "act":
            S.op(eng, lambda e: e.activation(out=out, in_=in_, func=AF.Copy), reads=reads, writes=writes)
        else:
            S.op(eng, lambda e: e.tensor_copy(out=out, in_=in_), reads=reads, writes=writes)

    def rsqrt_eps(src, dst, n, scale, reads_key, writes_key):
        ts("dve", dst, src, scale, EPS, ALU.mult, ALU.add, [reads_key], [writes_key])
        yi = nwA[:, 0:n]
        S.op("dve", lambda e, o=nwI[:, 0:n], i=dst.bitcast(I32): e.tensor_scalar(out=o, in0=i, scalar1=1, scalar2=None,
                                                                                  op0=ALU.arith_shift_right),
             reads=[writes_key], writes=["nwI"])
        S.op("dve", lambda e, o=yi.bitcast(I32), i=nwI[:, 0:n]: e.tensor_scalar(out=o, in0=i, scalar1=-1.0, scalar2=1597463007.0,
                                                                                op0=ALU.mult, op1=ALU.add),
             reads=["nwI"], writes=["nwA"])
        for it in range(2):
            tt("dve", nwB[:, 0:n], yi, dst, ALU.mult, ["nwA", writes_key], ["nwB"])
            stt("dve", nwB[:, 0:n], nwB[:, 0:n], -0.5, yi, ALU.mult, ALU.mult, ["nwB", "nwA"], ["nwB"])
            if it == 0:
                stt("dve", yi, nwB[:, 0:n], 1.5, yi, ALU.add, ALU.mult, ["nwB", "nwA"], ["nwA"])
            else:
                stt("dve", dst, nwB[:, 0:n], 1.5, yi, ALU.add, ALU.mult, ["nwB", "nwA"], [writes_key])

    def wkeys():
        return [f"Wi{k}" for k in range(8)]

    def wkey(c0):
        return "WiA" if c0 < C_UB else ("WiB" if c0 < C_QC else "WiC")

    def proj_fm(ps_ap, pskey, c0, M, last=True):
        for k in range(8):
            mm(ps_ap, Wi[:, k, c0:c0 + M], hT[:, k, :], k == 0, k == 7, ["hT", wkey(c0)], [pskey], inc=(last and k == 7))

    def proj_tm(ps_ap, pskey, c0, N, last=True):
        for k in range(8):
            mm(ps_ap, hT[:, k, :], Wi[:, k, c0:c0 + N], k == 0, k == 7, ["hT", wkey(c0)], [pskey], inc=(last and k == 7))

    S.op("pool", lambda e: e.memset(identf[:, :], 0.0), writes=["identf"])
    S.op("pool", lambda e: e.affine_select(out=identf[:, :], in_=identf[:, :], pattern=[[-1, 128]],
                                           compare_op=ALU.not_equal, fill=1.0, base=0, channel_multiplier=1),
         reads=["identf"], writes=["identf"])
    cp("dve", ident[:, :], identf[:, :], ["identf"], ["ident"])
    S.op("pool", lambda e: e.memset(maskLT[:, :], 1.0), writes=["maskLT"])
    S.op("pool", lambda e: e.affine_select(out=maskLT[:, :], in_=maskLT[:, :], pattern=[[1, 128]],
                                           compare_op=ALU.is_ge, fill=0.0, base=0, channel_multiplier=-1),
         reads=["maskLT"], writes=["maskLT"])
    cp("dve", cmA[:, :], maskLT[:, :], ["maskLT"], ["cmA"])
    S.op("dve", lambda e: e.memset(cmA[0:64, 64:128], 0.0), reads=["cmA"], writes=["cmA"])
    S.op("pool", lambda e: e.memset(eye16[:, :, :], 0.0), writes=["eye16"])
    S.op("pool", lambda e: e.affine_select(out=eye16[:, :, :], in_=eye16[:, :, :], pattern=[[1, 16], [-1, 16]],
                                           compare_op=ALU.not_equal, fill=1.0, base=0, channel_multiplier=0),
         reads=["eye16"], writes=["eye16"])
    S.op("pool", lambda e: e.memset(esel[:, :, :], 0.0), writes=["esel"])
    S.op("pool", lambda e: e.affine_select(out=esel[:, :, :], in_=esel[:, :, :], pattern=[[1, 16], [0, 64]],
                                           compare_op=ALU.not_equal, fill=1.0, base=0, channel_multiplier=-1),
         reads=["esel"], writes=["esel"])
    S.op("dve", lambda e: e.memset(onesf[:, :], 1.0), writes=["onesf"])
    S.op("dve", lambda e: e.memset(ones_bf[:, :], 1.0), writes=["ones_bf"])
    S.op("dve", lambda e: e.memset(ones64[:, :], 1.0), writes=["ones64"])
    S.op("dve", lambda e: e.memset(cneg05[:, :], -0.5), writes=["cneg05"])
    S.op("dve", lambda e: e.memset(xsmp[:, :], 0.0), writes=["xsmp"])
    S.op("dve", lambda e: e.memset(kvtok[:, :], 0.0), writes=["kvtok"])
    S.op("dve", lambda e: e.memset(msgt[:, :], 0.0), writes=["msgt"])
    for d_ in (ssq, rst, den, rden, bst, bmv, nwA, nwB, Y):
        for i in range(2):
            S.op("dve", lambda e, t=d_.t[i]: e.memset(t[:, :], 1.0), writes=[f"{d_.name}{i}"])
    for i in range(2):
        S.op("dve", lambda e, t=nwI.t[i]: e.memset(t[:, :], 0), writes=[f"nwI{i}"])
    for i in range(3):
        S.op("dve", lambda e, i=i: e.memset(Vaug[i][:, :, :], 1.0), writes=[f"Vaug{i}"])
    S.op("dve", lambda e: e.memset(Vaugh[:, :, :], 1.0), writes=["Vaugh"])
    S.dma("pool", bcur[:, :], b_cur[:, :], writes=["bcur"])
    S.dma("pool", bprev[:, :], b_prev[:, :], writes=["bprev"])
    S.dma("pool", bprev0[:, :], b_prev0[:, :], writes=["bprev0"])
    S.dma("sp", bsmp[:, :], b_smp[:, :], writes=["bsmp"])
    S.dma("sp", cmk[:, :], cmask[:, :], writes=["cmk"])
    S.dma("sp", fG[:, :], fin_g[0:1, :].partition_broadcast(128), writes=["fG"])
    S.dma("sp", xsmp[0:NS, :], xsm[:, :], reads=[], writes=["xsmp"])
    nA = DEPTH * 8
    S.dma("sp", rowsA[0:nA, :], norm_g[:, :], writes=["rowsA"])
    S.dma("sp", rowsA[nA:nA + DEPTH * 2, :], bup[:, :], writes=["rowsA"])
    S.dma("sp", rowsB[:, :], sp_b[:, :], writes=["rowsB"])
    pA, kA = PSF()
    nr = DEPTH * 10
    tr(pA[:, 0:nr], rowsA[0:nr, :], identf[0:nr, 0:nr], ["rowsA", "identf"], [kA])
    cp("dve", colsA[:, 0:nA], pA[:, 0:nA], [kA], ["colsA"])
    ts("dve", colsA[:, nA:nr], pA[:, nA:nr], -1.0, None, ALU.mult, None, [kA], ["colsA"])
    pB, kB = PSF()
    tr(pB[:, 0:nA], rowsB[0:nA, :], identf[0:nA, 0:nA], ["rowsB", "identf"], [kB])
    cp("dve", bsT[:, :], pB[:, 0:nA], [kB], ["bsT"])

    def convert_layer(l):
        for k in range(8):
            S.dma("pool", Wib.ap()[l, k * 128:(k + 1) * 128, :], w_in[l, k * 128:(k + 1) * 128, :], writes=[f"Wcv{l}"])
        for k in range(6):
            S.dma("pool", Wob.ap()[l, k * 256:(k + 1) * 256, :], w_out[l, k * 256:(k + 1) * 256, :], writes=[f"Wcv{l}"])

    def load_layer_params(l):
        S.dma("sp", lnG[:, :], ln_g[l:l + 1, :].partition_broadcast(128), writes=["lnG"])
        S.dma("sp", lnB[:, :], ln_b[l:l + 1, :].partition_broadcast(128), writes=["lnB"])
        S.dma("sp", gnG[:, :], gn_g[l:l + 1, :].partition_broadcast(128), writes=["gnG"])
        S.dma("sp", esink[:, :], sinks[l:l + 1, :].partition_broadcast(128), writes=["esink"])
        act(esink[:, :], esink[:, :], AF.Exp, ["esink"], ["esink"])
        S.dma("pool", wupb[:, :], wup[l, :, :], writes=["wupb"])
        S.dma("sp", wsStage, sp_w[l].rearrange("h i j -> i h j"), writes=["SS"])
        for half in range(2):
            pw, kw = PSF()
            for hq in range(4):
                h = half * 4 + hq
                tr(pw[:, hq * 128:(hq + 1) * 128], wsStage[:, h, :], identf[:, :], ["SS", "identf"], [kw], inc=(hq == 3))
            tt("dve", WsT[:, half * 4:half * 4 + 4, :], pw[:, :].rearrange("p (h i) -> p h i", h=4),
               maskLT[:, :].unsqueeze(1).to_broadcast([128, 4, 128]), ALU.mult, [kw, "maskLT"], ["WsT"])
        cp("dve", w0rowf[0:1, 0:8], wsStage[0:1, :, 0], ["SS"], ["w0rowf"])
        cp("dve", w0rowf[0:1, 8:16], bsT[0:1, l * 8:(l + 1) * 8], ["bsT"], ["w0rowf"])
        pw, kw = PSF()
        mm(pw[:, 0:16], onesf[0:1, :], w0rowf[0:1, :], True, True, ["onesf", "w0rowf"], [kw], True)
        cp("dve", w00[:, :], pw[:, 0:8], [kw], ["w00"])
        cp("dve", b00[:, :], pw[:, 8:16], [kw], ["b00"])
        if l == 0:
            q, wsrc, wosrc, rk = "pool", w_in[l], w_out[l], []
        else:
            q, wsrc, wosrc, rk = "sp", Wib.ap()[l], Wob.ap()[l], [f"Wcv{l}"]
        for k in range(8):
            src = wsrc[k * 128:(k + 1) * 128, :]
            S.dma(q, Wi[:, k, C_QC:DIN], src[:, C_QC:DIN], reads=rk, writes=["WiC"])
        for k in range(8):
            src = wsrc[k * 128:(k + 1) * 128, :]
            for a_ in range(2):
                S.dma(q, Wi[:, k, 0:512].rearrange("p (j a i) -> p j a i", j=4, a=2, i=64)[:, :, a_, :],
                      src[:, a_ * 256:(a_ + 1) * 256].rearrange("p (j i) -> p j i", j=4, i=64), reads=rk, writes=["WiA"])
            S.dma(q, Wi[:, k, 512:C_UB], src[:, 512:C_UB], reads=rk, writes=["WiA"])
        for k in range(8):
            src = wsrc[k * 128:(k + 1) * 128, :]
            S.dma(q, Wi[:, k, C_UB:C_QC], src[:, C_UB:C_QC], reads=rk, writes=["WiB"])
        for k in range(12):
            S.dma(q, Wo[:, k, :], wosrc[k * 128:(k + 1) * 128, :], reads=rk, writes=[f"Wo{k}"])

    prefetched = set()

    def load_x(l, blk, ph=2):
        xt = Xb[blk % 2]
        key = f"Xb{blk % 2}"
        if (l, blk, ph) in prefetched:
            return xt, key
        src = xp if l == 0 else Xs.ap()
        S.dma("sp", xt[:, :], src[blk * 128:(blk + 1) * 128, :], reads=[f"Xs{blk}"], writes=[key])
        return xt, key

    def norm_T(l, xt, xkey):
        act(xsb[:, :], xt[:, :], AF.Square, [xkey], ["xsb", "ssq"], accum=ssq[:, 0:1])
        rsqrt_eps(ssq[:, 0:1], rst[:, 0:1], 1, 1.0 / D, "ssq", "rst")
        act(xsb[:, :], xt[:, :], AF.Copy, [xkey, "rst"], ["xsb"], scale=rst[:, 0:1])

    def norm_T2(l):
        pt, kt = PSB()
        for k in range(8):
            tr(pt[:, k * 128:(k + 1) * 128], xsb[:, k * 128:(k + 1) * 128], ident[:, :], ["xsb", "ident"], [kt], inc=(k == 7))
        tt("dve", hT[:, :, :], pt[:, :].rearrange("p (k t) -> p k t", k=8),
           colsA[:, l * 8:(l + 1) * 8].unsqueeze(2).to_broadcast([128, 8, 128]), ALU.mult, [kt, "colsA"], ["hT"])

    def proj_kv(cur, want_tok):
        pk, kk = PSF()
        proj_fm(pk[:, 0:128], kk, C_KA, 128, last=False)
        proj_tm(pk[:, 128:384], kk, C_KA, 256)
        cp("act", KTb[cur][:, :], pk[:, 0:128], [kk], [f"KTb{cur}"])
        for a in range(2):
            cp("dve", Vaug[cur][:, a, 0:64], pk[:, 256 + a * 64:256 + (a + 1) * 64], [kk], [f"Vaug{cur}"])
        if want_tok:
            cp("dve", kvtok[:, :], pk[:, 128:384], [kk], ["kvtok"])
        return pk, kk

    def gate(c0, out_tile, out_key):
        pg, kg = PSF()
        proj_tm(pg[:, :], kg, c0, 512)
        act(Tt[:, :], pg[:, :], AF.Tanh, [kg], ["Tt"], scale=0.5)
        stt("dve", out_tile[:, :], Tt[:, :], 1.0, pg[:, :], ALU.add, ALU.mult, ["Tt", kg], [out_key])

    def gla_decay_a(l):
        pl, kl = PSF()
        proj_fm(pl[0:16, 0:128], kl, C_LR, 16)
        cp("act", lrTb[:, :], pl[0:16, 0:128], [kl], ["lrTb"])

    def gla_decay_b(l):
        pz, kz = PSF()
        for hh in range(2):
            mm(pz[:, hh * 128:(hh + 1) * 128], wupb[0:16, hh * 128:(hh + 1) * 128], lrTb[0:16, :], True, True,
               ["wupb", "lrTb"], [kz], inc=(hh == 1))
        for hh in range(2):
            c = DEPTH * 8 + l * 2 + hh
            act(ee[:, hh * 128:(hh + 1) * 128], pz[:, hh * 128:(hh + 1) * 128], AF.Exp, [kz, "colsA"], ["ee"],
                scale=-1.0, bias=colsA[:, c:c + 1])
        act(ee[:, :], ee[:, :], AF.Ln, ["ee"], ["ee"], bias=1.0)
        act(aa[:, :], ee[:, :], AF.Exp, ["ee"], ["aa"], scale=-1.0 / 16.0)

    def gla_decay_c():
        for hh in range(2):
            for c in range(2):
                o = hh * 128 + c * 64
                S.op("dve", lambda e, o=o, eo=EbT[:, o:o + 64]: e.tensor_tensor_scan(out=eo, data0=aa[:, o:o + 64], data1=ones64[:, :],
                                                                                   initial=1.0, op0=ALU.mult, op1=ALU.mult),
                     reads=["aa", "ones64"], writes=["EbT"])
        S.op("dve", lambda e, ei=EbT[:, :]: e.reciprocal(out=EnbT[:, :], in_=ei), reads=["EbT"], writes=["EnbT"])

    def gla_decay(l, is_sample):
        gla_decay_a(l)
        gla_decay_b(l)
        if not is_sample:
            gla_decay_c()

    def gla_state_chunk(c, want_p, want_bf=True):
        pd, kd = PSF()
        for h in range(4):
            hh, hp = h // 2, h % 2
            mm(pd[hp * 64:(hp + 1) * 64, hh * 128:(hh + 1) * 128], kitok[c * 64:(c + 1) * 64, h * 64:(h + 1) * 64],
               vcb[c * 64:(c + 1) * 64, h * 128:(h + 1) * 128], True, True, ["kitok", "vcb"], [kd], inc=(h == 3))
        tt("dve", Sst[:, :], Sst[:, :], pd[:, 0:256], ALU.add, ["Sst", kd], ["Sst"])
        for hh in range(2):
            col = hh * 128 + c * 64 + 63
            ts("dve", Sst[:, hh * 128:(hh + 1) * 128], Sst[:, hh * 128:(hh + 1) * 128], EbT[:, col:col + 1], None,
               ALU.mult, None, ["Sst", "EbT"], ["Sst"])
        if want_bf:
            cp("act", Sbf[(c + 1) % 2][:, :], Sst[:, :], ["Sst"], [f"Sbf{(c + 1) % 2}"])
        if want_p:
            a_ap = EbT[:, :].rearrange("p (hh t) -> p hh t", hh=2)[:, :, c * 64 + 63]
            tt("dve", Ptot[:, :], Ptot[:, :], a_ap, ALU.mult, ["Ptot", "EbT"], ["Ptot"])

    def kvp_a():
        pqk, kq = PSF()
        for hh in range(2):
            proj_fm(pqk[:, 256 + hh * 128:256 + (hh + 1) * 128], kq, C_KC + hh * 128, 128, last=(hh == 1))
        tt("dve", kiT[:, :], pqk[:, 256:512], EnbT[:, :], ALU.mult, [kq, "EnbT"], ["kiT"])

    def kvp_b():
        pt, kt = PSB()
        for hh in range(2):
            tr(pt[:, hh * 128:(hh + 1) * 128], kiT[:, hh * 128:(hh + 1) * 128], ident[:, :], ["kiT", "ident"], [kt], inc=(hh == 1))
        cp("act", kitok[:, :], pt[:, 0:256], [kt], ["kitok"])
        pv, kv = PSF()
        proj_tm(pv[:, :], kv, C_VC, 512)
        cp("act", vcb[:, :], pv[:, :], [kv], ["vcb"])

    def gla_kv_proj():
        kvp_a()
        kvp_b()

    def group_norm_out(po, kpo, rows):
        for h in range(4):
            act(xsb[0:rows, h * 128:(h + 1) * 128], po[0:rows, h * 128:(h + 1) * 128], AF.Square, [kpo], ["xsb", "ssq"],
                accum=ssq[0:rows, 4 + h:5 + h])
        rsqrt_eps(ssq[:, 4:8], rst[:, 4:8], 4, 1.0 / 128.0, "ssq", "rst")
        for h in range(4):
            stt("dve", Y[0:rows, 1024 + h * 128:1024 + (h + 1) * 128], po[0:rows, h * 128:(h + 1) * 128], rst[0:rows, 4 + h:5 + h],
                Gc[0:rows, h * 128:(h + 1) * 128], ALU.mult, ALU.mult, [kpo, "rst", "Gc"], ["Y"])

    def attn_norm_out(pos, kpos, rows):
        for kv in range(2):
            v = pos[kv][0:rows, 0:260].rearrange("p (g e) -> p g e", g=4)
            tt("dve", den[0:rows, kv * 4:(kv + 1) * 4], v[:, :, 64], esink[0:rows, kv * 4:(kv + 1) * 4], ALU.add,
               [kpos[kv], "esink"], ["den"])
        S.op("dve", lambda e, ro=rden[0:rows, 0:8], di=den[0:rows, 0:8]: e.reciprocal(out=ro, in_=di), reads=["den"], writes=["rden"])
        for h in range(8):
            kv, g = h // 4, h % 4
            stt("dve", Y[0:rows, h * 64:(h + 1) * 64], pos[kv][0:rows, g * 65:g * 65 + 64], rden[0:rows, h:h + 1],
                Ga[0:rows, h * 64:(h + 1) * 64], ALU.mult, ALU.mult, [kpos[kv], "rden", "Ga"], ["Y"])

    def ln_v(l):
        pv, kv = PSF()
        proj_tm(pv[:, :], kv, C_VB, 512)
        S.op("dve", lambda e, bo=bst[:, 0:6], pi=pv[:, :]: e.bn_stats(out=bo, in_=pi), reads=[kv], writes=["bst"])
        S.op("dve", lambda e, mo=bmv[:, 0:2], bi=bst[:, 0:6]: e.bn_aggr(out=mo, in_=bi), reads=["bst"], writes=["bmv"])
        rsqrt_eps(bmv[:, 1:2], rst[:, 1:2], 1, 1.0, "bmv", "rst")
        stt("dve", vnf[:, :], pv[:, :], bmv[:, 0:1], lnG[:, :], ALU.subtract, ALU.mult, [kv, "bmv", "lnG"], ["vnf"])
        stt("dve", vnf[:, :], vnf[:, :], rst[:, 1:2], lnB[:, :], ALU.mult, ALU.add, ["vnf", "rst", "lnB"], ["vnf"])

    def out_proj(l, xt, xkey, rows, dst_ap, dst_key, final):
        out_proj_a()
        out_proj_b(l, xt, xkey, rows, final)

    def out_proj_a():
        pt0, kt0 = PSB()
        for k in range(8):
            tr(pt0[:, k * 128:(k + 1) * 128], Y[:, k * 128:(k + 1) * 128], ident[:, :], ["Y", "ident"], [kt0], inc=(k == 7))
        cp("act", yT[:, 0:8, :], pt0[:, :].rearrange("p (k t) -> p k t", k=8), [kt0], ["yT"])
        pt1, kt1 = PSB()
        for k in range(4):
            tr(pt1[:, k * 128:(k + 1) * 128], Y[:, (8 + k) * 128:(9 + k) * 128], ident[:, :], ["Y", "ident"], [kt1], inc=(k == 3))
        cp("dve", yT[:, 8:12, :], pt1[:, 0:512].rearrange("p (k t) -> p k t", k=4), [kt1], ["yT"])

    def out_proj_b(l, xt, xkey, rows, final):
        for n in range(2):
            po, ko = PSF()
            for k in range(12):
                mm(po[:, :], yT[:, k, :], Wo[:, k, n * 512:(n + 1) * 512], k == 0, k == 11, ["yT", f"Wo{k}"], [ko], inc=(k == 11))
            stt("dve", xt[0:rows, n * 512:(n + 1) * 512], po[0:rows, :], 0.5, xt[0:rows, n * 512:(n + 1) * 512], ALU.mult, ALU.add,
                [ko, xkey], [xkey])
        if not final:
            return
        act(xsb[0:rows, :], xt[0:rows, :], AF.Square, [xkey], ["xsb", "ssq"], accum=ssq[0:rows, 2:3])
        rsqrt_eps(ssq[:, 2:3], rst[:, 2:3], 1, 1.0 / D, "ssq", "rst")
        stt("dve", xt[0:rows, :], xt[0:rows, :], rst[0:rows, 2:3], fG[0:rows, :], ALU.mult, ALU.mult, [xkey, "rst", "fG"], [xkey])

    def phase1_block(l, blk):
        p = blk % 2
        par["p"] = p
        xt, xkey = load_x(l, blk, 1)
        norm_T(l, xt, xkey)
        yield
        par["p"] = p
        norm_T2(l)
        yield
        par["p"] = p
        gla_decay_a(l)
        if blk == NB - 1:
            pk, kk = PSF()
            proj_fm(pk[:, 0:128], kk, C_KA, 128, last=False)
            proj_tm(pk[:, 128:384], kk, C_KA, 256)
            cp("dve", msgt[:, 256:384], pk[:, 0:128], [kk], ["msgt"])
            cp("dve", msgt[:, 384:512], pk[:, 256:384], [kk], ["msgt"])
        yield
        par["p"] = p
        gla_decay_b(l)
        yield
        par["p"] = p
        gla_decay_c()
        kvp_a()
        yield
        par["p"] = p
        kvp_b()
        yield
        par["p"] = p
        for c in range(2):
            gla_state_chunk(c, True, False)

    def exchange_a(l):
        cp("dve", msgt[:, 0:256], Sst[:, :], ["Sst"], ["msgt"])
        cp("dve", msgt[:, 512:514], Ptot[:, :], ["Ptot"], ["msgt"])
        S.dma("pool", msg.ap(), msgt[:, :], reads=["msgt"], writes=["msg"])
        S.allgather(msg.ap().opt(), gath.ap().opt(), [[0, 1, 2, 3], [4, 5, 6, 7]], reads=["msg"], writes=["gath"])

    def exchange_b(l):
        S.op("dve", lambda e: e.memset(Sst[:, :], 0.0), reads=[], writes=["Sst"])
        S.op("dve", lambda e: e.memset(aa[:, :], 0.0), reads=[], writes=["aa"])
        for j in range(3):
            S.dma("pool", msgt[:, :], gath.ap()[j * 128:(j + 1) * 128, :], reads=["gath"], writes=["msgt"])
            for hh in range(2):
                stt("dve", ee[:, hh * 128:(hh + 1) * 128], Sst[:, hh * 128:(hh + 1) * 128], msgt[:, 512 + hh:513 + hh],
                    msgt[:, hh * 128:(hh + 1) * 128], ALU.mult, ALU.add, ["Sst", "msgt"], ["ee"])
            tt("dve", ee[:, :], ee[:, :], Sst[:, :], ALU.subtract, ["ee", "Sst"], ["ee"])
            stt("dve", Sst[:, :], ee[:, :], cmk[:, j:j + 1], Sst[:, :], ALU.mult, ALU.add, ["ee", "cmk", "Sst"], ["Sst"])
            stt("dve", aa[:, :], msgt[:, 256:512], cmk[:, 3 + j:4 + j], aa[:, :], ALU.mult, ALU.add, ["msgt", "cmk", "aa"], ["aa"])
        cp("act", Sbf[0][:, :], Sst[:, :], ["Sst"], ["Sbf0"])
        cp("act", KTh[:, :], aa[:, 0:128], ["aa"], ["KTh"])
        cp("dve", Vaugh[:, :, 0:64], aa[:, 128:256].rearrange("p (a d) -> p a d", a=2), ["aa"], ["Vaugh"])

    def phase2_block(l, blk, last_layer):
        p = blk % 2
        cur = blk % 3
        prv = (blk + 2) % 3
        par["p"] = p
        xt, xkey = load_x(l, blk)
        norm_T(l, xt, xkey)
        yield
        par["p"] = p
        norm_T2(l)
        yield
        par["p"] = p
        pq, kq = PSF()
        for j in range(4):
            proj_fm(pq[:, j * 128:(j + 1) * 128], kq, C_QA + j * 128, 128, last=(j == 3))
        act(QTs[:, :], pq[:, :], AF.Copy, [kq], ["QTs"], scale=0.125)
        proj_kv(cur, blk == NB - 1)
        if blk == NB - 1:
            S.dma("sp", okp[l, :, :], kvtok[:, 0:128], reads=["kvtok"])
            S.dma("sp", ovp[l, :, :], kvtok[:, 128:256], reads=["kvtok"])
        gate(C_GA, Ga, "Ga")
        yield
        par["p"] = p
        gate(C_GB, UG, "UG")
        pu, ku = PSF()
        proj_tm(pu[:, :], ku, C_UB, 512)
        tt("dve", UG[:, :], pu[:, :], UG[:, :], ALU.mult, [ku, "UG"], ["UG"])
        ln_v(l)
        cp("act", vnb[:, :], vnf[:, :], ["vnf"], ["vnb"])
        yield
        par["p"] = p
        gla_decay_a(l)
        gate(C_GC, Gc, "Gc")
        tt("dve", Gc[:, :], Gc[:, :], gnG[:, :], ALU.mult, ["Gc", "gnG"], ["Gc"])
        yield
        par["p"] = p
        gla_decay_b(l)
        yield
        par["p"] = p
        gla_decay_c()
        pqk, kqk = PSF()
        for hh in range(2):
            proj_fm(pqk[:, hh * 128:(hh + 1) * 128], kqk, C_QC + hh * 128, 128, last=(hh == 1))
        stt("dve", qdT[:, :], pqk[:, 0:256], 0.125, EbT[:, :], ALU.mult, ALU.mult, [kqk, "EbT"], ["qdT"])
        kvp_a()
        yield
        par["p"] = p
        kvp_b()
        yield
        par["p"] = p
        if blk == 0:
            kprev, kprev_key, vprev, vprev_key, bp, bp_key = KTh, "KTh", Vaugh, "Vaugh", bprev0, "bprev0"
        else:
            kprev, kprev_key, vprev, vprev_key, bp, bp_key = KTb[prv], f"KTb{prv}", Vaug[prv], f"Vaug{prv}", bprev, "bprev"
        for kv in range(2):
            for kb in range(2):
                kt_, ktk, bt, btk = (kprev, kprev_key, bp, bp_key) if kb == 0 else (KTb[cur], f"KTb{cur}", bcur, "bcur")
                ps, kps = PSF()
                mm(ps[:, :], kt_[kv * 64:(kv + 1) * 64, :], QTs[kv * 64:(kv + 1) * 64, :], True, False, [ktk, "QTs"], [kps], False)
                mm(ps[:, :], ident[:, :], bt[:, kv * 512:(kv + 1) * 512], False, True, ["ident", btk], [kps], True)
                act(PT[:, kv * 2 + kb, :], ps[:, :], AF.Exp, [kps], ["PT"])
        yield
        par["p"] = p
        pos, kpos = [], []
        for kv in range(2):
            po, kpo = PSF()
            pos.append(po)
            kpos.append(kpo)
            for g in range(4):
                for kb in range(2):
                    va, vak = (vprev, vprev_key) if kb == 0 else (Vaug[cur], f"Vaug{cur}")
                    mm(po[:, g * 65:(g + 1) * 65], PT[:, kv * 2 + kb, g * 128:(g + 1) * 128], va[:, kv, 0:65], kb == 0, kb == 1,
                       ["PT", vak], [kpo], inc=(g == 3 and kb == 1))
        attn_norm_out(pos, kpos, 128)
        yield
        par["p"] = p
        pm, km = PSF()
        for h in range(8):
            mm(pm[:, h * 64:(h + 1) * 64], WsT[:, h, :], vnb[:, h * 64:(h + 1) * 64], True, True, ["WsT", "vnb"], [km], inc=(h == 7))
        for h in range(8):
            stt("dve", Y[:, 512 + h * 64:512 + (h + 1) * 64], pm[:, h * 64:(h + 1) * 64], bsT[:, l * 8 + h:l * 8 + h + 1],
                UG[:, h * 64:(h + 1) * 64], ALU.add, ALU.mult, [km, "bsT", "UG"], ["Y"])
        yield
        par["p"] = p
        pas = [PSF(), PSF()]
        for hp in range(2):
            pa, ka = pas[hp]
            for hh in range(2):
                mm(pa[:, hh * 128:(hh + 1) * 128], kiT[hp * 64:(hp + 1) * 64, hh * 128:(hh + 1) * 128],
                   qdT[hp * 64:(hp + 1) * 64, hh * 128:(hh + 1) * 128], True, True, ["kiT", "qdT"], [ka], inc=(hh == 1))
        for h in range(4):
            hh, hp = h // 2, h % 2
            pa, ka = pas[hp]
            tt("dve", attT[:, h * 128:(h + 1) * 128], pa[:, hh * 128:(hh + 1) * 128], cmA[:, :], ALU.mult, [ka, "cmA"], ["attT"])
        gla_state_chunk(0, False)
        yield
        par["p"] = p
        po, kpo = PSF()
        for h in range(4):
            hh, hp = h // 2, h % 2
            mm(po[:, h * 128:(h + 1) * 128], attT[:, h * 128:(h + 1) * 128], vcb[:, h * 128:(h + 1) * 128], True, False,
               ["attT", "vcb"], [kpo], False)
            for c in range(2):
                mm(po[c * 64:(c + 1) * 64, h * 128:(h + 1) * 128],
                   qdT[hp * 64:(hp + 1) * 64, hh * 128 + c * 64:hh * 128 + (c + 1) * 64],
                   Sbf[c][hp * 64:(hp + 1) * 64, hh * 128:(hh + 1) * 128], False, True, ["qdT", f"Sbf{c}"], [kpo],
                   inc=(h == 3 and c == 1))
        gla_state_chunk(1, False)
        group_norm_out(po, kpo, 128)
        yield
        par["p"] = p
        out_proj_a()
        yield
        par["p"] = p
        out_proj_b(l, xt, xkey, 128, last_layer)
        if last_layer:
            S.dma("sp", yp[blk * 128:(blk + 1) * 128, :], xt[:, :], reads=[xkey])
        else:
            S.dma("sp", Xs.ap()[blk * 128:(blk + 1) * 128, :], xt[:, :], reads=[xkey], writes=[f"Xs{blk}"])

    def pipeline(gens, window=2, tag=""):
        import os
        window = int(os.environ.get("KWIN" + tag, str(window)))
        pending = list(gens)
        active = []
        while pending or active:
            nxt = []
            for g in active:
                try:
                    next(g)
                    nxt.append(g)
                except StopIteration:
                    pass
            active = nxt
            if pending and len(active) < window:
                g = pending.pop(0)
                try:
                    next(g)
                    active.append(g)
                except StopIteration:
                    pass

    def sample_block(l, last_layer):
        R = NS
        par["p"] = 0
        norm_T(l, xsmp, "xsmp")
        norm_T2(l)
        pq, kq = PSF()
        for j in range(4):
            proj_fm(pq[:, j * 128:(j + 1) * 128], kq, C_QA + j * 128, 128, last=(j == 3))
        act(QTs[:, :], pq[:, :], AF.Copy, [kq], ["QTs"], scale=0.125)
        pk, kk = PSF()
        proj_tm(pk[:, 0:256], kk, C_KA, 256)
        cp("dve", kvtok[:, :], pk[:, 0:256], [kk], ["kvtok"])
        gate(C_GA, Ga, "Ga")
        S.dma("sp", KVw[0:127, :, :], swk[l, :, 1:128, :].rearrange("i r f -> r i f"), writes=["KVw"])
        S.dma("sp", KVw[127:128, :, :], kvtok[0:NS, 0:128], reads=["kvtok"], writes=["KVw"])
        S.dma("sp", oks[l].rearrange("i r f -> r i f"), KVw[:, :, :], reads=["KVw"])
        for q4 in range(4):
            pt, kt = PSF()
            for ii in range(4):
                i = q4 * 4 + ii
                tr(pt[:, ii * 128:(ii + 1) * 128], KVw[:, i, :], identf[:, :], ["KVw", "identf"], [kt], inc=(ii == 3))
            cp("act", KwT[:, q4 * 4:(q4 + 1) * 4, :], pt[:, :].rearrange("p (i r) -> p i r", i=4), [kt], ["KVx"])
        QTv = QTs[:, :].rearrange("p (j t) -> p j t", j=4)
        pls = [PSF(), PSF()]
        for kv in range(2):
            pl, kl = pls[kv]
            for i in range(NS):
                mm(pl[:, i * 4:i * 4 + 4], KwT[kv * 64:(kv + 1) * 64, i, :], QTv[kv * 64:(kv + 1) * 64, :, i],
                   True, True, ["KVx", "QTs"], [kl], inc=(i == NS - 1))
        for kv in range(2):
            pl, kl = pls[kv]
            tt("dve", Ls[:, kv * 64:(kv + 1) * 64].rearrange("p (i j) -> p i j", j=4), pl[:, 0:64].rearrange("p (i j) -> p i j", j=4),
               bsmp[:, kv * 4:(kv + 1) * 4].unsqueeze(1).to_broadcast([128, NS, 4]), ALU.add, [kl, "bsmp"], ["Ls"])
        act(PTs[:, :], Ls[:, :], AF.Exp, ["Ls"], ["PTs"])
        for kv in range(2):
            tt("dve", PTm[:, kv * 4:(kv + 1) * 4, :, :],
               PTs[:, kv * 64:(kv + 1) * 64].rearrange("p (i j) -> p j i", j=4).unsqueeze(3).to_broadcast([128, 4, 16, 16]),
               eye16[:, :, :].unsqueeze(1).to_broadcast([128, 4, 16, 16]), ALU.mult, ["PTs", "eye16"], ["PT"])
        S.dma("sp", KVw[0:127, :, :], swv[l, :, 1:128, :].rearrange("i r f -> r i f"), writes=["KVw"])
        S.dma("sp", KVw[127:128, :, :], kvtok[0:NS, 128:256], reads=["kvtok"], writes=["KVw"])
        S.dma("sp", ovs[l].rearrange("i r f -> r i f"), KVw[:, :, :], reads=["KVw"])
        S.op("dve", lambda e: e.memset(KVx[:, :], 1.0), reads=[], writes=["KVx"])
        cp("dve", Vaus[:, :, :, 0:64], KVw[:, :, :].rearrange("p i (a d) -> p i a d", a=2), ["KVw"], ["KVx"])
        pos, kpos = [], []
        for kv in range(2):
            po, kpo = PSF()
            pos.append(po)
            kpos.append(kpo)
            for g in range(4):
                h = kv * 4 + g
                for i in range(NS):
                    mm(po[0:R, g * 65:(g + 1) * 65], PTm[:, h, i, :], Vaus[:, i, kv, 0:65], i == 0, i == NS - 1, ["PT", "KVx"], [kpo],
                       inc=(g == 3 and i == NS - 1))
        attn_norm_out(pos, kpos, R)
        gate(C_GB, UG, "UG")
        pu, ku = PSF()
        proj_tm(pu[:, :], ku, C_UB, 512)
        tt("dve", UG[:, :], pu[:, :], UG[:, :], ALU.mult, [ku, "UG"], ["UG"])
        ln_v(l)
        S.dma("sp", ocv[l, :, :], vnf[0:R, :], reads=["vnf"])
        tt("dve", Tt[0:R, :].rearrange("p (h d) -> p h d", h=8), vnf[0:R, :].rearrange("p (h d) -> p h d", h=8),
           w00[0:R, :].unsqueeze(2).to_broadcast([R, 8, 64]), ALU.mult, ["vnf", "w00"], ["Tt"])
        tt("dve", Tt[0:R, :].rearrange("p (h d) -> p h d", h=8), Tt[0:R, :].rearrange("p (h d) -> p h d", h=8),
           b00[0:R, :].unsqueeze(2).to_broadcast([R, 8, 64]), ALU.add, ["Tt", "b00"], ["Tt"])
        tt("dve", Y[0:R, 512:1024], Tt[0:R, :], UG[0:R, :], ALU.mult, ["Tt", "UG"], ["Y"])
        gla_decay(l, True)
        pqk, kqk = PSF()
        for hh in range(2):
            proj_fm(pqk[:, hh * 128:(hh + 1) * 128], kqk, C_QC + hh * 128, 128, last=False)
        for hh in range(2):
            proj_fm(pqk[:, 256 + hh * 128:256 + (hh + 1) * 128], kqk, C_KC + hh * 128, 128, last=(hh == 1))
        ts("dve", qTs[:, :], pqk[:, 0:256], 0.125, None, ALU.mult, None, [kqk], ["qTs"])
        cp("dve", kTs[:, :], pqk[:, 256:512], [kqk], ["kTs"])
        pv, kv_ = PSF()
        proj_tm(pv[:, :], kv_, C_VC, 512)
        cp("act", vcb[:, :], pv[:, :], [kv_], ["vcb"])
        gate(C_GC, Gc, "Gc")
        tt("dve", Gc[:, :], Gc[:, :], gnG[:, :], ALU.mult, ["Gc", "gnG"], ["Gc"])
        q_v = qTs[:, :].rearrange("p (hh t) -> p hh t", hh=2)[:, :, 0:NS]
        tt("dve", QM[:, :, :, :], q_v.unsqueeze(3).to_broadcast([128, 2, 16, 16]),
           eye16[:, :, :].unsqueeze(1).to_broadcast([128, 2, 16, 16]), ALU.mult, ["qTs", "eye16"], ["QM"])
        a_v = aa[:, :].rearrange("p (hh t) -> p hh t", hh=2)
        for q4 in range(NS // 4):
            S.dma("sp", SS[:, :, :, :], sgl[l, q4 * 4:(q4 + 1) * 4].rearrange("i (hh p) v -> p i hh v", p=128), writes=["SS"])
            tt("dve", SS[:, :, :, :], SS[:, :, :, :],
               a_v[:, :, q4 * 4:(q4 + 1) * 4].rearrange("p hh i -> p i hh").unsqueeze(3).to_broadcast([128, 4, 2, 128]),
               ALU.mult, ["SS", "aa"], ["SS"])
            for i2 in range(2):
                pvb, kvb = PSF()
                for il in range(2):
                    i = q4 * 4 + i2 * 2 + il
                    for h in range(4):
                        hh, hp = h // 2, h % 2
                        mm(pvb[hp * 64:(hp + 1) * 64, (il * 2 + hh) * 128:(il * 2 + hh + 1) * 128], esel[0:NS, i, :],
                           vcb[0:NS, h * 128:(h + 1) * 128], True, True, ["esel", "vcb"], [kvb], inc=(il == 1 and h == 3))
                for il in range(2):
                    i = q4 * 4 + i2 * 2 + il
                    for hh in range(2):
                        stt("dve", SS[:, i2 * 2 + il, hh, :], pvb[:, (il * 2 + hh) * 128:(il * 2 + hh + 1) * 128],
                            kTs[:, hh * 128 + i:hh * 128 + i + 1], SS[:, i2 * 2 + il, hh, :], ALU.mult, ALU.add,
                            [kvb, "kTs", "SS"], ["SS"])
            S.dma("sp", ogs[l, q4 * 4:(q4 + 1) * 4].rearrange("i (hh p) v -> p i hh v", p=128), SS[:, :, :, :], reads=["SS"])
            cp("act", SSb[:, :, :, :], SS[:, :, :, :], ["SS"], ["SSb"])
            pos2 = [PSF(), PSF()]
            for hp in range(2):
                po, kpo = pos2[hp]
                for hh in range(2):
                    for il in range(4):
                        i = q4 * 4 + il
                        mm(po[0:R, hh * 128:(hh + 1) * 128], QM[hp * 64:(hp + 1) * 64, hh, i, :], SSb[hp * 64:(hp + 1) * 64, il, hh, :],
                           il == 0, il == 3, ["QM", "SSb"], [kpo], inc=(hh == 1 and il == 3))
            for h in range(4):
                hh, hp = h // 2, h % 2
                po, kpo = pos2[hp]
                if q4 == 0:
                    cp("dve", vnf[0:R, h * 128:(h + 1) * 128], po[0:R, hh * 128:(hh + 1) * 128], [kpo], ["vnf"])
                else:
                    tt("dve", vnf[0:R, h * 128:(h + 1) * 128], vnf[0:R, h * 128:(h + 1) * 128], po[0:R, hh * 128:(hh + 1) * 128],
                       ALU.add, ["vnf", kpo], ["vnf"])
        group_norm_out(vnf, "vnf", R)
        out_proj(l, xsmp, "xsmp", R, None, None, last_layer)
        if last_layer:
            S.dma("sp", ys[:, :], xsmp[0:R, :], reads=["xsmp"])

    for l in range(DEPTH):
        par["p"] = 0
        load_layer_params(l)
        S.op("dve", lambda e: e.memset(Sst[:, :], 0.0), reads=[], writes=["Sst"])
        S.op("dve", lambda e: e.memset(Ptot[:, :], 1.0), reads=[], writes=["Ptot"])
        pipeline([phase1_block(l, blk) for blk in range(NB)], tag="1")
        par["p"] = 0
        exchange_a(l)
        sample_block(l, l == DEPTH - 1)
        par["p"] = 0
        exchange_b(l)
        if l + 1 < DEPTH:
            convert_layer(l + 1)
        pipeline([phase2_block(l, blk, l == DEPTH - 1) for blk in range(NB)], tag="2")
        par["p"] = 0
        S.dma("sp", ogp[l].rearrange("(hh p) v -> p hh v", p=128), Sst[:, :].rearrange("p (hh v) -> p hh v", hh=2), reads=["Sst"])
    S.finish("sp")
    S.emit()
    return nc


def t5_bucket(dist):
    n = np.maximum(dist, 0)
    max_exact = 16
    large = max_exact + (np.log(np.maximum(n, 1) / max_exact) / np.log(128 / max_exact) * (32 - max_exact)).astype(np.int32)
    large = np.minimum(large, 31)
    return np.where(n < max_exact, n, large).astype(np.int32)


_PROG = {}


def prepare(inputs, NB, DEPTH):
    f32 = lambda a: np.ascontiguousarray(np.asarray(a, dtype=np.float32))
    xpr = f32(inputs["x_prompt"])
    xsa = f32(inputs["x_sample"]).reshape(128, D)
    swk = f32(inputs["state_swa_k"]).reshape(DEPTH, 128, 128, 128)
    swv = f32(inputs["state_swa_v"]).reshape(DEPTH, 128, 128, 128)
    sgl = f32(inputs["state_gla"]).reshape(DEPTH, 128, 256, 128)
    rel_bias = f32(inputs["rel_bias"])
    i = np.arange(128)[:, None]
    j = np.arange(256)[None, :]
    dist = i + 128 - j
    band = (dist >= 0) & (dist < 128)
    table = np.concatenate([rel_bias, np.full((1, 8), NEG, np.float32)], 0)
    full = table[np.where(band, t5_bucket(dist), 32)]
    b_prev = np.ascontiguousarray(full[:, 0:128, :].transpose(1, 2, 0)).reshape(128, 1024)
    b_cur = np.ascontiguousarray(full[:, 128:256, :].transpose(1, 2, 0)).reshape(128, 1024)
    b_neg = np.full((128, 1024), NEG, np.float32)
    b_smp = np.ascontiguousarray(rel_bias[t5_bucket(127 - np.arange(128))])
    shared = {
        "w_in": f32(inputs["w_in"]), "w_out": f32(inputs["w_out"]),
        "norm_g": f32(inputs["norm_g"]).reshape(DEPTH * 8, 128), "fin_g": f32(inputs["final_norm_g"]).reshape(1, D),
        "sinks": f32(inputs["sinks"]), "sp_w": f32(inputs["spatial_w"]),
        "sp_b": f32(inputs["spatial_b"]).reshape(DEPTH * 8, 128),
        "ln_g": f32(inputs["chunk_ln_g"]), "ln_b": f32(inputs["chunk_ln_b"]),
        "wup": f32(inputs["gla_w_up"]), "bup": f32(inputs["gla_b_up"]).reshape(DEPTH * 2, 128),
        "gn_g": f32(inputs["gla_norm_g"]), "b_cur": b_cur, "b_prev": b_prev, "b_smp": b_smp,
    }
    NT = NB * 128
    in_maps = []
    for c in range(8):
        b, s = c // 4, c % 4
        cm = np.zeros((128, 8), np.float32)
        for jj in range(3):
            cm[:, jj] = 1.0 if jj < s else 0.0
            cm[:, 3 + jj] = 1.0 if jj == s - 1 else 0.0
        m = dict(shared)
        m.update({
            "xp": np.ascontiguousarray(xpr[b, s * NT:(s + 1) * NT, :]),
            "xs": np.ascontiguousarray(xsa[c * NS:(c + 1) * NS]),
            "swk": np.ascontiguousarray(swk[:, c * NS:(c + 1) * NS]),
            "swv": np.ascontiguousarray(swv[:, c * NS:(c + 1) * NS]),
            "sgl": np.ascontiguousarray(sgl[:, c * NS:(c + 1) * NS]),
            "b_prev0": b_neg if s == 0 else b_prev,
            "cmask": cm,
        })
        in_maps.append(m)
    return in_maps


def assemble(r, DEPTH):
    y_prompt = np.stack([np.concatenate([r[b * 4 + s]["yp"] for s in range(4)], 0) for b in range(2)], 0)
    y_sample = np.concatenate([r[c]["ys"] for c in range(8)], 0).reshape(128, 1, D)
    kp = np.stack([r[b * 4 + 3]["okp"] for b in range(2)], 1).reshape(DEPTH, 2, 128, 2, 64)
    vp = np.stack([r[b * 4 + 3]["ovp"] for b in range(2)], 1).reshape(DEPTH, 2, 128, 2, 64)
    gp = np.stack([r[b * 4 + 3]["ogp"] for b in range(2)], 1).reshape(DEPTH, 2, 4, 64, 128)
    ks = np.concatenate([r[c]["oks"] for c in range(8)], 1).reshape(DEPTH, 128, 128, 2, 64)
    vs = np.concatenate([r[c]["ovs"] for c in range(8)], 1).reshape(DEPTH, 128, 128, 2, 64)
    gs = np.concatenate([r[c]["ogs"] for c in range(8)], 1).reshape(DEPTH, 128, 4, 64, 128)
    cv = np.concatenate([r[c]["ocv"] for c in range(8)], 1).reshape(DEPTH, 128, 1, 512)
    return tuple(np.ascontiguousarray(np.asarray(a, dtype=np.float32)) for a in (y_prompt, y_sample, kp, vp, gp, ks, vs, gs, cv))


def run(inputs, NB, DEPTH):
    in_maps = prepare(inputs, NB, DEPTH)
    key = (NB, DEPTH)
    if key not in _PROG:
        _PROG[key] = build_program(NB, DEPTH)
    res = run_bass_kernel_spmd(_PROG[key], in_maps, core_ids=list(range(8)))
    return assemble(res.results, DEPTH)


def kernel(**inputs):
    return run(inputs, 16, 4)
```
